# Optimizing a Trainium2 kernel written in Bass

```python
import math
import jax, jax.numpy as jnp
from jax import lax
import numpy as np

D_MODEL = 1024
BATCH = 4
SEQ = 4096
DEPTH = 2
DEC_BATCH = 32
DEC_SEQ = 8
PAST_LEN = 8192
PAGE_SIZE = 128

GLA_HEADS = 4
GLA_DK = 64
GLA_DV = 128
GLA_WIDTH = GLA_HEADS * GLA_DV
GLA_GATE_RANK = 16
GLA_TAU = 16.0
GLA_CHUNK = 64
ML_HEADS = 4
ML_DH = 128
ML_WIDTH = ML_HEADS * ML_DH
ML_CONV = 4
ML_CHUNK = 64
MB_HEADS = 8
MB_DH = 64
MB_WIDTH = MB_HEADS * MB_DH
MB_BLOCK = 256
MB_TOPK = 3
MB_QCHUNK = 64
N_BRANCH = 3
EPS = 1e-6
NEG = -1e30
POOL_FACTOR = 1.25

IN_SPLITS = (GLA_HEADS * GLA_DK, GLA_HEADS * GLA_DK, GLA_WIDTH, GLA_GATE_RANK, GLA_WIDTH,
             ML_WIDTH, ML_HEADS, ML_HEADS, ML_WIDTH, ML_WIDTH,
             MB_WIDTH, MB_WIDTH, MB_WIDTH, MB_WIDTH,
             N_BRANCH * D_MODEL)
D_IN = (2 * GLA_HEADS * GLA_DK + 2 * GLA_WIDTH + GLA_GATE_RANK + 3 * ML_WIDTH + 2 * ML_HEADS
        + 4 * MB_WIDTH + N_BRANCH * D_MODEL)

kernel_name = "gla_mlstm_moba_parallel_gated_step"


def rms_norm(x, g):
    xf = x.astype(jnp.float32)
    y = xf * lax.rsqrt(jnp.mean(xf * xf, axis=-1, keepdims=True) + EPS)
    return (y * g.astype(jnp.float32)).astype(x.dtype)


def gla_scan(q, k, v, log_a, s0):
    B, L, H, _ = q.shape
    c = min(GLA_CHUNK, L)
    n = -(-L // c)
    pad = n * c - L

    def prep(a):
        a = jnp.pad(a.astype(jnp.float32), ((0, 0), (0, pad), (0, 0), (0, 0)))
        return a.reshape(B, n, c, H, -1).transpose(1, 0, 3, 2, 4)

    tri = jnp.tril(jnp.ones((c, c), bool))

    def step(S, inp):
        qc, kc, vc, ac = inp
        b = jnp.cumsum(ac, axis=2)
        inter = jnp.einsum('bhtd,bhdv->bhtv', qc * jnp.exp(b), S)
        diff = b[:, :, :, None, :] - b[:, :, None, :, :]
        w = jnp.exp(jnp.where(tri[:, :, None], diff, -jnp.inf))
        A = jnp.einsum('bhtd,bhsd,bhtsd->bhts', qc, kc, w)
        out = inter + jnp.einsum('bhts,bhsv->bhtv', A, vc)
        bl = b[:, :, -1:, :]
        S = (jnp.exp(bl[:, :, 0, :])[..., None] * S
             + jnp.einsum('bhsd,bhsv->bhdv', kc * jnp.exp(bl - b), vc))
        return S, out

    S, o = lax.scan(step, s0.astype(jnp.float32), (prep(q), prep(k), prep(v), prep(log_a)))
    o = o.transpose(1, 0, 3, 2, 4).reshape(B, n * c, H, -1)[:, :L]
    return o, S


def mlstm_scan(q, k, v, ig, lf, C0, n0, m0):
    B, L, H, _ = q.shape
    c = min(ML_CHUNK, L)
    n = -(-L // c)
    pad = n * c - L

    def prep4(a):
        a = jnp.pad(a.astype(jnp.float32), ((0, 0), (0, pad), (0, 0), (0, 0)))
        return a.reshape(B, n, c, H, -1).transpose(1, 0, 3, 2, 4)

    def prep3(a, fill):
        a = jnp.pad(a.astype(jnp.float32), ((0, 0), (0, pad), (0, 0)), constant_values=fill)
        return a.reshape(B, n, c, H).transpose(1, 0, 3, 2)

    tri = jnp.tril(jnp.ones((c, c), bool))

    def step(carry, inp):
        C, nv, m = carry
        qc, kc, vc, ic, fc = inp
        F = jnp.cumsum(fc, axis=-1)
        Dm = jnp.where(tri, F[..., :, None] - F[..., None, :] + ic[..., None, :], NEG)
        lin = F + m[..., None]
        m_t = jnp.maximum(lin, Dm.max(-1))
        W = jnp.exp(Dm - m_t[..., None])
        w_in = jnp.exp(lin - m_t)
        Sqk = jnp.einsum('bhtd,bhsd->bhts', qc, kc) * W
        num = (jnp.einsum('bhts,bhsv->bhtv', Sqk, vc)
               + w_in[..., None] * jnp.einsum('bhvd,bhtd->bhtv', C, qc))
        den = Sqk.sum(-1) + w_in * jnp.einsum('bhd,bhtd->bht', nv, qc)
        h = num / jnp.maximum(jnp.abs(den), jnp.exp(-m_t))[..., None]
        m_new = m_t[..., -1]
        w_end = jnp.exp(F[..., -1:] - F + ic - m_new[..., None])
        decay = jnp.exp(F[..., -1] + m - m_new)
        C = decay[..., None, None] * C + jnp.einsum('bhs,bhsv,bhsd->bhvd', w_end, vc, kc)
        nv = decay[..., None] * nv + jnp.einsum('bhs,bhsd->bhd', w_end, kc)
        return (C, nv, m_new), h

    carry0 = (C0.astype(jnp.float32), n0.astype(jnp.float32), m0.astype(jnp.float32))
    (C, nv, m), h = lax.scan(step, carry0, (prep4(q), prep4(k), prep4(v), prep3(ig, NEG), prep3(lf, 0.0)))
    h = h.transpose(1, 0, 3, 2, 4).reshape(B, n * c, H, -1)[:, :L]
    return h, C, nv, m


def moba_attention(q, k_all, v_all, offset):
    B, Lq, H, dh = q.shape
    T = k_all.shape[1]
    nb = -(-T // MB_BLOCK)
    padk = nb * MB_BLOCK - T
    kb = jnp.pad(k_all, ((0, 0), (0, padk), (0, 0), (0, 0))).reshape(B, nb, MB_BLOCK, H, dh).transpose(0, 3, 1, 2, 4)
    vb = jnp.pad(v_all, ((0, 0), (0, padk), (0, 0), (0, 0))).reshape(B, nb, MB_BLOCK, H, dh).transpose(0, 3, 1, 2, 4)
    kmean = kb.astype(jnp.float32).mean(axis=3)
    k_sel = min(MB_TOPK, nb)
    qc = min(MB_QCHUNK, Lq)
    nq = -(-Lq // qc)
    padq = nq * qc - Lq
    qp = jnp.pad(q, ((0, 0), (0, padq), (0, 0), (0, 0))).reshape(B, nq, qc, H, dh)
    qp = qp.transpose(0, 1, 3, 2, 4).reshape(B * nq, H, qc, dh)
    pos = jnp.minimum(offset + jnp.arange(nq * qc), T - 1).reshape(nq, qc)
    posr = jnp.tile(pos, (B, 1))
    bidx = jnp.repeat(jnp.arange(B), nq)
    hh = jnp.arange(H)[:, None, None]
    offs = jnp.arange(MB_BLOCK)

    def one(args):
        qi, pi, b = args
        kbb, vbb, kmb = kb[b], vb[b], kmean[b]
        own = pi // MB_BLOCK
        bs = jnp.einsum('hqd,hnd->hqn', qi.astype(jnp.float32), kmb)
        past = jnp.arange(nb)[None, None, :] < own[None, :, None]
        bs = jnp.where(past, bs, -jnp.inf)
        _, top = lax.top_k(bs, k_sel)
        valid = top < own[None, :, None]
        idx = jnp.concatenate([top, jnp.broadcast_to(own[None, :, None], (H, qc, 1))], axis=-1)
        kg = kbb[hh, idx]
        vg = vbb[hh, idx]
        s = jnp.einsum('hqd,hqjkd->hqjk', qi, kg).astype(jnp.float32)
        kpos = idx[..., None] * MB_BLOCK + offs
        sel = jnp.concatenate([valid, jnp.ones((H, qc, 1), bool)], axis=-1)
        mask = sel[..., None] & (kpos <= pi[None, :, None, None])
        s = jnp.where(mask, s, -jnp.inf)
        pr = jax.nn.softmax(s.reshape(H, qc, -1), axis=-1).reshape(s.shape).astype(vg.dtype)
        return jnp.einsum('hqjk,hqjkd->qhd', pr, vg)

    out = lax.map(one, (qp, posr, bidx))
    return out.reshape(B, nq * qc, H, dh)[:, :Lq]


def hybrid_layer(x, p, gla_s, ml_c, ml_n, ml_m, ml_conv, k_past, v_past):
    B, L, _ = x.shape
    h = rms_norm(x, p['norm_g'])
    u = h @ p['w_in']
    (g_q, g_k, g_v, g_a, g_z, m_x, m_i, m_f, m_o, m_z,
     b_q, b_k, b_v, b_z, gate) = jnp.split(u, np.cumsum(IN_SPLITS)[:-1].tolist(), axis=-1)

    gq = g_q.reshape(B, L, GLA_HEADS, GLA_DK) * (GLA_DK ** -0.5)
    gk = g_k.reshape(B, L, GLA_HEADS, GLA_DK)
    gv = g_v.reshape(B, L, GLA_HEADS, GLA_DV)
    log_a = jax.nn.log_sigmoid((g_a @ p['gla_w_a2'] + p['gla_b_a']).astype(jnp.float32)) / GLA_TAU
    go, gla_new = gla_scan(gq, gk, gv, log_a.reshape(B, L, GLA_HEADS, GLA_DK), gla_s)
    go = rms_norm(go.astype(x.dtype), p['gla_norm']).reshape(B, L, GLA_WIDTH) * jax.nn.silu(g_z)
    y_gla = go @ p['gla_w_out']

    xp = jnp.concatenate([ml_conv.astype(m_x.dtype), m_x], axis=1)
    conv = sum((xp[:, j:j + L] * p['ml_conv_w'][j] for j in range(ML_CONV)), p['ml_conv_b'])
    conv_new = xp[:, -(ML_CONV - 1):]
    mc = jax.nn.silu(conv)
    mch = mc.reshape(B, L, ML_HEADS, ML_DH)
    mq = jnp.einsum('blhd,hde->blhe', mch, p['ml_w_q'])
    mk = jnp.einsum('blhd,hde->blhe', mch, p['ml_w_k']) * (ML_DH ** -0.5)
    mv = jnp.einsum('blhd,hde->blhe', m_x.reshape(B, L, ML_HEADS, ML_DH), p['ml_w_v'])
    ig = (m_i + p['ml_b_i']).astype(jnp.float32)
    lf = jax.nn.log_sigmoid((m_f + p['ml_b_f']).astype(jnp.float32))
    mh, c_new, n_new, m_new = mlstm_scan(mq, mk, mv, ig, lf, ml_c, ml_n, ml_m)
    mh = mh.astype(x.dtype) * jax.nn.sigmoid(m_o).reshape(B, L, ML_HEADS, ML_DH)
    mh = rms_norm(mh, p['ml_norm']).reshape(B, L, ML_WIDTH) + p['ml_skip'] * mc
    y_ml = (mh * jax.nn.silu(m_z)) @ p['ml_w_out']

    bq = rms_norm(b_q.reshape(B, L, MB_HEADS, MB_DH), p['mb_q_norm']) * (MB_DH ** -0.5)
    bk = rms_norm(b_k.reshape(B, L, MB_HEADS, MB_DH), p['mb_k_norm'])
    bv = b_v.reshape(B, L, MB_HEADS, MB_DH)
    k_all = jnp.concatenate([k_past.astype(bk.dtype), bk], axis=1)
    v_all = jnp.concatenate([v_past.astype(bv.dtype), bv], axis=1)
    bo = moba_attention(bq, k_all, v_all, k_past.shape[1])
    y_mb = (bo.reshape(B, L, MB_WIDTH) * jax.nn.silu(b_z)) @ p['mb_w_out']

    g = jax.nn.sigmoid(gate).reshape(B, L, N_BRANCH, D_MODEL)
    y = g[:, :, 0] * y_gla + g[:, :, 1] * y_ml + g[:, :, 2] * y_mb
    x = x + y @ p['w_out']
    dt = x.dtype
    return x, (gla_new.astype(dt), c_new.astype(dt), n_new.astype(dt), m_new.astype(dt),
               conv_new.astype(dt), bk, bv)


def setup_inputs(seed: int = 0) -> dict:
    key = jax.random.key(seed)
    ks = jax.random.split(key, 40)
    f32 = jnp.float32
    n_pages = PAST_LEN // PAGE_SIZE
    n_pool = int(math.ceil(POOL_FACTOR * DEC_BATCH * n_pages))

    def nrm(k, shape, scale):
        return jax.random.normal(k, shape, f32) * scale

    def gain(k, shape):
        return 1.0 + 0.02 * jax.random.normal(k, shape, f32)

    page_table = jax.random.permutation(ks[4], n_pool)[:DEC_BATCH * n_pages].reshape(DEC_BATCH, n_pages).astype(jnp.int32)
    return {
        "x_prompt": nrm(ks[0], (BATCH, SEQ, D_MODEL), 1.0),
        "x_sample": nrm(ks[1], (DEC_BATCH, DEC_SEQ, D_MODEL), 1.0),
        "cache_k": nrm(ks[2], (DEPTH, n_pool, PAGE_SIZE, MB_HEADS, MB_DH), 1.0),
        "cache_v": nrm(ks[3], (DEPTH, n_pool, PAGE_SIZE, MB_HEADS, MB_DH), 1.0),
        "page_table": page_table,
        "state_gla": nrm(ks[5], (DEPTH, DEC_BATCH, GLA_HEADS, GLA_DK, GLA_DV), 1.0),
        "state_mlstm_c": nrm(ks[6], (DEPTH, DEC_BATCH, ML_HEADS, ML_DH, ML_DH), 1.0),
        "state_mlstm_n": nrm(ks[7], (DEPTH, DEC_BATCH, ML_HEADS, ML_DH), 1.0),
        "state_mlstm_m": nrm(ks[8], (DEPTH, DEC_BATCH, ML_HEADS), 1.0),
        "state_mlstm_conv": nrm(ks[9], (DEPTH, DEC_BATCH, ML_CONV - 1, ML_WIDTH), 1.0),
        "norm_g": gain(ks[10], (DEPTH, D_MODEL)),
        "w_in": nrm(ks[11], (DEPTH, D_MODEL, D_IN), D_MODEL ** -0.5),
        "gla_w_a2": nrm(ks[12], (DEPTH, GLA_GATE_RANK, GLA_HEADS * GLA_DK), GLA_GATE_RANK ** -0.5),
        "gla_b_a": nrm(ks[13], (DEPTH, GLA_HEADS * GLA_DK), 0.1),
        "gla_norm": gain(ks[14], (DEPTH, GLA_DV)),
        "gla_w_out": nrm(ks[15], (DEPTH, GLA_WIDTH, D_MODEL), GLA_WIDTH ** -0.5),
        "ml_conv_w": nrm(ks[16], (DEPTH, ML_CONV, ML_WIDTH), ML_CONV ** -0.5),
        "ml_conv_b": nrm(ks[17], (DEPTH, ML_WIDTH), 0.02),
        "ml_w_q": nrm(ks[18], (DEPTH, ML_HEADS, ML_DH, ML_DH), ML_DH ** -0.5),
        "ml_w_k": nrm(ks[19], (DEPTH, ML_HEADS, ML_DH, ML_DH), ML_DH ** -0.5),
        "ml_w_v": nrm(ks[20], (DEPTH, ML_HEADS, ML_DH, ML_DH), ML_DH ** -0.5),
        "ml_b_i": nrm(ks[21], (DEPTH, ML_HEADS), 0.1),
        "ml_b_f": jnp.linspace(3.0, 6.0, ML_HEADS, dtype=f32)[None, :] + nrm(ks[22], (DEPTH, ML_HEADS), 0.1),
        "ml_norm": gain(ks[23], (DEPTH, ML_DH)),
        "ml_skip": gain(ks[24], (DEPTH, ML_WIDTH)),
        "ml_w_out": nrm(ks[25], (DEPTH, ML_WIDTH, D_MODEL), ML_WIDTH ** -0.5),
        "mb_q_norm": gain(ks[26], (DEPTH, MB_DH)),
        "mb_k_norm": gain(ks[27], (DEPTH, MB_DH)),
        "mb_w_out": nrm(ks[28], (DEPTH, MB_WIDTH, D_MODEL), MB_WIDTH ** -0.5),
        "w_out": nrm(ks[29], (DEPTH, D_MODEL, D_MODEL), D_MODEL ** -0.5),
    }


def reference(x_prompt, x_sample, cache_k, cache_v, page_table, state_gla, state_mlstm_c, state_mlstm_n,
              state_mlstm_m, state_mlstm_conv, norm_g, w_in, gla_w_a2, gla_b_a, gla_norm, gla_w_out,
              ml_conv_w, ml_conv_b, ml_w_q, ml_w_k, ml_w_v, ml_b_i, ml_b_f, ml_norm, ml_skip, ml_w_out,
              mb_q_norm, mb_k_norm, mb_w_out, w_out):
    B = x_prompt.shape[0]
    DB = x_sample.shape[0]
    n_pages = page_table.shape[1]
    dt = x_prompt.dtype
    xp, xs = x_prompt, x_sample
    p_list, s_list = [], []
    for l in range(DEPTH):
        p = dict(norm_g=norm_g[l], w_in=w_in[l], gla_w_a2=gla_w_a2[l], gla_b_a=gla_b_a[l],
                 gla_norm=gla_norm[l], gla_w_out=gla_w_out[l], ml_conv_w=ml_conv_w[l],
                 ml_conv_b=ml_conv_b[l], ml_w_q=ml_w_q[l], ml_w_k=ml_w_k[l], ml_w_v=ml_w_v[l],
                 ml_b_i=ml_b_i[l], ml_b_f=ml_b_f[l], ml_norm=ml_norm[l], ml_skip=ml_skip[l],
                 ml_w_out=ml_w_out[l], mb_q_norm=mb_q_norm[l], mb_k_norm=mb_k_norm[l],
                 mb_w_out=mb_w_out[l], w_out=w_out[l])
        xp, sp = hybrid_layer(
            xp, p,
            jnp.zeros((B, GLA_HEADS, GLA_DK, GLA_DV), dt),
            jnp.zeros((B, ML_HEADS, ML_DH, ML_DH), dt),
            jnp.zeros((B, ML_HEADS, ML_DH), dt),
            jnp.zeros((B, ML_HEADS), dt),
            jnp.zeros((B, ML_CONV - 1, ML_WIDTH), dt),
            jnp.zeros((B, 0, MB_HEADS, MB_DH), dt),
            jnp.zeros((B, 0, MB_HEADS, MB_DH), dt))
        p_list.append(sp)
        k_past = cache_k[l][page_table].reshape(DB, n_pages * PAGE_SIZE, MB_HEADS, MB_DH)
        v_past = cache_v[l][page_table].reshape(DB, n_pages * PAGE_SIZE, MB_HEADS, MB_DH)
        xs, ss = hybrid_layer(xs, p, state_gla[l], state_mlstm_c[l], state_mlstm_n[l], state_mlstm_m[l],
                              state_mlstm_conv[l], k_past, v_past)
        s_list.append(ss)
    p_gla, p_mc, p_mn, p_mm, p_conv, p_k, p_v = [jnp.stack(z) for z in zip(*p_list)]
    s_gla, s_mc, s_mn, s_mm, s_conv, s_k, s_v = [jnp.stack(z) for z in zip(*s_list)]
    return (xp, xs, p_gla, p_mc, p_mn, p_mm, p_conv, p_k, p_v,
            s_gla, s_mc, s_mn, s_mm, s_conv, s_k, s_v)
```

```python
import contextlib
import numpy as np
import concourse.bass as bass
import concourse.mybir as mybir
from concourse.bass_utils import run_bass_kernel_spmd

F32 = mybir.dt.float32
BF16 = mybir.dt.bfloat16
I32 = mybir.dt.int32
AF = mybir.ActivationFunctionType
ALU = mybir.AluOpType
AX = mybir.AxisListType

D = 1024
D_IN = 8216
EPS = 1e-6
NEG = -1e30
MB_NEG = -30000.0
C_GQ, C_GK, C_GV, C_GA, C_GZ = 0, 256, 512, 1024, 1040
C_MX, C_MI, C_MF, C_MO, C_MZ = 1552, 2064, 2068, 2072, 2584
C_BQ, C_BK, C_BV, C_BZ = 3096, 3608, 4120, 4632
C_GATE = 5144


class Dep:
    __slots__ = ("w", "r")

    def __init__(self):
        self.w = None
        self.r = []


class B:
    __slots__ = ("t", "d", "ps")

    def __init__(self, t, ps=False):
        self.t = t
        self.d = Dep()
        self.ps = ps

    def __getitem__(self, k):
        return self.t[k]


class Prog:
    ENGS = ("pe", "act", "dve", "pool", "sp")

    N_SW = 8

    def __init__(self, nc, n_dma_sems=48):
        self.nc = nc
        self.ops = {e: [] for e in self.ENGS}
        self.n_dma_sems = n_dma_sems
        self.dma_rr = 0
        self.sw_rr = 0
        self.dma_val = [0] * n_dma_sems
        self.final_dma = []

    def _deps(self, eng, reads, writes):
        deps = []
        for b in reads:
            t = b.d
            if t.w is not None:
                deps.append(t.w)
        for b in writes:
            t = b.d
            if t.w is not None:
                deps.append(t.w)
            deps.extend(t.r)
        if eng == "pe":
            deps = [d for d in deps if not (d[0] == "eng" and d[1] == "pe")]
        return deps

    def _mark(self, ref, reads, writes):
        for b in reads:
            r = b.d.r
            r.append(ref)
            if len(r) > 24:
                last = {}
                for x in r:
                    key = (x[0], x[1])
                    if key not in last or x[2] > last[key][2]:
                        last[key] = x
                b.d.r = list(last.values())
        for b in writes:
            b.d.w = ref
            b.d.r = []

    def op(self, eng, fn, reads=(), writes=()):
        if any(b.ps for b in reads):
            writes = list(writes) + [b for b in reads if b.ps]
            reads = [b for b in reads if not b.ps]
        lst = self.ops[eng]
        deps = self._deps(eng, reads, writes)
        lst.append(dict(kind="c", fn=fn, deps=deps, inc=False))
        self._mark(("eng", eng, len(lst) - 1), reads, writes)

    def dma(self, fn, reads=(), writes=(), q="sp", final=False):
        lst = self.ops[q]
        deps = self._deps(q, reads, writes)
        if q == "pool":
            k = self.n_dma_sems - self.N_SW + self.sw_rr
            self.sw_rr = (self.sw_rr + 1) % self.N_SW
        else:
            k = self.dma_rr
            self.dma_rr = (self.dma_rr + 1) % (self.n_dma_sems - self.N_SW)
        prev = self.dma_val[k]
        self.dma_val[k] += 16
        val = self.dma_val[k]
        if prev > 0:
            deps.append(("dma", k, prev))
        lst.append(dict(kind="d", fn=fn, deps=deps, sem=k, val=val))
        self._mark(("dma", k, val), reads, writes)
        if final:
            self.final_dma.append((k, val))

    def emit(self):
        nc = self.nc
        for e in self.ENGS:
            for o in self.ops[e]:
                for d in o["deps"]:
                    if d[0] == "eng":
                        self.ops[d[1]][d[2]]["inc"] = True
        vals = {}
        for e in self.ENGS:
            c = 0
            v = []
            for o in self.ops[e]:
                if o["kind"] == "c" and o["inc"]:
                    c += 1
                v.append(c)
            vals[e] = v
        with contextlib.ExitStack() as st:
            esem = {e: st.enter_context(nc.semaphore("s_" + e)) for e in self.ENGS}
            dsem = [st.enter_context(nc.semaphore("d%d" % i)) for i in range(self.n_dma_sems)]
            block = st.enter_context(nc.Block())
            engobj = {"pe": "tensor", "act": "scalar", "dve": "vector", "pool": "gpsimd", "sp": "sync"}

            def run(e, eng):
                seen = {}
                dseen = {}
                for o in self.ops[e]:
                    need = {}
                    dneed = {}
                    for d in o["deps"]:
                        if d[0] == "eng":
                            v = vals[d[1]][d[2]]
                            if v > seen.get(d[1], 0):
                                need[d[1]] = max(need.get(d[1], 0), v)
                        else:
                            if d[2] > dseen.get(d[1], 0):
                                dneed[d[1]] = max(dneed.get(d[1], 0), d[2])
                    for k2, v in need.items():
                        eng.wait_ge(esem[k2], v)
                        seen[k2] = v
                    for k2, v in dneed.items():
                        eng.wait_ge(dsem[k2], v)
                        dseen[k2] = v
                    if o["kind"] == "b":
                        continue
                    ins = o["fn"](eng)
                    if o["kind"] == "d":
                        ins.then_inc(dsem[o["sem"]], 16)
                    elif o["inc"]:
                        ins.then_inc(esem[e], 1)
                if e == "sp":
                    for (k2, v) in self.final_dma:
                        if v > dseen.get(k2, 0):
                            eng.wait_ge(dsem[k2], v)
                            dseen[k2] = v

            for e in self.ENGS:
                if not self.ops[e] and e != "sp":
                    continue
                getattr(block, engobj[e])(lambda eng, e=e: run(e, eng))


K_ID, K_TRIS, K_SUTS, K_TRIN, K_TRI01, K_NEGM, K_ONES, K_SELL128, K_SELL8 = [128 * i for i in range(9)]
K_CAUS = 128 * 9
K_BD = K_CAUS + 4 * 512
K_IOTA = K_BD + 8
K_ID32 = K_IOTA + 1
K_OH = K_ID32 + 32 * 64
NCONST = K_OH + 16 * 128


def make_consts():
    c = np.zeros((128, NCONST), np.float32)
    p = np.arange(128)[:, None]
    f = np.arange(128)[None, :]
    c[:, K_ID:K_ID + 128] = (p == f)
    c[:, K_TRIS:K_TRIS + 128] = (p <= f) * (-1.0 / 16.0)
    c[:, K_SUTS:K_SUTS + 128] = (p > f) * (-1.0 / 16.0)
    c[:, K_TRIN:K_TRIN + 128] = (p <= f) * (-1.0)
    c[:, K_TRI01:K_TRI01 + 128] = (p <= f)
    c[:, K_NEGM:K_NEGM + 128] = np.where(f <= p, 0.0, NEG)
    c[:, K_ONES:K_ONES + 128] = 1.0
    c[:, K_SELL128:K_SELL128 + 128] = (p == 127)
    c[:, K_SELL8:K_SELL8 + 128] = (p == 7)
    f5 = np.arange(512)[None, :]
    for r in range(4):
        c[:, K_CAUS + 512 * r:K_CAUS + 512 * (r + 1)] = ((128 * r + p) <= f5)
    c[:, K_BD:K_BD + 8] = ((p // 8) == np.arange(8)[None, :])
    c[:, K_IOTA] = np.arange(128)
    k32 = np.arange(128)[:, None, None]
    n32 = np.arange(32)[None, :, None]
    c[:, K_ID32:K_ID32 + 32 * 64] = np.broadcast_to((k32 == n32), (128, 32, 64)).reshape(128, 2048)
    c[:, K_OH:K_OH + 2048] = np.broadcast_to((np.arange(128)[:, None, None] == np.arange(16)[None, :, None]), (128, 16, 128)).reshape(128, 2048)
    return c


class Cfg:
    def __init__(self, L=4096, NS=4, TS=8, NPG=64, NPOOL=2560, DEPTH=2, branches="gmb", sample=True, prompt=True):
        self.L, self.NS, self.TS, self.NPG, self.NPOOL, self.DEPTH = L, NS, TS, NPG, NPOOL, DEPTH
        self.branches, self.sample, self.prompt = branches, sample, prompt


def build(cfg):
    nc = bass.Bass("TRN2", target_bir_lowering=False)
    import os
    STOP = int(os.environ.get("KSTOP", "99"))
    L, NS, TS, NPG, NPOOL, DEPTH = cfg.L, cfg.NS, cfg.TS, cfg.NPG, cfg.NPOOL, cfg.DEPTH
    NT, NG = L // 128, L // 512
    NBLK = L // 256
    P = Prog(nc)
    uid = [0]

    def din(name, shape, dt=F32):
        return nc.dram_tensor(name, list(shape), dt, kind="ExternalInput").ap()

    def dout(name, shape, dt=F32):
        return B(nc.dram_tensor(name, list(shape), dt, kind="ExternalOutput").ap())

    def dscr(name, shape, dt):
        return nc.dram_tensor(name, list(shape), dt).ap()

    xp = din("xp", [L, D]); xs = din("xs", [NS * TS, D])
    ck = [din("ck%d" % l_, [NPOOL * 128, 512]) for l_ in range(DEPTH)]; cv = [din("cv%d" % l_, [NPOOL * 128, 512]) for l_ in range(DEPTH)]
    pt = din("pt", [NS, NPG], I32)
    sg_in = din("sg", [DEPTH, NS, 4, 64, 128]); sc_in = din("sc", [DEPTH, NS, 4, 128, 128])
    sn_in = din("sn", [DEPTH, NS, 4, 128]); sm_in = din("sm", [DEPTH, NS, 4]); scv_in = din("scv", [DEPTH, NS, 3, 512])
    w = {}
    for name, shape in (("norm_g", [DEPTH, D]), ("w_in", [DEPTH, D, D_IN]), ("gla_w_a2", [DEPTH, 16, 256]),
                        ("gla_b_a", [DEPTH, 256]), ("gla_norm", [DEPTH, 128]), ("gla_w_out", [DEPTH, 512, D]),
                        ("ml_conv_w", [DEPTH, 4, 512]), ("ml_conv_b", [DEPTH, 512]), ("ml_w_q", [DEPTH, 4, 128, 128]),
                        ("ml_w_k", [DEPTH, 4, 128, 128]), ("ml_w_v", [DEPTH, 4, 128, 128]), ("ml_b_i", [DEPTH, 4]),
                        ("ml_b_f", [DEPTH, 4]), ("ml_norm", [DEPTH, 128]), ("ml_skip", [DEPTH, 512]),
                        ("ml_w_out", [DEPTH, 512, D]), ("mb_q_norm", [DEPTH, 64]), ("mb_k_norm", [DEPTH, 64]),
                        ("mb_w_out", [DEPTH, 512, D]), ("w_out", [DEPTH, D, D])):
        w[name] = din(name, shape)
    consts_d = din("consts", [128, NCONST])

    yp = dout("yp", [L, D]); ys = dout("ys", [NS * TS, D])
    p_gla = dout("p_gla", [DEPTH, 4, 64, 128]); p_mc = dout("p_mc", [DEPTH, 4, 128, 128])
    p_mn = dout("p_mn", [DEPTH, 4, 128]); p_mm = dout("p_mm", [DEPTH, 4]); p_conv = dout("p_conv", [DEPTH, 3, 512])
    p_k = dout("p_k", [DEPTH, L, 512]); p_v = dout("p_v", [DEPTH, L, 512])
    s_gla = dout("s_gla", [DEPTH, NS, 4, 64, 128]); s_mc = dout("s_mc", [DEPTH, NS, 4, 128, 128])
    s_mn = dout("s_mn", [DEPTH, NS, 4, 128]); s_mm = dout("s_mm", [DEPTH, NS, 4]); s_conv = dout("s_conv", [DEPTH, NS, 3, 512])
    s_k = dout("s_k", [DEPTH, NS * TS, 512]); s_v = dout("s_v", [DEPTH, NS * TS, 512])

    hT_d = dscr("hT_d", [max(NG, 1), 128, 8, 512], BF16); hT_dep = [B(None) for _ in range(max(NG, 1))]
    yT_d = dscr("yT_d", [max(NG, 1), 128, 8, 512], BF16); yT_dep = [B(None) for _ in range(max(NG, 1))]
    x1_d = dscr("x1_d", [L, D], F32); x1_dep = [B(None) for _ in range(max(NT, 1))]
    qT_d = dscr("qT_d", [max(NG, 1), 128, 4, 2, 512], BF16); qT_dep = [B(None) for _ in range(max(NG, 1))]
    mbias_d = dscr("mbias_d", [max(NG, 1), 16, 8, 512], BF16); mbias_dep = [B(None) for _ in range(max(NG, 1))]

    with contextlib.ExitStack() as gst:
        cur = [gst]

        def sb(shape, dt, name="t"):
            uid[0] += 1
            return B(cur[0].enter_context(nc.sbuf_tensor("%s_%d" % (name, uid[0]), list(shape), dt)))

        def zsb(shape, dt, name="z"):
            b = sb(shape, dt, name)
            nd = len(shape)
            P.op("pool", lambda e: e.memset(b[(slice(None),) * nd], 0.0), [], [b])
            return b

        def psum(dt=F32, name="ps"):
            uid[0] += 1
            n = 512 if dt == F32 else 1024
            return B(cur[0].enter_context(nc.psum_tensor("%s_%d" % (name, uid[0]), [128, n], dt)), ps=True)

        class Rot:
            def __init__(self, mk, n):
                self.b = [mk() for _ in range(n)]
                self.i = 0

            def next(self):
                b = self.b[self.i % len(self.b)]
                self.i += 1
                return b

        def V(fn, r, wr): P.op("dve", fn, r, wr)
        def A(fn, r, wr): P.op("act", fn, r, wr)
        def G(fn, r, wr): P.op("pool", fn, r, wr)
        def M(fn, r, wr): P.op("pe", fn, r, wr)
        def DMA(fn, r, wr, q="sp", final=False): P.dma(fn, r, wr, q=q, final=final)

        def barrier():
            refs = []
            for e in Prog.ENGS:
                if P.ops[e]:
                    n = len(P.ops[e]) - 1
                    if P.ops[e][n]["kind"] == "c":
                        refs.append(("eng", e, n))
                    else:
                        for m in range(n, -1, -1):
                            if P.ops[e][m]["kind"] == "c":
                                refs.append(("eng", e, m))
                                break
            for k in range(P.n_dma_sems):
                if P.dma_val[k] > 0:
                    refs.append(("dma", k, P.dma_val[k]))
            for e in Prog.ENGS:
                P.ops[e].append(dict(kind="b", fn=None, deps=list(refs), inc=False))

        cst = sb([128, 8 * 128 + 16], F32, "cstf")
        cstb = sb([128, 3 * 128 + 4 * 512], BF16, "cstb")
        IDF = cst[:, 0:128]; TRIS = cst[:, 128:256]; SUTS = cst[:, 256:384]; TRIN = cst[:, 384:512]
        NEGM = cst[:, 512:640]; ONESF = cst[:, 640:768]; SELL128 = cst[:, 768:896]; SELL8 = cst[:, 896:1024]
        BDm = cst[:, 1024:1032]; IOTA = cst[:, 1032:1033]
        IDB = cstb[:, 0:128]; TRI01 = cstb[:, 128:256]; ONESB = cstb[:, 256:384]
        CAUS = [cstb[:, 384 + 512 * r:384 + 512 * (r + 1)] for r in range(4)]
        ohb = sb([16, 16, 128], BF16, "ohb")
        with contextlib.ExitStack() as pst:
            cur[0] = pst
            stg = sb([128, NCONST], F32, "cstg")
            DMA(lambda e: e.dma_start(out=stg[:, :], in_=consts_d[:, :]), [], [stg])
            for dst, src in ((0, K_ID), (128, K_TRIS), (256, K_SUTS), (384, K_TRIN), (512, K_NEGM), (640, K_ONES),
                             (768, K_SELL128), (896, K_SELL8)):
                V(lambda e, dst=dst, src=src: e.tensor_copy(out=cst[:, dst:dst + 128], in_=stg[:, src:src + 128]), [stg], [cst])
            V(lambda e: e.tensor_copy(out=cst[:, 1024:1033], in_=stg[:, K_BD:K_BD + 9]), [stg], [cst])
            for dst, src in ((0, K_ID), (128, K_TRI01), (256, K_ONES)):
                V(lambda e, dst=dst, src=src: e.tensor_copy(out=cstb[:, dst:dst + 128], in_=stg[:, src:src + 128]), [stg], [cstb])
            V(lambda e: e.tensor_copy(out=cstb[:, 384:384 + 2048], in_=stg[:, K_CAUS:K_CAUS + 2048]), [stg], [cstb])
            V(lambda e: e.tensor_copy(out=ohb[:, :, :], in_=stg[0:16, K_OH:K_OH + 2048].rearrange("p (n s) -> p n s", n=16)), [stg], [ohb])
            barrier()
        cur[0] = gst

        NTOK = NS * TS
        hTs = sb([128, 8, NTOK], BF16, "hTs"); yTs = sb([128, 8, NTOK], BF16, "yTs")
        xs1_d = dscr("xs1_d", [NTOK, D], F32); xs1_dep = B(None)

        cast_rr = [0]

        def load_w(dst, c0, ncols, src, stgpool, kc=8, rows=128):
            step = 256
            for a in range(0, ncols, step):
                n = min(step, ncols - a)
                s = stgpool.next()
                DMA(lambda e, s=s, a=a, n=n: e.dma_start(
                    out=s[0:rows, 0:kc * n].rearrange("p (k n) -> p k n", k=kc), in_=src(a, n)), [], [s])
                eng = ("pool", "dve", "act")[cast_rr[0] % 3]
                cast_rr[0] += 1
                if eng == "act":
                    fn = lambda e, s=s, a=a, n=n: e.activation(
                        out=dst[0:rows, :, c0 + a:c0 + a + n], in_=s[0:rows, 0:kc * n].rearrange("p (k n) -> p k n", k=kc), func=AF.Copy)
                else:
                    fn = lambda e, s=s, a=a, n=n: e.tensor_copy(
                        out=dst[0:rows, :, c0 + a:c0 + a + n], in_=s[0:rows, 0:kc * n].rearrange("p (k n) -> p k n", k=kc))
                P.op(eng, fn, [s], [dst])

        class WLoad:
            def __enter__(self):
                self.st = contextlib.ExitStack()
                self.prev = cur[0]
                self.st.__enter__()
                cur[0] = self.st
                self.pool = Rot(lambda: sb([128, 8 * 256], F32, "wstg"), 3)
                cur[0] = self.prev
                return self.pool

            def __exit__(self, *a):
                barrier()
                self.st.__exit__(*a)
                return False

        def w_in_src(l, col0):
            return lambda a, n: w["w_in"][l, :, col0 + a:col0 + a + n].rearrange("(k p) n -> p k n", p=128)

        def tok_proj(po, pob, hT, hTb, t0, T, W, Wb_, c0, n):
            for k in range(8):
                M(lambda e, k=k: e.matmul(po, lhsT=hT[:, k, t0:t0 + T], rhs=W[:, k, c0:c0 + n], start=(k == 0), stop=(k == 7)),
                  [hTb, Wb_], [pob])

        def feat_proj(po, pob, W, Wb_, c0, m, hT, hTb, t0, N):
            for k in range(8):
                M(lambda e, k=k: e.matmul(po, lhsT=W[:, k, c0:c0 + m], rhs=hT[:, k, t0:t0 + N], start=(k == 0), stop=(k == 7)),
                  [hTb, Wb_], [pob])

        def rstd_from_ss(ss_ap, ssb, out_ap, outb, tmp_ap, tmpb, inv_n):
            A(lambda e: e.activation(out=tmp_ap, in_=ss_ap, func=AF.Ln, scale=inv_n, bias=EPS), [ssb], [tmpb])
            A(lambda e: e.activation(out=out_ap, in_=tmp_ap, func=AF.Exp, scale=-0.5), [tmpb], [outb])

        def bvec(src_ap, n, name):
            t = sb([128, n], F32, name)
            DMA(lambda e: e.dma_start(out=t[:, :], in_=src_ap.partition_broadcast(128)), [], [t])
            return t

        def norm_tile(xt, T, gn, hTg, off, wk):
            junk, ss, lnv, rstd, hb, pT = wk["junk"], wk["ss"].next(), wk["lnv"].next(), wk["rstd"].next(), wk["hb"].next(), wk["pT"].next()
            A(lambda e: e.activation(out=junk[0:T, :], in_=xt[0:T, :], func=AF.Square, accum_out=ss[0:T, 0:1]), [xt], [junk, ss])
            rstd_from_ss(ss[0:T, 0:1], ss, rstd[0:T, 0:1], rstd, lnv[0:T, 0:1], lnv, 1.0 / D)
            V(lambda e: e.scalar_tensor_tensor(out=hb[0:T, :], in0=xt[0:T, :], scalar=rstd[0:T, 0:1], in1=gn[0:T, :],
                                               op0=ALU.mult, op1=ALU.mult), [xt, rstd, gn], [hb])
            pTv = pT[:, :].rearrange("p (k t) -> p k t", k=8)
            for k in range(8):
                M(lambda e, k=k: e.transpose(out=pTv[:, k, 0:T], in_=hb[0:T, k * 128:(k + 1) * 128], identity=IDB[0:T, 0:T]),
                  [hb, cstb], [pT])
            A(lambda e: e.activation(out=hTg[:, :, off:off + T], in_=pTv[:, :, 0:T], func=AF.Copy), [pT], [hTg])

        def norm_work():
            return dict(junk=sb([128, D], BF16, "junk"), ss=Rot(lambda: sb([128, 1], F32, "ss"), 2),
                        lnv=Rot(lambda: sb([128, 1], F32, "lnv"), 2), rstd=Rot(lambda: sb([128, 1], F32, "rstd"), 2),
                        hb=Rot(lambda: sb([128, D], BF16, "hb"), 2), pT=Rot(lambda: psum(BF16, "pTn"), 2))

        def phase_N(l):
            with contextlib.ExitStack() as pst:
                cur[0] = pst
                gn = bvec(w["norm_g"][l:l + 1, :], D, "gn")
                wk = norm_work()
                xpool = Rot(lambda: sb([128, D], F32, "xt"), 3)
                hpool = Rot(lambda: sb([128, 8, 512], BF16, "hTg"), 2)
                for g in range(NG):
                    hTg = hpool.next()
                    for j in range(4):
                        i = 4 * g + j
                        xt = xpool.next()
                        DMA(lambda e, xt=xt, i=i: e.dma_start(out=xt[:, :], in_=xp[i * 128:(i + 1) * 128, :]), [], [xt])
                        norm_tile(xt, 128, gn, hTg, j * 128, wk)
                    DMA(lambda e, hTg=hTg, g=g: e.dma_start(out=hT_d[g], in_=hTg[:, :, :]), [hTg], [hT_dep[g]])
                if cfg.sample:
                    xt = xpool.next()
                    DMA(lambda e, xt=xt: e.dma_start(out=xt[0:NTOK, :], in_=xs[:, :]), [], [xt])
                    norm_tile(xt, NTOK, gn, hTs, 0, wk)
                barrier()
            cur[0] = gst

        def branch_out(ZT_list, Zb_, Wo, Wob, Wgt, Wgtb, hTg, N, yTg, first, wk, kdim=128):
            for c in range(8):
                pg = wk["pG"].next()
                feat_proj(pg[:, 0:N], pg, Wgt, Wgtb, c * 128, 128, hTg, hTg, 0, N)
                sg = wk["sig"].next()
                A(lambda e, pg=pg, sg=sg: e.activation(out=sg[:, 0:N], in_=pg[:, 0:N], func=AF.Sigmoid), [pg], [sg])
                py = wk["pY"].next()
                nterm = len(ZT_list)
                for ti, (lhs_fn, rhs_ap) in enumerate(ZT_list):
                    M(lambda e, py=py, lhs_fn=lhs_fn, rhs_ap=rhs_ap, ti=ti, c=c: e.matmul(
                        py[:, 0:N], lhsT=lhs_fn(c), rhs=rhs_ap, start=(ti == 0), stop=(ti == nterm - 1)), [Zb_, Wob], [py])
                if first:
                    V(lambda e, py=py, sg=sg, c=c: e.tensor_tensor(out=yTg[:, c, 0:N], in0=py[:, 0:N], in1=sg[:, 0:N], op=ALU.mult),
                      [py, sg], [yTg])
                else:
                    tm = wk["tmpy"].next()
                    V(lambda e, py=py, sg=sg, tm=tm: e.tensor_tensor(out=tm[:, 0:N], in0=py[:, 0:N], in1=sg[:, 0:N], op=ALU.mult),
                      [py, sg], [tm])
                    G(lambda e, tm=tm, c=c: e.tensor_tensor(out=yTg[:, c, 0:N], in0=yTg[:, c, 0:N], in1=tm[:, 0:N], op=ALU.add),
                      [tm, yTg], [yTg])

        def branch_work(pR):
            return dict(pG=pR, pY=pR,
                        sig=Rot(lambda: sb([128, 512], F32, "sig"), 2), tmpy=Rot(lambda: sb([128, 512], F32, "tmpy"), 2))

        def load_gate_w(Wgt, l, bidx, stgpool):
            load_w(Wgt, 0, D, w_in_src(l, C_GATE + bidx * D), stgpool)

        def load_outw(Wo, name, l, stgpool):
            load_w(Wo, 0, D, lambda a, n: w[name][l, :, a:a + n].rearrange("(k p) n -> p k n", p=128), stgpool, kc=4)

        def small_bf16(src_ap, rows, cols, name):
            s = sb([rows, cols], F32, name + "_f")
            d = sb([rows, cols], BF16, name)
            DMA(lambda e: e.dma_start(out=s[:, :], in_=src_ap), [], [s])
            V(lambda e: e.tensor_copy(out=d[:, :], in_=s[:, :]), [s], [d])
            return d

        def gla_consts(l):
            return dict(wa2=small_bf16(w["gla_w_a2"][l], 16, 256, "wa2"),
                        ba=small_bf16(w["gla_b_a"][l:l + 1, :], 1, 256, "ba"),
                        gnorm=bvec(w["gla_norm"][l:l + 1, :], 128, "gnorm"))

        def gla_work():
            return dict(
                pTr=psum(BF16, "pTr"), pB=psum(F32, "pB"), pA=psum(F32, "pA"), pO=psum(F32, "pO"), pSU=psum(F32, "pSU"),
                pR=Rot(lambda: psum(F32, "pR"), 3),
                qT=sb([128, 2, 512], F32, "qTs"), kT=sb([128, 2, 512], F32, "kTs"), aT=sb([16, 512], BF16, "aTs"),
                sp=Rot(lambda: sb([128, 256], F32, "sp"), 2), E1=Rot(lambda: sb([128, 2, 128], F32, "E1"), 2),
                E2=Rot(lambda: sb([128, 2, 128], F32, "E2"), 2), E3=Rot(lambda: sb([128, 256], F32, "E3"), 2),
                qt=Rot(lambda: zsb([128, 2, 2, 128], BF16, "qtm"), 2), kt=Rot(lambda: sb([128, 2, 128], BF16, "kt"), 2),
                vtok=Rot(lambda: sb([128, 512], BF16, "vtok"), 2), kh=Rot(lambda: sb([128, 256], BF16, "kh"), 2),
                sz=Rot(lambda: sb([128, 512], F32, "sz"), 2), AT=Rot(lambda: sb([128, 4, 128], BF16, "AT"), 2),
                osq=Rot(lambda: sb([128, 512], F32, "osq"), 1), ss4=Rot(lambda: sb([128, 4], F32, "ss4"), 2),
                ln4=Rot(lambda: sb([128, 4], F32, "ln4"), 2), rs4=Rot(lambda: sb([128, 4], F32, "rs4"), 2),
                t1=Rot(lambda: sb([128, 512], F32, "t1"), 2), g1=Rot(lambda: sb([128, 512], F32, "g1"), 2),
                Zg=Rot(lambda: sb([128, 512], BF16, "Zg"), 2))

        def gla_group_prep(Wg, hTg, N, wk, hoff=0):
            for c in range(2):
                pf = wk["pR"].next()
                feat_proj(pf[:, 0:N], pf, Wg, Wg, C_GQ + c * 128, 128, hTg, hTg, hoff, N)
                A(lambda e, pf=pf, c=c: e.activation(out=wk["qT"][:, c, 0:N], in_=pf[:, 0:N], func=AF.Copy), [pf], [wk["qT"]])
                pf = wk["pR"].next()
                feat_proj(pf[:, 0:N], pf, Wg, Wg, C_GK + c * 128, 128, hTg, hTg, hoff, N)
                V(lambda e, pf=pf, c=c: e.tensor_copy(out=wk["kT"][:, c, 0:N], in_=pf[:, 0:N]), [pf], [wk["kT"]])
            pf = wk["pR"].next()
            feat_proj(pf[0:16, 0:N], pf, Wg, Wg, C_GA, 16, hTg, hTg, hoff, N)
            A(lambda e, pf=pf: e.activation(out=wk["aT"][0:16, 0:N], in_=pf[0:16, 0:N], func=AF.Copy), [pf], [wk["aT"]])

        def gla_tile(T, t0, Wg, hTg, gc, S2, S2b, ZgT, wk, hoff=None, zoff=None):
            import os
            KSTEP = int(os.environ.get('KSTEP', '99'))
            hc = t0 if hoff is None else hoff
            zc = t0 if zoff is None else zoff
            pTrb, pB, pA, pO, pSU = wk["pTr"], wk["pB"], wk["pA"], wk["pO"], wk["pSU"]
            pMisc = wk["pR"].next()
            qT, kT, aT = wk["qT"], wk["kT"], wk["aT"]
            wa2, ba, gnorm = gc["wa2"], gc["ba"], gc["gnorm"]
            M(lambda e: e.matmul(pMisc[0:T, 0:256], lhsT=aT[0:16, t0:t0 + T], rhs=wa2[0:16, :], start=True, stop=False), [aT, wa2], [pMisc])
            M(lambda e: e.matmul(pMisc[0:T, 0:256], lhsT=ONESB[0:1, 0:T], rhs=ba[0:1, :], start=False, stop=True), [cstb, ba], [pMisc])
            if KSTEP < 2:
                return
            sp = wk["sp"].next()
            A(lambda e: e.activation(out=sp[0:T, :], in_=pMisc[0:T, 0:256], func=AF.Exp, scale=-1.0), [pMisc], [sp])
            A(lambda e: e.activation(out=sp[0:T, :], in_=sp[0:T, :], func=AF.Ln, bias=1.0), [sp], [sp])
            if KSTEP < 3:
                return
            pBv = pB[:, 0:256].rearrange("p (c t) -> p c t", c=2)
            for c in range(2):
                M(lambda e, c=c: e.matmul(pBv[:, c, 0:T], lhsT=sp[0:T, c * 128:(c + 1) * 128], rhs=TRIS[0:T, 0:T], start=True, stop=True), [sp, cst], [pB])
            M(lambda e: e.matmul(pB[0:T, 256:512], lhsT=SUTS[0:T, 0:T], rhs=sp[0:T, 0:256], start=True, stop=True), [sp, cst], [pB])
            E1, E2, E3 = wk["E1"].next(), wk["E2"].next(), wk["E3"].next()
            A(lambda e: e.activation(out=E1[:, :, 0:T], in_=pBv[:, :, 0:T], func=AF.Exp), [pB], [E1])
            A(lambda e: e.activation(out=E2[:, :, 0:T], in_=pBv[:, :, 0:T], func=AF.Exp, scale=-1.0), [pB], [E2])
            A(lambda e: e.activation(out=E3[0:T, :], in_=pB[0:T, 256:512], func=AF.Exp), [pB], [E3])
            if KSTEP < 5:
                return
            qt, kt = wk["qt"].next(), wk["kt"].next()
            for hl in range(2):
                r = hl * 64
                V(lambda e, hl=hl, r=r: e.scalar_tensor_tensor(out=qt[r:r + 64, :, hl, 0:T], in0=qT[r:r + 64, :, t0:t0 + T], scalar=0.125,
                                                               in1=E1[r:r + 64, :, 0:T], op0=ALU.mult, op1=ALU.mult), [qT, E1], [qt])
            V(lambda e: e.tensor_tensor(out=kt[:, :, 0:T], in0=kT[:, :, t0:t0 + T], in1=E2[:, :, 0:T], op=ALU.mult), [kT, E2], [kt])
            if KSTEP < 6:
                return
            vtok, kh, sz = wk["vtok"].next(), wk["kh"].next(), wk["sz"].next()
            pv = wk["pR"].next()
            tok_proj(pv[0:T, 0:512], pv, hTg, hTg, hc, T, Wg, Wg, C_GV, 512)
            A(lambda e: e.activation(out=vtok[0:T, :], in_=pv[0:T, :], func=AF.Copy), [pv], [vtok])
            pk = wk["pR"].next()
            tok_proj(pk[0:T, 0:256], pk, hTg, hTg, hc, T, Wg, Wg, C_GK, 256)
            V(lambda e: e.tensor_tensor(out=kh[0:T, :], in0=pk[0:T, 0:256], in1=E3[0:T, :], op=ALU.mult), [pk, E3], [kh])
            pz = wk["pR"].next()
            tok_proj(pz[0:T, 0:512], pz, hTg, hTg, hc, T, Wg, Wg, C_GZ, 512)
            A(lambda e: e.activation(out=sz[0:T, :], in_=pz[0:T, :], func=AF.Silu), [pz], [sz])
            if KSTEP < 7:
                return
            pAv = pA[:, :].rearrange("p (h t) -> p h t", h=4)
            for h in range(4):
                c, r = h // 2, (h % 2) * 64
                M(lambda e, h=h, c=c: e.matmul(pAv[0:T, h, 0:T], lhsT=kt[:, c, 0:T], rhs=qt[:, c, h % 2, 0:T], start=True, stop=True),
                  [kt, qt], [pA])
            AT = wk["AT"].next()
            V(lambda e: e.tensor_tensor(out=AT[0:T, :, 0:T], in0=pAv[0:T, :, 0:T],
                                        in1=TRI01[0:T, 0:T].unsqueeze(1).to_broadcast([T, 4, T]), op=ALU.mult), [pA, cstb], [AT])
            if KSTEP < 8:
                return
            for h in range(4):
                c, r = h // 2, (h % 2) * 64
                M(lambda e, h=h: e.matmul(pO[0:T, h * 128:(h + 1) * 128], lhsT=AT[0:T, h, 0:T], rhs=vtok[0:T, h * 128:(h + 1) * 128],
                                          start=True, stop=False), [AT, vtok], [pO])
                M(lambda e, h=h, c=c: e.matmul(pO[0:T, h * 128:(h + 1) * 128], lhsT=qt[:, c, h % 2, 0:T], rhs=S2b[:, c, :],
                                               start=False, stop=True), [qt, S2b], [pO])
            if KSTEP < 9:
                return
            for c in range(2):
                M(lambda e, c=c: e.matmul(pSU[:, c * 256:(c + 1) * 256], lhsT=kh[0:T, c * 128:(c + 1) * 128], rhs=vtok[0:T, c * 256:(c + 1) * 256],
                                          start=True, stop=True), [kh, vtok], [pSU])
            for c in range(2):
                for hl in range(2):
                    r = hl * 64
                    V(lambda e, c=c, hl=hl, r=r: e.scalar_tensor_tensor(
                        out=S2[r:r + 64, c, :], in0=S2[r:r + 64, c, :], scalar=E1[r:r + 64, c, T - 1:T],
                        in1=pSU[r:r + 64, c * 256 + hl * 128:c * 256 + (hl + 1) * 128], op0=ALU.mult, op1=ALU.add), [S2, E1, pSU], [S2])
            G(lambda e: e.tensor_copy(out=S2b[:, :, :], in_=S2[:, :, :]), [S2], [S2b])
            if KSTEP < 10:
                return
            osq, ss4, ln4, rs4, t1, g1, Zg = [wk[k].next() for k in ("osq", "ss4", "ln4", "rs4", "t1", "g1", "Zg")]
            A(lambda e: e.activation(out=osq[0:T, :], in_=pO[0:T, :], func=AF.Square), [pO], [osq])
            V(lambda e: e.tensor_reduce(out=ss4[0:T, :], in_=osq[0:T, :].rearrange("p (h v) -> p h v", h=4), axis=AX.X, op=ALU.add), [osq], [ss4])
            rstd_from_ss(ss4[0:T, :], ss4, rs4[0:T, :], rs4, ln4[0:T, :], ln4, 1.0 / 128)
            V(lambda e: e.tensor_tensor(out=t1[0:T, :].rearrange("p (h v) -> p h v", h=4), in0=pO[0:T, :].rearrange("p (h v) -> p h v", h=4),
                                        in1=rs4[0:T, :].unsqueeze(2).to_broadcast([T, 4, 128]), op=ALU.mult), [pO, rs4], [t1])
            G(lambda e: e.tensor_tensor(out=g1[0:T, :].rearrange("p (h v) -> p h v", h=4), in0=sz[0:T, :].rearrange("p (h v) -> p h v", h=4),
                                        in1=gnorm[0:T, :].unsqueeze(1).to_broadcast([T, 4, 128]), op=ALU.mult), [sz, gnorm], [g1])
            V(lambda e: e.tensor_tensor(out=Zg[0:T, :], in0=t1[0:T, :], in1=g1[0:T, :], op=ALU.mult), [t1, g1], [Zg])
            if KSTEP < 11:
                return
            pTr = pTrb[:, 0:512].rearrange("p (k t) -> p k t", k=4)
            for k4 in range(4):
                M(lambda e, k4=k4: e.transpose(out=pTr[:, k4, 0:T], in_=Zg[0:T, k4 * 128:(k4 + 1) * 128], identity=IDB[0:T, 0:T]), [Zg, cstb], [pTrb])
            A(lambda e: e.activation(out=ZgT[:, :, zc:zc + T], in_=pTr[:, :, 0:T], func=AF.Copy), [pTrb], [ZgT])

        def phase_G(l, first):
            with contextlib.ExitStack() as pst:
                cur[0] = pst
                Wg = sb([128, 8, 1552], BF16, "Wg"); Wgt = sb([128, 8, D], BF16, "Wgt"); Wo = sb([128, 4, D], BF16, "Wo")
                with WLoad() as stg:
                    load_w(Wg, 0, 1552, w_in_src(l, C_GQ), stg)
                    load_gate_w(Wgt, l, 0, stg)
                    load_outw(Wo, "gla_w_out", l, stg)
                gc = gla_consts(l)
                wk = gla_work(); bw = branch_work(wk["pR"])
                S2 = sb([128, 2, 128], F32, "S2"); S2b = sb([128, 2, 128], BF16, "S2b")
                V(lambda e: e.memset(S2[:, :, :], 0.0), [], [S2]); V(lambda e: e.memset(S2b[:, :, :], 0.0), [], [S2b])
                hpool = Rot(lambda: sb([128, 8, 512], BF16, "hTg"), 2)
                ypool = Rot(lambda: sb([128, 8, 512], BF16, "yTg"), 2)
                zpool = Rot(lambda: sb([128, 4, 512], BF16, "ZgT"), 2)
                for g in range(NG):
                    hTg = hpool.next(); yTg = ypool.next(); ZgT = zpool.next()
                    DMA(lambda e, hTg=hTg, g=g: e.dma_start(out=hTg[:, :, :], in_=hT_d[g]), [hT_dep[g]], [hTg])
                    if not first:
                        DMA(lambda e, yTg=yTg, g=g: e.dma_start(out=yTg[:, :, :], in_=yT_d[g]), [yT_dep[g]], [yTg])
                    gla_group_prep(Wg, hTg, 512, wk)
                    for j in range(4):
                        gla_tile(128, j * 128, Wg, hTg, gc, S2, S2b, ZgT, wk)
                    terms = [((lambda c, k4=k4: Wo[:, k4, c * 128:(c + 1) * 128]), ZgT[:, k4, :]) for k4 in range(4)]
                    if STOP >= 4:
                        branch_out(terms, ZgT, Wo, Wo, Wgt, Wgt, hTg, 512, yTg, first, bw)
                    DMA(lambda e, yTg=yTg, g=g: e.dma_start(out=yT_d[g], in_=yTg[:, :, :]), [yTg], [yT_dep[g]])
                DMA(lambda e: e.dma_start(out=p_gla[l].rearrange("(c hl) d v -> (hl d) c v", hl=2), in_=S2[:, :, :]), [S2], [p_gla], final=True)
                if cfg.sample:
                    ZgTs = zpool.next()
                    for sq_ in range(NS):
                        DMA(lambda e, sq_=sq_: e.dma_start(out=S2[:, :, :], in_=sg_in[l, sq_].rearrange("(c hl) d v -> (hl d) c v", hl=2)), [], [S2])
                        G(lambda e: e.tensor_copy(out=S2b[:, :, :], in_=S2[:, :, :]), [S2], [S2b])
                        gla_group_prep(Wg, hTs, TS, wk, hoff=sq_ * TS)
                        gla_tile(TS, 0, Wg, hTs, gc, S2, S2b, ZgTs, wk, hoff=sq_ * TS, zoff=sq_ * TS)
                        DMA(lambda e, sq_=sq_: e.dma_start(out=s_gla[l, sq_].rearrange("(c hl) d v -> (hl d) c v", hl=2), in_=S2[:, :, :]), [S2], [], final=True)
                    terms = [((lambda c, k4=k4: Wo[:, k4, c * 128:(c + 1) * 128]), ZgTs[:, k4, 0:NTOK]) for k4 in range(4)]
                    branch_out(terms, ZgTs, Wo, Wo, Wgt, Wgt, hTs, NTOK, yTs, first, bw)
                barrier()
            cur[0] = gst

        def phase_O(l):
            last = (l == DEPTH - 1)
            with contextlib.ExitStack() as pst:
                cur[0] = pst
                Wout = sb([128, 8, D], BF16, "Wout")
                with WLoad() as stg:
                    load_w(Wout, 0, D, lambda a, n: w["w_out"][l, :, a:a + n].rearrange("(k p) n -> p k n", p=128), stg)
                pR = Rot(lambda: psum(F32, "pRo"), 4)
                xpool = Rot(lambda: sb([128, D], F32, "xt"), 3)
                xnpool = Rot(lambda: sb([128, D], F32, "xn"), 3)
                ypool = Rot(lambda: sb([128, 8, 512], BF16, "yTg"), 2)
                if not last:
                    gn = bvec(w["norm_g"][l + 1:l + 2, :], D, "gn")
                    wkn = norm_work()
                    hpool = Rot(lambda: sb([128, 8, 512], BF16, "hTg"), 2)
                for g in range(NG):
                    yTg = ypool.next()
                    DMA(lambda e, yTg=yTg, g=g: e.dma_start(out=yTg[:, :, :], in_=yT_d[g]), [yT_dep[g]], [yTg])
                    if not last:
                        hTg = hpool.next()
                    for j in range(4):
                        i = 4 * g + j
                        xt = xpool.next(); xn = xnpool.next()
                        if l == 0:
                            DMA(lambda e, xt=xt, i=i: e.dma_start(out=xt[:, :], in_=xp[i * 128:(i + 1) * 128, :]), [], [xt])
                        else:
                            DMA(lambda e, xt=xt, i=i: e.dma_start(out=xt[:, :], in_=x1_d[i * 128:(i + 1) * 128, :]), [x1_dep[i]], [xt])
                        for half in range(2):
                            py = pR.next()
                            for c in range(8):
                                M(lambda e, py=py, c=c, j=j, half=half, yTg=yTg: e.matmul(
                                    py[:, 0:512], lhsT=yTg[:, c, j * 128:(j + 1) * 128], rhs=Wout[:, c, half * 512:(half + 1) * 512],
                                    start=(c == 0), stop=(c == 7)), [yTg, Wout], [py])
                            V(lambda e, py=py, xt=xt, xn=xn, half=half: e.tensor_tensor(
                                out=xn[:, half * 512:(half + 1) * 512], in0=py[:, 0:512], in1=xt[:, half * 512:(half + 1) * 512], op=ALU.add),
                              [py, xt], [xn])
                        if last:
                            DMA(lambda e, xn=xn, i=i: e.dma_start(out=yp[i * 128:(i + 1) * 128, :], in_=xn[:, :]), [xn], [], final=True)
                        else:
                            DMA(lambda e, xn=xn, i=i: e.dma_start(out=x1_d[i * 128:(i + 1) * 128, :], in_=xn[:, :]), [xn], [x1_dep[i]])
                            norm_tile(xn, 128, gn, hTg, j * 128, wkn)
                    if not last:
                        DMA(lambda e, hTg=hTg, g=g: e.dma_start(out=hT_d[g], in_=hTg[:, :, :]), [hTg], [hT_dep[g]])
                if cfg.sample:
                    xt = xpool.next(); xn = xnpool.next()
                    if l == 0:
                        DMA(lambda e, xt=xt: e.dma_start(out=xt[0:NTOK, :], in_=xs[:, :]), [], [xt])
                    else:
                        DMA(lambda e, xt=xt: e.dma_start(out=xt[0:NTOK, :], in_=xs1_d[:, :]), [xs1_dep], [xt])
                    for half in range(2):
                        py = pR.next()
                        for c in range(8):
                            M(lambda e, py=py, c=c, half=half: e.matmul(py[0:NTOK, 0:512], lhsT=yTs[:, c, 0:NTOK], rhs=Wout[:, c, half * 512:(half + 1) * 512],
                                                                        start=(c == 0), stop=(c == 7)), [yTs, Wout], [py])
                        V(lambda e, py=py, xt=xt, xn=xn, half=half: e.tensor_tensor(out=xn[0:NTOK, half * 512:(half + 1) * 512], in0=py[0:NTOK, 0:512],
                                                                                   in1=xt[0:NTOK, half * 512:(half + 1) * 512], op=ALU.add), [py, xt], [xn])
                    if last:
                        DMA(lambda e, xn=xn: e.dma_start(out=ys[:, :], in_=xn[0:NTOK, :]), [xn], [], final=True)
                    else:
                        DMA(lambda e, xn=xn: e.dma_start(out=xs1_d[:, :], in_=xn[0:NTOK, :]), [xn], [xs1_dep])
                        norm_tile(xn, NTOK, gn, hTs, 0, wkn)
                barrier()
            cur[0] = gst


        RS128 = 128.0 ** -0.5

        def ml_consts(l):
            cw = sb([128, 4, 4], F32, "cw"); cb = sb([128, 4], F32, "cb")
            for c in range(4):
                for j in range(4):
                    DMA(lambda e, c=c, j=j: e.dma_start(out=cw[:, c, j:j + 1], in_=w["ml_conv_w"][l, j, c * 128:(c + 1) * 128].rearrange("(p o) -> p o", o=1)), [], [cw])
                DMA(lambda e, c=c: e.dma_start(out=cb[:, c:c + 1], in_=w["ml_conv_b"][l, c * 128:(c + 1) * 128].rearrange("(p o) -> p o", o=1)), [], [cb])
            return dict(cw=cw, cb=cb, bi=bvec(w["ml_b_i"][l:l + 1, :], 4, "bi"), bf=bvec(w["ml_b_f"][l:l + 1, :], 4, "bf"),
                        mlnorm=bvec(w["ml_norm"][l:l + 1, :], 128, "mlnorm"), skip=bvec(w["ml_skip"][l:l + 1, :], 512, "skip"))

        def ml_weights_alloc():
            return sb([128, 4, 128], BF16, "wq"), sb([128, 4, 128], BF16, "wk"), sb([128, 4, 128], BF16, "wv")

        def ml_weights(l, stg, wts):
            wq, wk_, wv = wts
            for dst, nm in ((wq, "ml_w_q"), (wk_, "ml_w_k"), (wv, "ml_w_v")):
                load_w(dst, 0, 128, (lambda a, n, nm=nm: w[nm][l, :, :, a:a + n].rearrange("h d n -> d h n")), stg, kc=4)
            return wq, wk_, wv

        def ml_work():
            def ones_aug():
                b = sb([128, 4, 128], BF16, "vaug")
                return b
            return dict(
                pR=Rot(lambda: psum(F32, "pR"), 2), pK=psum(F32, "pK"), pBF=psum(BF16, "pBF"), pX1=psum(F32, "pX1"),
                pX2=psum(F32, "pX2"), pX3=psum(F32, "pX3"), pS=psum(F32, "pS"),
                xT=sb([128, 4, 3 + 512], F32, "xTe"), xTb=sb([128, 4, 512], BF16, "xTb"), mcT=sb([128, 4, 512], BF16, "mcT"),
                qTm=sb([128, 4, 512], BF16, "qTm"), kTm=sb([128, 4, 512], BF16, "kTm"),
                acc=Rot(lambda: sb([128, 512], F32, "acc"), 2),
                vaug=Rot(lambda: sb([128, 4, 128], BF16, "vaug"), 2),
                g4=Rot(lambda: sb([128, 4], F32, "g4"), 24), t12=Rot(lambda: sb([128, 12], F32, "t12"), 2),
                dg=Rot(lambda: sb([128, 4, 128], F32, "dg"), 1), Dm=Rot(lambda: sb([128, 4, 128], F32, "Dm"), 2),
                Wt=Rot(lambda: sb([128, 4, 128], F32, "Wt"), 1), Sqk=Rot(lambda: sb([128, 4, 128], BF16, "Sqk"), 2),
                SqkT=Rot(lambda: sb([128, 4, 128], BF16, "SqkT"), 2), P1s=Rot(lambda: sb([128, 512], F32, "P1s"), 1),
                f5=Rot(lambda: sb([128, 512], F32, "f5"), 6), kw=Rot(lambda: sb([128, 4, 128], BF16, "kw"), 2),
                Zm=Rot(lambda: sb([128, 512], BF16, "Zm"), 2))

        def ml_group_prep(Wm, hTg, N, mc_, wts, wk, hoff=0):
            import os
            if int(os.environ.get('KSTEP', '99')) < 0:
                return
            KSUB = int(os.environ.get('KSUB', '99'))
            wq, wk_, wv = wts
            xT, xTb, mcT, qTm, kTm = wk["xT"], wk["xTb"], wk["mcT"], wk["qTm"], wk["kTm"]
            cw, cb = mc_["cw"], mc_["cb"]
            for c in range(4):
                pf = wk["pR"].next()
                feat_proj(pf[:, 0:N], pf, Wm, Wm, c * 128, 128, hTg, hTg, hoff, N)
                KVAR = os.environ.get('KVAR', '')
                if KVAR != 'noact':
                    A(lambda e, pf=pf, c=c: e.activation(out=xT[:, c, 3:3 + N], in_=pf[:, 0:N], func=AF.Copy), [pf], [xT])
                if KVAR != 'nodve':
                    V(lambda e, pf=pf, c=c: e.tensor_copy(out=xTb[:, c, 0:N], in_=pf[:, 0:N]), [pf], [xTb])
                if KSUB < 2:
                    continue
                acc = wk["acc"].next()
                V(lambda e, c=c, acc=acc: e.tensor_scalar(out=acc[:, 0:N], in0=xT[:, c, 0:N], scalar1=cw[:, c, 0:1], scalar2=cb[:, c:c + 1],
                                                          op0=ALU.mult, op1=ALU.add), [xT, cw, cb], [acc])
                for j in range(1, 4):
                    V(lambda e, c=c, j=j, acc=acc: e.scalar_tensor_tensor(out=acc[:, 0:N], in0=xT[:, c, j:j + N], scalar=cw[:, c, j:j + 1], in1=acc[:, 0:N],
                                                                         op0=ALU.mult, op1=ALU.add), [xT, cw, acc], [acc])
                if KSUB < 3:
                    continue
                A(lambda e, c=c, acc=acc: e.activation(out=mcT[:, c, 0:N], in_=acc[:, 0:N], func=AF.Silu), [acc], [mcT])
            if KSUB < 4:
                return
            for h in range(4):
                pf = wk["pR"].next()
                M(lambda e, pf=pf, h=h: e.matmul(pf[:, 0:N], lhsT=wq[:, h, :], rhs=mcT[:, h, 0:N], start=True, stop=True), [wq, mcT], [pf])
                A(lambda e, pf=pf, h=h: e.activation(out=qTm[:, h, 0:N], in_=pf[:, 0:N], func=AF.Copy), [pf], [qTm])
                pf = wk["pR"].next()
                M(lambda e, pf=pf, h=h: e.matmul(pf[:, 0:N], lhsT=wk_[:, h, :], rhs=mcT[:, h, 0:N], start=True, stop=True), [wk_, mcT], [pf])
                A(lambda e, pf=pf, h=h: e.activation(out=kTm[:, h, 0:N], in_=pf[:, 0:N], func=AF.Identity, scale=RS128), [pf], [kTm])

        def ml_carry(N, wk):
            xT = wk["xT"]
            V(lambda e: e.tensor_copy(out=xT[:, :, 0:3], in_=xT[:, :, N:N + 3]), [xT], [xT])

        def ml_tile(T, t0, Wm, hTg, mc_, wts, st, ZmT, wk, hoff=None, zoff=None):
            import os
            KSTEP = int(os.environ.get('KSTEP', '99'))
            if KSTEP < 1:
                return
            hc = t0 if hoff is None else hoff
            zc = t0 if zoff is None else zoff
            wq, wk_, wv = wts
            CT, CTb, nT, nb, mprev = st["CT"], st["CTb"], st["nT"], st["nb"], st["mprev"]
            xTb, mcT, qTm, kTm = wk["xTb"], wk["mcT"], wk["qTm"], wk["kTm"]
            pK, pBF, pX1, pX2, pX3, pS = wk["pK"], wk["pBF"], wk["pX1"], wk["pX2"], wk["pX3"], wk["pS"]
            g4 = lambda: wk["g4"].next()
            SELL = SELL128 if T == 128 else SELL8
            v3 = lambda ap: ap.rearrange("p (h v) -> p h v", h=4)
            pif = wk["pR"].next()
            tok_proj(pif[0:T, 0:8], pif, hTg, hTg, hc, T, Wm, Wm, 512, 8)
            uf, ig = g4(), g4()
            V(lambda e: e.tensor_tensor(out=uf[0:T, :], in0=pif[0:T, 4:8], in1=mc_["bf"][0:T, :], op=ALU.add), [pif, mc_["bf"]], [uf])
            V(lambda e: e.tensor_tensor(out=ig[0:T, :], in0=pif[0:T, 0:4], in1=mc_["bi"][0:T, :], op=ALU.add), [pif, mc_["bi"]], [ig])
            spf = g4()
            A(lambda e: e.activation(out=spf[0:T, :], in_=uf[0:T, :], func=AF.Exp, scale=-1.0), [uf], [spf])
            A(lambda e: e.activation(out=spf[0:T, :], in_=spf[0:T, :], func=AF.Ln, bias=1.0), [spf], [spf])
            for h in range(4):
                M(lambda e, h=h: e.matmul(pK[0:T, h * 128:(h + 1) * 128], lhsT=mcT[:, h, t0:t0 + T], rhs=wk_[:, h, :], start=True, stop=True), [mcT, wk_], [pK])
            pv = wk["pR"].next()
            for h in range(4):
                M(lambda e, h=h: e.matmul(pv[0:T, h * 128:(h + 1) * 128], lhsT=xTb[:, h, t0:t0 + T], rhs=wv[:, h, :], start=True, stop=True), [xTb, wv], [pv])
            vaug = wk["vaug"].next()
            A(lambda e: e.activation(out=vaug[0:T, :, :], in_=v3(pv[0:T, :]), func=AF.Copy), [pv], [vaug])
            pmc = pBF[:, 0:512].rearrange("p (c f) -> p c f", c=4)
            pTr = pBF[:, 512:1024].rearrange("p (h t) -> p h t", h=4)
            for c in range(4):
                M(lambda e, c=c: e.transpose(out=pmc[0:T, c, :], in_=mcT[:, c, t0:t0 + T], identity=IDB[:, :]), [mcT, cstb], [pBF])
            mcs = wk["f5"].next()
            V(lambda e: e.tensor_tensor(out=mcs[0:T, :], in0=pBF[0:T, 0:512], in1=mc_["skip"][0:T, :], op=ALU.mult), [pBF, mc_["skip"]], [mcs])
            if KSTEP < 3:
                return
            M(lambda e: e.matmul(pS[0:T, 0:4], lhsT=TRIN[0:T, 0:T], rhs=spf[0:T, 0:4], start=True, stop=True), [spf, cst], [pS])
            Fs, r = g4(), g4()
            V(lambda e: e.tensor_copy(out=Fs[0:T, :], in_=pS[0:T, 0:4]), [pS], [Fs])
            V(lambda e: e.tensor_tensor(out=r[0:T, :], in0=ig[0:T, :], in1=Fs[0:T, :], op=ALU.subtract), [ig, Fs], [r])
            if KSTEP < 4:
                return
            dg = wk["dg"].next()
            V(lambda e: e.tensor_tensor(out=dg[0:T, :, 0:T], in0=IDF[0:T, 0:T].unsqueeze(1).to_broadcast([T, 4, T]),
                                        in1=r[0:T, :].unsqueeze(2).to_broadcast([T, 4, T]), op=ALU.mult), [cst, r], [dg])
            pRb = pX1[:, :].rearrange("p (h t) -> p h t", h=4)
            for h in range(4):
                M(lambda e, h=h: e.matmul(pRb[0:T, h, 0:T], lhsT=ONESF[0:T, 0:T], rhs=dg[0:T, h, 0:T], start=True, stop=True), [dg, cst], [pX1])
            if KSTEP < 5:
                return
            Dm = wk["Dm"].next()
            for h in range(4):
                V(lambda e, h=h: e.scalar_tensor_tensor(out=Dm[0:T, h, 0:T], in0=pRb[0:T, h, 0:T], scalar=Fs[0:T, h:h + 1], in1=NEGM[0:T, 0:T],
                                                        op0=ALU.add, op1=ALU.add), [pX1, Fs, cst], [Dm])
            mx, lin, t12, negm, win, enm = g4(), g4(), wk["t12"].next(), g4(), g4(), g4()
            V(lambda e: e.tensor_reduce(out=mx[0:T, :], in_=Dm[0:T, :, 0:T], axis=AX.X, op=ALU.max), [Dm], [mx])
            V(lambda e: e.tensor_tensor(out=lin[0:T, :], in0=Fs[0:T, :], in1=mprev[0:T, :], op=ALU.add), [Fs, mprev], [lin])
            V(lambda e: e.tensor_tensor(out=t12[0:T, 8:12], in0=lin[0:T, :], in1=mx[0:T, :], op=ALU.max), [lin, mx], [t12])
            V(lambda e: e.tensor_scalar(out=negm[0:T, :], in0=t12[0:T, 8:12], scalar1=-1.0, scalar2=None, op0=ALU.mult), [t12], [negm])
            V(lambda e: e.tensor_tensor(out=t12[0:T, 4:8], in0=lin[0:T, :], in1=t12[0:T, 8:12], op=ALU.subtract), [lin, t12], [t12])
            V(lambda e: e.tensor_tensor(out=t12[0:T, 0:4], in0=Fs[0:T, :], in1=t12[0:T, 8:12], op=ALU.subtract), [Fs, t12], [t12])
            A(lambda e: e.activation(out=win[0:T, :], in_=t12[0:T, 4:8], func=AF.Exp), [t12], [win])
            A(lambda e: e.activation(out=enm[0:T, :], in_=negm[0:T, :], func=AF.Exp), [negm], [enm])
            if KSTEP < 7:
                return
            Wt = wk["Wt"].next()
            for h in range(4):
                A(lambda e, h=h: e.activation(out=Wt[0:T, h, 0:T], in_=Dm[0:T, h, 0:T], func=AF.Exp, bias=negm[0:T, h:h + 1]), [Dm, negm], [Wt])
            if KSTEP < 8:
                return
            pQK = pX1[:, :].rearrange("p (h t) -> p h t", h=4)
            for h in range(4):
                M(lambda e, h=h: e.matmul(pQK[0:T, h, 0:T], lhsT=qTm[:, h, t0:t0 + T], rhs=kTm[:, h, t0:t0 + T], start=True, stop=True), [qTm, kTm], [pX1])
            Sqk = wk["Sqk"].next()
            V(lambda e: e.tensor_tensor(out=Sqk[0:T, :, 0:T], in0=pQK[0:T, :, 0:T], in1=Wt[0:T, :, 0:T], op=ALU.mult), [pX1, Wt], [Sqk])
            rs = g4()
            V(lambda e: e.tensor_reduce(out=rs[0:T, :], in_=Sqk[0:T, :, 0:T], axis=AX.X, op=ALU.add), [Sqk], [rs])
            for h in range(4):
                M(lambda e, h=h: e.transpose(out=pTr[0:T, h, 0:T], in_=Sqk[0:T, h, 0:T], identity=IDB[0:T, 0:T]), [Sqk, cstb], [pBF])
            SqkT = wk["SqkT"].next()
            A(lambda e: e.activation(out=SqkT[0:T, :, 0:T], in_=pTr[0:T, :, 0:T], func=AF.Copy), [pBF], [SqkT])
            if KSTEP < 10:
                return
            for h in range(4):
                M(lambda e, h=h: e.matmul(pX2[0:T, h * 128:(h + 1) * 128], lhsT=SqkT[0:T, h, 0:T], rhs=vaug[0:T, h, :], start=True, stop=True), [SqkT, vaug], [pX2])
            for h in range(4):
                M(lambda e, h=h: e.matmul(pX3[0:T, h * 128:(h + 1) * 128], lhsT=qTm[:, h, t0:t0 + T], rhs=CTb[:, h, :], start=True, stop=True), [qTm, CTb], [pX3])
                M(lambda e, h=h: e.matmul(pS[0:T, 4 + h:5 + h], lhsT=qTm[:, h, t0:t0 + T], rhs=nb[:, h:h + 1], start=True, stop=True), [qTm, nb], [pS])
            P1s = wk["P1s"].next()
            A(lambda e: e.activation(out=P1s[0:T, :], in_=pX2[0:T, :], func=AF.Copy), [pX2], [P1s])
            comb = wk["f5"].next()
            for h in range(4):
                V(lambda e, h=h: e.scalar_tensor_tensor(out=comb[0:T, h * 128:(h + 1) * 128], in0=pX3[0:T, h * 128:(h + 1) * 128], scalar=win[0:T, h:h + 1],
                                                        in1=P1s[0:T, h * 128:(h + 1) * 128], op0=ALU.mult, op1=ALU.add), [pX3, win, P1s], [comb])
            den, d2, rd = g4(), g4(), g4()
            V(lambda e: e.tensor_tensor(out=den[0:T, :], in0=pS[0:T, 4:8], in1=win[0:T, :], op=ALU.mult), [pS, win], [den])
            V(lambda e: e.tensor_tensor(out=den[0:T, :], in0=den[0:T, :], in1=rs[0:T, :], op=ALU.add), [den, rs], [den])
            V(lambda e: e.tensor_scalar(out=d2[0:T, :], in0=den[0:T, :], scalar1=-1.0, scalar2=None, op0=ALU.mult), [den], [d2])
            V(lambda e: e.tensor_tensor(out=d2[0:T, :], in0=d2[0:T, :], in1=den[0:T, :], op=ALU.max), [d2, den], [d2])
            V(lambda e: e.tensor_tensor(out=d2[0:T, :], in0=d2[0:T, :], in1=enm[0:T, :], op=ALU.max), [d2, enm], [d2])
            V(lambda e: e.reciprocal(out=rd[0:T, :], in_=d2[0:T, :]), [d2], [rd])
            if KSTEP < 12:
                return
            po = wk["pR"].next()
            tok_proj(po[0:T, 0:512], po, hTg, hTg, hc, T, Wm, Wm, 520, 512)
            sgo = wk["f5"].next()
            A(lambda e: e.activation(out=sgo[0:T, :], in_=po[0:T, :], func=AF.Sigmoid), [po], [sgo])
            pz = wk["pR"].next()
            tok_proj(pz[0:T, 0:512], pz, hTg, hTg, hc, T, Wm, Wm, 1032, 512)
            szm = wk["f5"].next()
            A(lambda e: e.activation(out=szm[0:T, :], in_=pz[0:T, :], func=AF.Silu), [pz], [szm])
            t2 = wk["f5"].next()
            V(lambda e: e.tensor_tensor(out=v3(t2[0:T, :]), in0=v3(comb[0:T, :]), in1=rd[0:T, :].unsqueeze(2).to_broadcast([T, 4, 128]), op=ALU.mult), [comb, rd], [t2])
            G(lambda e: e.tensor_tensor(out=t2[0:T, :], in0=t2[0:T, :], in1=sgo[0:T, :], op=ALU.mult), [t2, sgo], [t2])
            sq = wk["f5"].next()
            ss4, ln4, rs4 = g4(), g4(), g4()
            A(lambda e: e.activation(out=sq[0:T, :], in_=t2[0:T, :], func=AF.Square), [t2], [sq])
            V(lambda e: e.tensor_reduce(out=ss4[0:T, :], in_=v3(sq[0:T, :]), axis=AX.X, op=ALU.add), [sq], [ss4])
            rstd_from_ss(ss4[0:T, :], ss4, rs4[0:T, :], rs4, ln4[0:T, :], ln4, 1.0 / 128)
            V(lambda e: e.tensor_tensor(out=v3(t2[0:T, :]), in0=v3(t2[0:T, :]), in1=rs4[0:T, :].unsqueeze(2).to_broadcast([T, 4, 128]), op=ALU.mult), [t2, rs4], [t2])
            G(lambda e: e.tensor_tensor(out=v3(t2[0:T, :]), in0=v3(t2[0:T, :]), in1=mc_["mlnorm"][0:T, :].unsqueeze(1).to_broadcast([T, 4, 128]), op=ALU.mult),
              [t2, mc_["mlnorm"]], [t2])
            G(lambda e: e.tensor_tensor(out=t2[0:T, :], in0=t2[0:T, :], in1=mcs[0:T, :], op=ALU.add), [t2, mcs], [t2])
            Zm = wk["Zm"].next()
            V(lambda e: e.tensor_tensor(out=Zm[0:T, :], in0=t2[0:T, :], in1=szm[0:T, :], op=ALU.mult), [t2, szm], [Zm])
            for k4 in range(4):
                M(lambda e, k4=k4: e.transpose(out=pTr[:, k4, 0:T], in_=Zm[0:T, k4 * 128:(k4 + 1) * 128], identity=IDB[0:T, 0:T]), [Zm, cstb], [pBF])
            A(lambda e: e.activation(out=ZmT[:, :, zc:zc + T], in_=pTr[:, :, 0:T], func=AF.Copy), [pBF], [ZmT])
            if KSTEP < 14:
                return
            M(lambda e: e.matmul(pS[:, 8:20], lhsT=SELL[0:T, :], rhs=t12[0:T, 0:12], start=True, stop=True), [t12, cst], [pS])
            wend, dec = g4(), g4()
            V(lambda e: e.tensor_tensor(out=wend[0:T, :], in0=r[0:T, :], in1=pS[0:T, 8:12], op=ALU.add), [r, pS], [wend])
            A(lambda e: e.activation(out=wend[0:T, :], in_=wend[0:T, :], func=AF.Exp), [wend], [wend])
            A(lambda e: e.activation(out=dec[:, :], in_=pS[:, 12:16], func=AF.Exp), [pS], [dec])
            V(lambda e: e.tensor_copy(out=mprev[:, :], in_=pS[:, 16:20]), [pS], [mprev])
            kw = wk["kw"].next()
            V(lambda e: e.scalar_tensor_tensor(out=kw[0:T, :, :], in0=v3(pK[0:T, :]), scalar=RS128, in1=wend[0:T, :].unsqueeze(2).to_broadcast([T, 4, 128]),
                                               op0=ALU.mult, op1=ALU.mult), [pK, wend], [kw])
            for h in range(4):
                M(lambda e, h=h: e.matmul(pX2[:, h * 128:(h + 1) * 128], lhsT=kw[0:T, h, :], rhs=vaug[0:T, h, :], start=True, stop=True), [kw, vaug], [pX2])
                M(lambda e, h=h: e.matmul(pS[:, 20 + h:21 + h], lhsT=kw[0:T, h, :], rhs=ONESB[0:T, 0:1], start=True, stop=True), [kw, cstb], [pS])
            for h in range(4):
                V(lambda e, h=h: e.scalar_tensor_tensor(out=CT[:, h, :], in0=CT[:, h, :], scalar=dec[:, h:h + 1], in1=pX2[:, h * 128:(h + 1) * 128],
                                                        op0=ALU.mult, op1=ALU.add), [CT, dec, pX2], [CT])
            V(lambda e: e.tensor_tensor(out=nT[:, :], in0=nT[:, :], in1=dec[:, :], op=ALU.mult), [nT, dec], [nT])
            V(lambda e: e.tensor_tensor(out=nT[:, :], in0=nT[:, :], in1=pS[:, 20:24], op=ALU.add), [nT, pS], [nT])
            G(lambda e: e.tensor_copy(out=CTb[:, :, :], in_=CT[:, :, :]), [CT], [CTb])
            G(lambda e: e.tensor_copy(out=nb[:, :], in_=nT[:, :]), [nT], [nb])

        def ml_state_out(st, wk, o_mc, o_mn, o_mm, o_conv, fin=True):
            CT, nT, mprev, xT = st["CT"], st["nT"], st["mprev"], wk["xT"]
            pX2, pS, pX3 = wk["pX2"], wk["pS"], wk["pX3"]
            for h in range(4):
                M(lambda e, h=h: e.transpose(out=pX2[:, h * 128:(h + 1) * 128], in_=CT[:, h, :], identity=IDF[:, :]), [CT, cst], [pX2])
            co = wk["f5"].next()
            V(lambda e: e.tensor_copy(out=co[:, :], in_=pX2[:, :]), [pX2], [co])
            DMA(lambda e: e.dma_start(out=o_mc.rearrange("h v d -> v h d"), in_=co[:, :].rearrange("p (h d) -> p h d", h=4)), [co], [], final=fin)
            M(lambda e: e.transpose(out=pS[0:4, 0:128], in_=nT[:, 0:4], identity=IDF[:, :]), [nT, cst], [pS])
            no = wk["f5"].next()
            V(lambda e: e.tensor_copy(out=no[0:4, 0:128], in_=pS[0:4, 0:128]), [pS], [no])
            DMA(lambda e: e.dma_start(out=o_mn, in_=no[0:4, 0:128]), [no], [], final=fin)
            DMA(lambda e: e.dma_start(out=o_mm, in_=mprev[0:1, 0:4]), [mprev], [], final=fin)
            for c in range(4):
                M(lambda e, c=c: e.transpose(out=pX3[0:3, c * 128:(c + 1) * 128], in_=xT[:, c, 0:3], identity=IDF[:, :]), [xT, cst], [pX3])
            cvo = wk["f5"].next()
            V(lambda e: e.tensor_copy(out=cvo[0:3, :], in_=pX3[0:3, :]), [pX3], [cvo])
            DMA(lambda e: e.dma_start(out=o_conv, in_=cvo[0:3, :]), [cvo], [], final=fin)

        def ml_state_new(zero=True):
            st = dict(CT=sb([128, 4, 128], F32, "CT"), CTb=sb([128, 4, 128], BF16, "CTb"), nT=sb([128, 4], F32, "nT"),
                      nb=sb([128, 4], BF16, "nb"), mprev=sb([128, 4], F32, "mprev"))
            if zero:
                for k_, b in st.items():
                    nd = len(b.t.shape)
                    V(lambda e, b=b, nd=nd: e.memset(b[(slice(None),) * nd], 0.0), [], [b])
            return st

        def phase_M(l, first):
            with contextlib.ExitStack() as pst:
                cur[0] = pst
                Wm = sb([128, 8, 1544], BF16, "Wm"); Wgt = sb([128, 8, D], BF16, "Wgt"); Wo = sb([128, 4, D], BF16, "Wo")
                wts = ml_weights_alloc()
                with WLoad() as stg:
                    load_w(Wm, 0, 1544, w_in_src(l, C_MX), stg)
                    load_gate_w(Wgt, l, 1, stg)
                    load_outw(Wo, "ml_w_out", l, stg)
                    ml_weights(l, stg, wts)
                mc_ = ml_consts(l)
                wk = ml_work(); bw = branch_work(wk["pR"])
                st = ml_state_new(True)
                V(lambda e: e.memset(wk["xT"][:, :, 0:3], 0.0), [], [wk["xT"]])
                hpool = Rot(lambda: sb([128, 8, 512], BF16, "hTg"), 2)
                ypool = Rot(lambda: sb([128, 8, 512], BF16, "yTg"), 2)
                zpool = Rot(lambda: sb([128, 4, 512], BF16, "ZmT"), 2)
                for g in range(NG):
                    hTg = hpool.next(); yTg = ypool.next(); ZmT = zpool.next()
                    DMA(lambda e, hTg=hTg, g=g: e.dma_start(out=hTg[:, :, :], in_=hT_d[g]), [hT_dep[g]], [hTg])
                    if not first:
                        DMA(lambda e, yTg=yTg, g=g: e.dma_start(out=yTg[:, :, :], in_=yT_d[g]), [yT_dep[g]], [yTg])
                    ml_group_prep(Wm, hTg, 512, mc_, wts, wk)
                    for j in range(4):
                        ml_tile(128, j * 128, Wm, hTg, mc_, wts, st, ZmT, wk)
                    ml_carry(512, wk)
                    terms = [((lambda c, k4=k4: Wo[:, k4, c * 128:(c + 1) * 128]), ZmT[:, k4, :]) for k4 in range(4)]
                    if STOP >= 4:
                        branch_out(terms, ZmT, Wo, Wo, Wgt, Wgt, hTg, 512, yTg, first, bw)
                    DMA(lambda e, yTg=yTg, g=g: e.dma_start(out=yT_d[g], in_=yTg[:, :, :]), [yTg], [yT_dep[g]])
                if int(os.environ.get('KSTEP', '99')) >= 15:
                    ml_state_out(st, wk, p_mc[l], p_mn[l], p_mm[l:l + 1, :], p_conv[l])
                if cfg.sample:
                    ZmTs = zpool.next()
                    ctok = sb([128, 4, 128], F32, "ctok"); ntok = sb([4, 128], F32, "ntok"); cvtok = sb([3, 512], F32, "cvtok")
                    for sq_ in range(NS):
                        DMA(lambda e, sq_=sq_: e.dma_start(out=ctok[:, :, :], in_=sc_in[l, sq_].rearrange("h v d -> v h d")), [], [ctok])
                        DMA(lambda e, sq_=sq_: e.dma_start(out=ntok[:, :], in_=sn_in[l, sq_]), [], [ntok])
                        DMA(lambda e, sq_=sq_: e.dma_start(out=cvtok[:, :], in_=scv_in[l, sq_]), [], [cvtok])
                        DMA(lambda e, sq_=sq_: e.dma_start(out=st["mprev"][:, :], in_=sm_in[l, sq_:sq_ + 1, :].partition_broadcast(128)), [], [st["mprev"]])
                        pX2, pS, pX3 = wk["pX2"], wk["pS"], wk["pX3"]
                        for h in range(4):
                            M(lambda e, h=h: e.transpose(out=pX2[:, h * 128:(h + 1) * 128], in_=ctok[:, h, :], identity=IDF[:, :]), [ctok, cst], [pX2])
                        V(lambda e: e.tensor_copy(out=st["CT"][:, :, :], in_=pX2[:, :].rearrange("p (h v) -> p h v", h=4)), [pX2], [st["CT"]])
                        G(lambda e: e.tensor_copy(out=st["CTb"][:, :, :], in_=st["CT"][:, :, :]), [st["CT"]], [st["CTb"]])
                        M(lambda e: e.transpose(out=pS[:, 24:28], in_=ntok[0:4, :], identity=IDF[0:4, 0:4]), [ntok, cst], [pS])
                        V(lambda e: e.tensor_copy(out=st["nT"][:, :], in_=pS[:, 24:28]), [pS], [st["nT"]])
                        G(lambda e: e.tensor_copy(out=st["nb"][:, :], in_=st["nT"][:, :]), [st["nT"]], [st["nb"]])
                        for c in range(4):
                            M(lambda e, c=c: e.transpose(out=pX3[:, c * 4:c * 4 + 3], in_=cvtok[0:3, c * 128:(c + 1) * 128], identity=IDF[0:3, 0:3]), [cvtok, cst], [pX3])
                        V(lambda e: e.tensor_copy(out=wk["xT"][:, :, 0:3], in_=pX3[:, 0:16].rearrange("p (c j) -> p c j", c=4)[:, :, 0:3]), [pX3], [wk["xT"]])
                        ml_group_prep(Wm, hTs, TS, mc_, wts, wk, hoff=sq_ * TS)
                        ml_tile(TS, 0, Wm, hTs, mc_, wts, st, ZmTs, wk, hoff=sq_ * TS, zoff=sq_ * TS)
                        ml_carry(TS, wk)
                        ml_state_out(st, wk, s_mc[l, sq_], s_mn[l, sq_], s_mm[l, sq_:sq_ + 1, :], s_conv[l, sq_])
                    terms = [((lambda c, k4=k4: Wo[:, k4, c * 128:(c + 1) * 128]), ZmTs[:, k4, 0:NTOK]) for k4 in range(4)]
                    branch_out(terms, ZmTs, Wo, Wo, Wgt, Wgt, hTs, NTOK, yTs, first, bw)
                barrier()
            cur[0] = gst


        def qk_norm(T, pq, nrm_bc, scale, wk, out_f32):
            sq = wk["f5"].next(); ss8, ln8, rs8 = wk["g8"].next(), wk["g8"].next(), wk["g8"].next()
            v8 = lambda ap: ap.rearrange("p (h d) -> p h d", h=8)
            A(lambda e: e.activation(out=sq[0:T, :], in_=pq[0:T, :], func=AF.Square), [pq], [sq])
            V(lambda e: e.tensor_reduce(out=ss8[0:T, :], in_=v8(sq[0:T, :]), axis=AX.X, op=ALU.add), [sq], [ss8])
            rstd_from_ss(ss8[0:T, :], ss8, rs8[0:T, :], rs8, ln8[0:T, :], ln8, 1.0 / 64)
            V(lambda e: e.tensor_tensor(out=v8(out_f32[0:T, :]), in0=v8(pq[0:T, :]), in1=rs8[0:T, :].unsqueeze(2).to_broadcast([T, 8, 64]), op=ALU.mult),
              [pq, rs8], [out_f32])
            V(lambda e: e.scalar_tensor_tensor(out=v8(out_f32[0:T, :]), in0=v8(out_f32[0:T, :]), scalar=scale,
                                               in1=nrm_bc[0:T, :].unsqueeze(1).to_broadcast([T, 8, 64]), op0=ALU.mult, op1=ALU.mult), [out_f32, nrm_bc], [out_f32])


        OTs = sb([64, 8, NTOK], F32, "OTs")
        NBP = NPG // 2

        def sample_moba(l, Wb, qn_bc, kn_bc):
            wk = dict(f5=Rot(lambda: sb([128, 512], F32, "f5s"), 8), g8=Rot(lambda: sb([128, 8], F32, "g8s"), 12),
                      b5=Rot(lambda: sb([128, 512], BF16, "b5s"), 6))
            pR = Rot(lambda: psum(F32, "pRs"), 3); pT = Rot(lambda: psum(BF16, "pTs"), 2); pBS = psum(F32, "pBSs")
            pOV = psum(F32, "pOV"); pMk = psum(F32, "pMk")
            id32f = sb([32, 2048], F32, "id32f"); id32 = sb([32, 32, 64], BF16, "id32")
            DMA(lambda e: e.dma_start(out=id32f[:, :], in_=consts_d[0:32, K_ID32:K_ID32 + 2048]), [], [id32f])
            V(lambda e: e.tensor_copy(out=id32[:, :, :], in_=id32f[:, :].rearrange("p (n t) -> p n t", n=32)), [id32f], [id32])
            ptb = sb([128, NPG], I32, "ptb"); ptf = sb([128, NPG], F32, "ptf"); idxi = sb([128, NPG], I32, "idxi")
            S_all = sb([128, NPG, 64], F32, "S_all"); Pm = sb([128, NPG, 64], BF16, "Pm")
            QBD = zsb([128, 4, 2, TS], BF16, "QBD"); KTn = sb([128, 4, TS], BF16, "KTn")
            KTt = Rot(lambda: sb([128, 4, 128], BF16, "KTt"), 2)
            bs_sb = sb([64, max(NBP, 8)], F32, "bs_sb"); m8 = sb([64, 8], F32, "m8s"); sel = sb([64, 32], F32, "sel"); selT = sb([32, 64], F32, "selT")
            Dexp = sb([32, 32, 64], BF16, "Dexp"); Pn = sb([TS, 8, TS], BF16, "Pn"); Pnf = sb([TS, 8, TS], F32, "Pnf")
            tmpo = sb([64, 8, 64], F32, "tmpo"); OD = sb([64, 64], F32, "OD"); rdn = sb([64, 1], F32, "rdn")
            ck2 = ck[l][:, :]; cv2 = cv[l][:, :]
            idx1 = Rot(lambda: sb([128, 1], I32, "idx1"), 4)
            for sq_ in range(NS):
                tsl = slice(sq_ * TS, (sq_ + 1) * TS)
                pv = pR.next(); tok_proj(pv[0:TS, 0:512], pv, hTs, hTs, sq_ * TS, TS, Wb, Wb, 1024, 512)
                vf = wk["f5"].next()
                A(lambda e, pv=pv, vf=vf: e.activation(out=vf[0:TS, :], in_=pv[0:TS, 0:512], func=AF.Copy), [pv], [vf])
                DMA(lambda e, vf=vf, tsl=tsl: e.dma_start(out=s_v[l, tsl, :], in_=vf[0:TS, :]), [vf], [], final=True)
                vbn = wk["b5"].next()
                G(lambda e, vf=vf, vbn=vbn: e.tensor_copy(out=vbn[0:TS, :], in_=vf[0:TS, :]), [vf], [vbn])
                pk = pR.next(); tok_proj(pk[0:TS, 0:512], pk, hTs, hTs, sq_ * TS, TS, Wb, Wb, 512, 512)
                kf = wk["f5"].next(); qk_norm(TS, pk, kn_bc, 1.0, wk, kf)
                DMA(lambda e, kf=kf, tsl=tsl: e.dma_start(out=s_k[l, tsl, :], in_=kf[0:TS, :]), [kf], [], final=True)
                kbn = wk["b5"].next()
                G(lambda e, kf=kf, kbn=kbn: e.tensor_copy(out=kbn[0:TS, :], in_=kf[0:TS, :]), [kf], [kbn])
                pq = pR.next(); tok_proj(pq[0:TS, 0:512], pq, hTs, hTs, sq_ * TS, TS, Wb, Wb, 0, 512)
                qf = wk["f5"].next(); qk_norm(TS, pq, qn_bc, 0.125, wk, qf)
                qbn = wk["b5"].next()
                G(lambda e, qf=qf, qbn=qbn: e.tensor_copy(out=qbn[0:TS, :], in_=qf[0:TS, :]), [qf], [qbn])
                pt_ = pT.next(); ptv = pt_[:, 0:4 * TS].rearrange("p (c t) -> p c t", c=4)
                for c in range(4):
                    M(lambda e, c=c, qbn=qbn, ptv=ptv: e.transpose(out=ptv[:, c, :], in_=qbn[0:TS, c * 128:(c + 1) * 128], identity=IDB[0:TS, 0:TS]), [qbn, cstb], [pt_])
                A(lambda e, ptv=ptv: e.activation(out=QBD[0:64, :, 0, :], in_=ptv[0:64, :, :], func=AF.Copy), [pt_], [QBD])
                V(lambda e, ptv=ptv: e.tensor_copy(out=QBD[64:128, :, 1, :], in_=ptv[64:128, :, :]), [pt_], [QBD])
                pt2 = pT.next(); ptv2 = pt2[:, 0:4 * TS].rearrange("p (c t) -> p c t", c=4)
                for c in range(4):
                    M(lambda e, c=c, kbn=kbn, ptv2=ptv2: e.transpose(out=ptv2[:, c, :], in_=kbn[0:TS, c * 128:(c + 1) * 128], identity=IDB[0:TS, 0:TS]), [kbn, cstb], [pt2])
                A(lambda e, ptv2=ptv2: e.activation(out=KTn[:, :, :], in_=ptv2[:, :, :], func=AF.Copy), [pt2], [KTn])
                DMA(lambda e, sq_=sq_: e.dma_start(out=ptb[:, :], in_=pt[sq_:sq_ + 1, :].partition_broadcast(128)), [], [ptb])
                V(lambda e: e.tensor_copy(out=ptf[:, :], in_=ptb[:, :]), [ptb], [ptf])
                V(lambda e: e.tensor_scalar(out=ptf[:, :], in0=ptf[:, :], scalar1=128.0, scalar2=IOTA, op0=ALU.mult, op1=ALU.add), [ptf, cst], [ptf])
                V(lambda e: e.tensor_copy(out=idxi[:, :], in_=ptf[:, :]), [ptf], [idxi])
                for j in range(NPG):
                    Kt = wk["f5"].next(); ix = idx1.next()
                    G(lambda e, ix=ix, j=j: e.tensor_copy(out=ix[:, :], in_=idxi[:, j:j + 1]), [idxi], [ix])
                    DMA(lambda e, Kt=Kt, ix=ix: e.indirect_dma_start(out=Kt[:, :], out_offset=None, in_=ck2,
                                                                     in_offset=bass.IndirectOffsetOnAxis(ap=ix[:, :], axis=0)), [ix], [Kt], q="pool")
                    Kb = wk["b5"].next()
                    V(lambda e, Kt=Kt, Kb=Kb: e.tensor_copy(out=Kb[:, :], in_=Kt[:, :]), [Kt], [Kb])
                    pk_ = pT.next(); pkv = pk_[:, 0:512].rearrange("p (c t) -> p c t", c=4)
                    for c in range(4):
                        M(lambda e, c=c, Kb=Kb, pkv=pkv: e.transpose(out=pkv[:, c, :], in_=Kb[:, c * 128:(c + 1) * 128], identity=IDB[:, :]), [Kb, cstb], [pk_])
                    kt_ = KTt.next(); pST = pR.next()
                    A(lambda e, pkv=pkv, kt_=kt_: e.activation(out=kt_[:, :, :], in_=pkv[:, :, :], func=AF.Copy), [pk_], [kt_])
                    for c in range(4):
                        M(lambda e, c=c, kt_=kt_, pST=pST: e.matmul(pST[:, c * 2 * TS:(c + 1) * 2 * TS], lhsT=kt_[:, c, :], rhs=QBD[:, c, :, :].rearrange("p a t -> p (a t)"),
                                                           start=True, stop=True), [kt_, QBD], [pST])
                    V(lambda e, j=j, pST=pST: e.tensor_copy(out=S_all[:, j, :], in_=pST[:, 0:64]), [pST], [S_all])
                    M(lambda e, j=j: e.matmul(pBS[0:64, j // 2:j // 2 + 1], lhsT=S_all[:, j, :], rhs=ONESF[:, 0:1], start=(j % 2 == 0), stop=(j % 2 == 1)),
                      [S_all, cst], [pBS])
                if NBP < 8:
                    V(lambda e: e.memset(bs_sb[:, :], NEG), [], [bs_sb])
                V(lambda e: e.tensor_copy(out=bs_sb[:, 0:NBP], in_=pBS[0:64, 0:NBP]), [pBS], [bs_sb])
                V(lambda e: e.max(out=m8[:, :], in_=bs_sb[:, :]), [bs_sb], [m8])
                V(lambda e: e.tensor_tensor(out=sel[:, 0:NBP], in0=bs_sb[:, 0:NBP], in1=m8[:, 2:3].to_broadcast([64, NBP]), op=ALU.is_ge), [bs_sb, m8], [sel])
                M(lambda e: e.transpose(out=pMk[0:NBP, 0:64], in_=sel[:, 0:NBP], identity=IDF[0:64, 0:64]), [sel, cst], [pMk])
                V(lambda e: e.tensor_copy(out=selT[0:NBP, :], in_=pMk[0:NBP, 0:64]), [pMk], [selT])
                V(lambda e: e.tensor_tensor(out=Dexp[0:NBP, 0:NBP, :], in0=id32[0:NBP, 0:NBP, :], in1=selT[0:NBP, :].unsqueeze(1).to_broadcast([NBP, NBP, 64]), op=ALU.mult),
                  [id32, selT], [Dexp])
                A(lambda e: e.activation(out=S_all[:, :, :], in_=S_all[:, :, :], func=AF.Exp), [S_all], [S_all])
                for n0 in range(0, NBP, 8):
                    nn = min(8, NBP - n0)
                    M(lambda e, n0=n0, nn=nn: e.matmul(pMk[:, 0:nn * 64], lhsT=ONESB[0:NBP, :], rhs=Dexp[0:NBP, n0:n0 + nn, :].rearrange("p n t -> p (n t)"),
                                                       start=True, stop=True), [Dexp, cstb], [pMk])
                    V(lambda e, n0=n0, nn=nn: e.tensor_tensor(
                        out=Pm[:, 2 * n0:2 * (n0 + nn), :].rearrange("p (n two) t -> p n two t", two=2),
                        in0=S_all[:, 2 * n0:2 * (n0 + nn), :].rearrange("p (n two) t -> p n two t", two=2),
                        in1=pMk[:, 0:nn * 64].rearrange("p (n t) -> p n t", n=nn).unsqueeze(2).to_broadcast([128, nn, 2, 64]), op=ALU.mult), [S_all, pMk], [Pm])
                pST = pR.next()
                for c in range(4):
                    M(lambda e, c=c, pST=pST: e.matmul(pST[0:TS, c * 2 * TS:(c + 1) * 2 * TS], lhsT=KTn[:, c, :], rhs=QBD[:, c, :, :].rearrange("p a t -> p (a t)"),
                                              start=True, stop=True), [KTn, QBD], [pST])
                A(lambda e, pST=pST: e.activation(out=Pnf[:, :, :], in_=pST[0:TS, 0:64].rearrange("p (h t) -> p h t", h=8), func=AF.Exp), [pST], [Pnf])
                V(lambda e: e.tensor_tensor(out=Pn[:, :, :], in0=Pnf[:, :, :], in1=TRI01[0:TS, 0:TS].unsqueeze(1).to_broadcast([TS, 8, TS]), op=ALU.mult), [Pnf, cstb], [Pn])
                for j in range(NPG):
                    Vt = wk["f5"].next(); ix = idx1.next()
                    G(lambda e, ix=ix, j=j: e.tensor_copy(out=ix[:, :], in_=idxi[:, j:j + 1]), [idxi], [ix])
                    DMA(lambda e, Vt=Vt, ix=ix: e.indirect_dma_start(out=Vt[:, :], out_offset=None, in_=cv2,
                                                                     in_offset=bass.IndirectOffsetOnAxis(ap=ix[:, :], axis=0)), [ix], [Vt], q="pool")
                    Vb = wk["b5"].next()
                    if j % 2 == 0:
                        V(lambda e, Vt=Vt, Vb=Vb: e.tensor_copy(out=Vb[:, :], in_=Vt[:, :]), [Vt], [Vb])
                    else:
                        A(lambda e, Vt=Vt, Vb=Vb: e.activation(out=Vb[:, :], in_=Vt[:, :], func=AF.Copy), [Vt], [Vb])
                    M(lambda e, j=j, Vb=Vb: e.matmul(pOV[0:64, 0:512], lhsT=Pm[:, j, :], rhs=Vb[:, :], start=(j == 0), stop=False), [Pm, Vb], [pOV])
                    M(lambda e, j=j: e.matmul(pBS[0:64, 64:65], lhsT=Pm[:, j, :], rhs=ONESB[:, 0:1], start=(j == 0), stop=False), [Pm, cstb], [pBS])
                M(lambda e, vbn=vbn: e.matmul(pOV[0:64, 0:512], lhsT=Pn[:, :, :].rearrange("p h t -> p (h t)"), rhs=vbn[0:TS, :], start=False, stop=True), [Pn, vbn], [pOV])
                M(lambda e: e.matmul(pBS[0:64, 64:65], lhsT=Pn[:, :, :].rearrange("p h t -> p (h t)"), rhs=ONESB[0:TS, 0:1], start=False, stop=True), [Pn, cstb], [pBS])
                V(lambda e: e.tensor_tensor(out=tmpo[:, :, :], in0=pOV[0:64, 0:512].rearrange("p (h d) -> p h d", h=8),
                                            in1=BDm[0:64, 0:8].unsqueeze(2).to_broadcast([64, 8, 64]), op=ALU.mult), [pOV, cst], [tmpo])
                V(lambda e: e.tensor_reduce(out=OD[:, :], in_=tmpo[:, :, :].rearrange("p h d -> p d h"), axis=AX.X, op=ALU.add), [tmpo], [OD])
                V(lambda e: e.reciprocal(out=rdn[:, :], in_=pBS[0:64, 64:65]), [pBS], [rdn])
                V(lambda e: e.tensor_scalar(out=OD[:, :], in0=OD[:, :], scalar1=rdn[:, 0:1], scalar2=None, op0=ALU.mult), [OD, rdn], [OD])
                M(lambda e: e.transpose(out=pMk[0:64, 0:64], in_=OD[:, :], identity=IDF[0:64, 0:64]), [OD, cst], [pMk])
                V(lambda e, sq_=sq_: e.tensor_copy(out=OTs[:, :, sq_ * TS:(sq_ + 1) * TS], in_=pMk[0:64, 0:64].rearrange("p (h t) -> p h t", h=8)), [pMk], [OTs])

        def phase_B1(l, KT2, Vall):
            with contextlib.ExitStack() as pst:
                cur[0] = pst
                Wb = sb([128, 8, 1536], BF16, "Wb")
                with WLoad() as stg:
                    load_w(Wb, 0, 1536, w_in_src(l, C_BQ), stg)
                qn_bc = bvec(w["mb_q_norm"][l:l + 1, :], 64, "qn"); kn_bc = bvec(w["mb_k_norm"][l:l + 1, :], 64, "kn")
                p1 = contextlib.ExitStack(); p1.__enter__(); cur[0] = p1
                wk = dict(f5=Rot(lambda: sb([128, 512], F32, "f5"), 6), g8=Rot(lambda: sb([128, 8], F32, "g8"), 12),
                          b5=Rot(lambda: sb([128, 512], BF16, "b5"), 4))
                pR = Rot(lambda: psum(F32, "pRb"), 3); pT = Rot(lambda: psum(BF16, "pTb"), 2); pBS = psum(F32, "pBS"); pMB = psum(BF16, "pMB")
                hpool = Rot(lambda: sb([128, 8, 512], BF16, "hTg"), 2)
                qpool = Rot(lambda: zsb([128, 4, 2, 512], BF16, "QT2m"), 2)
                bpool = Rot(lambda: sb([16, 8, 512], BF16, "biasT"), 2)
                KBf = sb([128, 4, 16], F32, "KBf"); KBb = sb([128, 4, 16], BF16, "KBb")
                V(lambda e: e.memset(KBf[:, :, :], 0.0), [], [KBf]); V(lambda e: e.memset(KBb[:, :, :], 0.0), [], [KBb])
                bsm = sb([128, 8, 16], F32, "bsm"); m8 = sb([128, 8, 8], F32, "m8"); ge = sb([128, 8, 16], F32, "ge")
                mbias = sb([128, 8, 16], BF16, "mbias")
                for g in range(NG):
                    hTg = hpool.next(); QT2m = qpool.next(); biasT = bpool.next()
                    DMA(lambda e, hTg=hTg, g=g: e.dma_start(out=hTg[:, :, :], in_=hT_d[g]), [hT_dep[g]], [hTg])
                    for j in range(4):
                        i = 4 * g + j; t0 = j * 128; own = i // 2
                        pv = pR.next()
                        tok_proj(pv[:, 0:512], pv, hTg, hTg, t0, 128, Wb, Wb, 1024, 512)
                        vf = wk["f5"].next()
                        A(lambda e, pv=pv, vf=vf: e.activation(out=vf[:, :], in_=pv[:, 0:512], func=AF.Copy), [pv], [vf])
                        DMA(lambda e, vf=vf, i=i: e.dma_start(out=p_v[l, i * 128:(i + 1) * 128, :], in_=vf[:, :]), [vf], [], final=True)
                        G(lambda e, vf=vf, i=i: e.tensor_copy(out=Vall[:, i, :], in_=vf[:, :]), [vf], [Vall])
                        pk = pR.next()
                        tok_proj(pk[:, 0:512], pk, hTg, hTg, t0, 128, Wb, Wb, 512, 512)
                        kf = wk["f5"].next()
                        qk_norm(128, pk, kn_bc, 1.0, wk, kf)
                        DMA(lambda e, kf=kf, i=i: e.dma_start(out=p_k[l, i * 128:(i + 1) * 128, :], in_=kf[:, :]), [kf], [], final=True)
                        kb = wk["b5"].next()
                        G(lambda e, kf=kf, kb=kb: e.tensor_copy(out=kb[:, :], in_=kf[:, :]), [kf], [kb])
                        pt_ = pT.next()
                        ptv = pt_[:, 0:512].rearrange("p (c t) -> p c t", c=4)
                        for c in range(4):
                            M(lambda e, c=c, kb=kb, ptv=ptv: e.transpose(out=ptv[:, c, :], in_=kb[:, c * 128:(c + 1) * 128], identity=IDB[:, :]), [kb, cstb], [pt_])
                        A(lambda e, ptv=ptv, i=i: e.activation(out=KT2[:, :, i * 128:(i + 1) * 128], in_=ptv[:, :, :], func=AF.Copy), [pt_], [KT2])
                        pq = pR.next()
                        tok_proj(pq[:, 0:512], pq, hTg, hTg, t0, 128, Wb, Wb, 0, 512)
                        qf = wk["f5"].next()
                        qk_norm(128, pq, qn_bc, 0.125, wk, qf)
                        qb = wk["b5"].next()
                        G(lambda e, qf=qf, qb=qb: e.tensor_copy(out=qb[:, :], in_=qf[:, :]), [qf], [qb])
                        pt2 = pT.next()
                        ptv2 = pt2[:, 0:512].rearrange("p (c t) -> p c t", c=4)
                        for c in range(4):
                            M(lambda e, c=c, qb=qb, ptv2=ptv2: e.transpose(out=ptv2[:, c, :], in_=qb[:, c * 128:(c + 1) * 128], identity=IDB[:, :]), [qb, cstb], [pt2])
                        A(lambda e, ptv2=ptv2, t0=t0, QT2m=QT2m: e.activation(out=QT2m[0:64, :, 0, t0:t0 + 128], in_=ptv2[0:64, :, :], func=AF.Copy), [pt2], [QT2m])
                        V(lambda e, ptv2=ptv2, t0=t0, QT2m=QT2m: e.tensor_copy(out=QT2m[64:128, :, 1, t0:t0 + 128], in_=ptv2[64:128, :, :]), [pt2], [QT2m])
                        G(lambda e: e.memset(mbias[:, :, :], 0.0), [], [mbias])
                        if own >= 4:
                            pbv = pBS[:, 0:128].rearrange("p (h n) -> p h n", h=8)
                            for h in range(8):
                                M(lambda e, h=h, QT2m=QT2m, t0=t0, pbv=pbv: e.matmul(pbv[:, h, :], lhsT=QT2m[:, h // 2, h % 2, t0:t0 + 128], rhs=KBb[:, h // 2, :],
                                                                                  start=True, stop=True), [QT2m, KBb], [pBS])
                            G(lambda e: e.memset(bsm[:, :, :], NEG), [], [bsm])
                            V(lambda e, pbv=pbv, own=own: e.tensor_copy(out=bsm[:, :, 0:own], in_=pbv[:, :, 0:own]), [pBS], [bsm])
                            for h in range(8):
                                V(lambda e, h=h: e.max(out=m8[:, h, :], in_=bsm[:, h, :]), [bsm], [m8])
                            V(lambda e: e.tensor_tensor(out=ge[:, :, :], in0=bsm[:, :, :], in1=m8[:, :, 2:3].to_broadcast([128, 8, 16]), op=ALU.is_ge), [bsm, m8], [ge])
                            V(lambda e, own=own: e.tensor_scalar(out=mbias[:, :, 0:own], in0=ge[:, :, 0:own], scalar1=-1.0, scalar2=-MB_NEG, op0=ALU.add, op1=ALU.mult),
                              [ge], [mbias])
                        pmv = pMB[0:16, 0:1024].rearrange("p (h t) -> p h t", h=8)
                        for h in range(8):
                            M(lambda e, h=h, pmv=pmv: e.transpose(out=pmv[:, h, :], in_=mbias[:, h, :], identity=IDB[:, :]), [mbias, cstb], [pMB])
                        A(lambda e, pmv=pmv, biasT=biasT, t0=t0: e.activation(out=biasT[0:16, :, t0:t0 + 128], in_=pmv[:, :, :], func=AF.Copy), [pMB], [biasT])
                        if i % 2 == 1:
                            n = i // 2
                            V(lambda e, n=n: e.tensor_reduce(out=KBf[:, :, n:n + 1], in_=KT2[:, :, n * 256:(n + 1) * 256], axis=AX.X, op=ALU.add), [KT2], [KBf])
                            V(lambda e: e.tensor_copy(out=KBb[:, :, :], in_=KBf[:, :, :]), [KBf], [KBb])
                    DMA(lambda e, QT2m=QT2m, g=g: e.dma_start(out=qT_d[g], in_=QT2m[:, :, :, :]), [QT2m], [qT_dep[g]])
                    DMA(lambda e, biasT=biasT, g=g: e.dma_start(out=mbias_d[g], in_=biasT[:, :, :]), [biasT], [mbias_dep[g]])
                barrier()
                p1.__exit__(None, None, None)
                cur[0] = pst
                if cfg.sample:
                    with contextlib.ExitStack() as p2:
                        cur[0] = p2
                        sample_moba(l, Wb, qn_bc, kn_bc)
                        barrier()
                    cur[0] = pst
            cur[0] = gst

        def phase_B2(l, first, KT2, Vall):
            with contextlib.ExitStack() as pst:
                cur[0] = pst
                Wz = sb([128, 8, 512], BF16, "Wz"); Wgt = sb([128, 8, D], BF16, "Wgt"); Wmb = sb([64, 8, D], BF16, "Wmb")
                with WLoad() as stg:
                    load_w(Wz, 0, 512, w_in_src(l, C_BZ), stg)
                    load_gate_w(Wgt, l, 2, stg)
                    load_w(Wmb, 0, D, lambda a, n: w["mb_w_out"][l, :, a:a + n].rearrange("(h p) n -> p h n", p=64), stg, kc=8, rows=64)
                pR = Rot(lambda: psum(F32, "pR2"), 2); pSr = Rot(lambda: psum(F32, "pS2"), 2); pOr = Rot(lambda: psum(F32, "pO2"), 2); pDr = Rot(lambda: psum(F32, "pD2"), 2)
                bw = branch_work(pR)
                hpool = Rot(lambda: sb([128, 8, 512], BF16, "hTg"), 1)
                ypool = Rot(lambda: sb([128, 8, 512], BF16, "yTg"), 1)
                qpool = Rot(lambda: sb([128, 4, 2, 512], BF16, "QT2m"), 1)
                bpool = Rot(lambda: sb([16, 8, 512], BF16, "biasT"), 1)
                ZbT = sb([64, 8, 512], BF16, "ZbT")
                PTp = Rot(lambda: sb([128, 512], BF16, "PT"), 4)
                szp = Rot(lambda: sb([64, 512], F32, "szT"), 2); rdp = Rot(lambda: sb([64, 512], F32, "rd"), 2); otp = Rot(lambda: sb([64, 512], F32, "ot"), 2)
                for g in range(NG):
                    hTg = hpool.next(); yTg = ypool.next(); QT2m = qpool.next(); biasT = bpool.next()
                    DMA(lambda e, hTg=hTg, g=g: e.dma_start(out=hTg[:, :, :], in_=hT_d[g]), [hT_dep[g]], [hTg])
                    if not first:
                        DMA(lambda e, yTg=yTg, g=g: e.dma_start(out=yTg[:, :, :], in_=yT_d[g]), [yT_dep[g]], [yTg])
                    DMA(lambda e, QT2m=QT2m, g=g: e.dma_start(out=QT2m[:, :, :, :], in_=qT_d[g]), [qT_dep[g]], [QT2m])
                    DMA(lambda e, biasT=biasT, g=g: e.dma_start(out=biasT[:, :, :], in_=mbias_d[g]), [mbias_dep[g]], [biasT])
                    nj = 4 * g + 4

                    def qk(h, j):
                        ps_ = pSr.next()
                        M(lambda e, ps_=ps_, j=j, h=h, QT2m=QT2m: e.matmul(ps_[:, 0:512], lhsT=KT2[:, h // 2, j * 128:(j + 1) * 128], rhs=QT2m[:, h // 2, h % 2, :],
                                                                       start=True, stop=False), [KT2, QT2m], [ps_])
                        M(lambda e, ps_=ps_, j=j, h=h, biasT=biasT: e.matmul(ps_[:, 0:512], lhsT=ohb[0:16, j // 2, :], rhs=biasT[0:16, h, :], start=False, stop=True),
                          [ohb, biasT], [ps_])
                        return ps_

                    seq = [(h, j) for h in range(8) for j in range(nj)]
                    pend = qk(*seq[0])
                    for idx_, (h, j) in enumerate(seq):
                        ps_ = pend
                        if j == 0:
                            pz = pR.next()
                            feat_proj(pz[0:64, 0:512], pz, Wz, Wz, h * 64, 64, hTg, hTg, 0, 512)
                            szT = szp.next()
                            A(lambda e, pz=pz, szT=szT: e.activation(out=szT[:, :], in_=pz[0:64, 0:512], func=AF.Silu), [pz], [szT])
                            pO = pOr.next(); pD = pDr.next()
                        PT = PTp.next()
                        A(lambda e, ps_=ps_, PT=PT: e.activation(out=PT[:, :], in_=ps_[:, 0:512], func=AF.Exp), [ps_], [PT])
                        if idx_ + 1 < len(seq):
                            pend = qk(*seq[idx_ + 1])
                        if j >= 4 * g:
                            G(lambda e, PT=PT, r=j - 4 * g: e.tensor_tensor(out=PT[:, :], in0=PT[:, :], in1=CAUS[r], op=ALU.mult), [PT, cstb], [PT])
                        M(lambda e, PT=PT, j=j, h=h, nj=nj, pO=pO: e.matmul(pO[0:64, 0:512], lhsT=Vall[:, j, h * 64:(h + 1) * 64], rhs=PT[:, :], start=(j == 0), stop=(j == nj - 1)),
                          [Vall, PT], [pO])
                        M(lambda e, PT=PT, j=j, nj=nj, pD=pD: e.matmul(pD[0:64, 0:512], lhsT=ONESB[:, 0:64], rhs=PT[:, :], start=(j == 0), stop=(j == nj - 1)), [cstb, PT], [pD])
                        if j == nj - 1:
                            rd = rdp.next(); ot = otp.next()
                            V(lambda e, rd=rd, pD=pD: e.reciprocal(out=rd[:, :], in_=pD[0:64, 0:512]), [pD], [rd])
                            V(lambda e, rd=rd, ot=ot, pO=pO: e.tensor_tensor(out=ot[:, :], in0=pO[0:64, 0:512], in1=rd[:, :], op=ALU.mult), [pO, rd], [ot])
                            G(lambda e, ot=ot, szT=szT, h=h: e.tensor_tensor(out=ZbT[:, h, :], in0=ot[:, :], in1=szT[:, :], op=ALU.mult), [ot, szT], [ZbT])
                    terms = [((lambda c, h=h: Wmb[0:64, h, c * 128:(c + 1) * 128]), ZbT[0:64, h, :]) for h in range(8)]
                    branch_out(terms, ZbT, Wmb, Wmb, Wgt, Wgt, hTg, 512, yTg, first, bw)
                    DMA(lambda e, yTg=yTg, g=g: e.dma_start(out=yT_d[g], in_=yTg[:, :, :]), [yTg], [yT_dep[g]])
                if cfg.sample:
                    for h in range(8):
                        pz = pR.next()
                        feat_proj(pz[0:64, 0:NTOK], pz, Wz, Wz, h * 64, 64, hTs, hTs, 0, NTOK)
                        szT = szp.next()
                        A(lambda e, pz=pz, szT=szT: e.activation(out=szT[:, 0:NTOK], in_=pz[0:64, 0:NTOK], func=AF.Silu), [pz], [szT])
                        V(lambda e, szT=szT, h=h: e.tensor_tensor(out=ZbT[:, h, 0:NTOK], in0=OTs[:, h, :], in1=szT[:, 0:NTOK], op=ALU.mult), [OTs, szT], [ZbT])
                    terms = [((lambda c, h=h: Wmb[0:64, h, c * 128:(c + 1) * 128]), ZbT[0:64, h, 0:NTOK]) for h in range(8)]
                    branch_out(terms, ZbT, Wmb, Wmb, Wgt, Wgt, hTs, NTOK, yTs, first, bw)
                barrier()
            cur[0] = gst

        def phase_B(l, first):
            with contextlib.ExitStack() as bst:
                cur[0] = bst
                KT2 = sb([128, 4, L], BF16, "KT2"); Vall = sb([128, NT, 512], BF16, "Vall")
                phase_B1(l, KT2, Vall)
                cur[0] = bst
                phase_B2(l, first, KT2, Vall)
            cur[0] = gst

        for l in range(DEPTH):
            if cfg.prompt:
                if l == 0:
                    phase_N(0)
                if STOP == 1:
                    break
                first = True
                if "g" in cfg.branches:
                    phase_G(l, first); first = False
                    if STOP <= 4:
                        break
                if "m" in cfg.branches:
                    phase_M(l, first); first = False
                    if STOP <= 4:
                        break
                if "b" in cfg.branches:
                    phase_B(l, first); first = False
                phase_O(l)
        P.emit()
    return nc


W_NAMES = ("norm_g", "w_in", "gla_w_a2", "gla_b_a", "gla_norm", "gla_w_out", "ml_conv_w", "ml_conv_b", "ml_w_q", "ml_w_k",
           "ml_w_v", "ml_b_i", "ml_b_f", "ml_norm", "ml_skip", "ml_w_out", "mb_q_norm", "mb_k_norm", "mb_w_out", "w_out")
_CACHE = {}


def run(cfg, inputs, n_cores=8):
    key = (cfg.L, cfg.NS, cfg.TS, cfg.NPG, cfg.NPOOL, cfg.DEPTH, cfg.branches, cfg.sample, cfg.prompt)
    if key not in _CACHE:
        _CACHE[key] = build(cfg)
    nc = _CACHE[key]
    f32 = lambda a: np.ascontiguousarray(np.asarray(a), dtype=np.float32)
    consts = make_consts()
    Bp = inputs["x_prompt"].shape[0]
    NS, TS, DEPTH = cfg.NS, cfg.TS, cfg.DEPTH
    ck = f32(inputs["cache_k"]).reshape(DEPTH, cfg.NPOOL * 128, 512)
    cv = f32(inputs["cache_v"]).reshape(DEPTH, cfg.NPOOL * 128, 512)
    wts = {k: f32(inputs[k]) for k in W_NAMES}
    in_maps = []
    for c in range(n_cores):
        b = c % Bp
        sl = slice(c * NS, (c + 1) * NS)
        m = dict(wts)
        m["consts"] = consts
        m["xp"] = f32(inputs["x_prompt"][b])
        m["xs"] = f32(inputs["x_sample"][sl]).reshape(NS * TS, D)
        for l_ in range(DEPTH):
            m["ck%d" % l_] = ck[l_]
            m["cv%d" % l_] = cv[l_]
        m["pt"] = np.ascontiguousarray(np.asarray(inputs["page_table"])[sl], dtype=np.int32)
        m["sg"] = f32(np.asarray(inputs["state_gla"])[:, sl])
        m["sc"] = f32(np.asarray(inputs["state_mlstm_c"])[:, sl])
        m["sn"] = f32(np.asarray(inputs["state_mlstm_n"])[:, sl])
        m["sm"] = f32(np.asarray(inputs["state_mlstm_m"])[:, sl])
        m["scv"] = f32(np.asarray(inputs["state_mlstm_conv"])[:, sl])
        in_maps.append(m)
    res = run_bass_kernel_spmd(nc, in_maps, core_ids=list(range(n_cores))).results
    L = cfg.L
    nb = min(Bp, n_cores)
    st = lambda name, shp: np.stack([res[b][name].reshape(shp) for b in range(nb)], axis=1)
    y_prompt = np.stack([res[b]["yp"] for b in range(nb)], 0)
    y_sample = np.concatenate([res[c]["ys"].reshape(NS, TS, D) for c in range(n_cores)], 0)
    p_gla = st("p_gla", (DEPTH, 4, 64, 128)); p_mc = st("p_mc", (DEPTH, 4, 128, 128)); p_mn = st("p_mn", (DEPTH, 4, 128))
    p_mm = st("p_mm", (DEPTH, 4)); p_conv = st("p_conv", (DEPTH, 3, 512))
    p_k = st("p_k", (DEPTH, L, 8, 64)); p_v = st("p_v", (DEPTH, L, 8, 64))
    cs = lambda name, shp: np.concatenate([res[c][name].reshape(shp) for c in range(n_cores)], axis=1)
    s_gla = cs("s_gla", (DEPTH, NS, 4, 64, 128)); s_mc = cs("s_mc", (DEPTH, NS, 4, 128, 128)); s_mn = cs("s_mn", (DEPTH, NS, 4, 128))
    s_mm = cs("s_mm", (DEPTH, NS, 4)); s_conv = cs("s_conv", (DEPTH, NS, 3, 512))
    s_k = cs("s_k", (DEPTH, NS, TS, 8, 64)); s_v = cs("s_v", (DEPTH, NS, TS, 8, 64))
    return (y_prompt, y_sample, p_gla, p_mc, p_mn, p_mm, p_conv, p_k, p_v, s_gla, s_mc, s_mn, s_mm, s_conv, s_k, s_v)


def kernel(**inputs):
    return run(Cfg(), inputs, n_cores=8)
```

```python
import contextlib
import numpy as np
import concourse.bass as bass
import concourse.mybir as mybir
from concourse.bass_utils import run_bass_kernel_spmd

F32 = mybir.dt.float32
BF16 = mybir.dt.bfloat16
I32 = mybir.dt.int32
AF = mybir.ActivationFunctionType
ALU = mybir.AluOpType
AX = mybir.AxisListType

D = 1024
D_IN = 8216
EPS = 1e-6
NEG = -1e30
MB_NEG = -30000.0
C_GQ, C_GK, C_GV, C_GA, C_GZ = 0, 256, 512, 1024, 1040
C_MX, C_MI, C_MF, C_MO, C_MZ = 1552, 2064, 2068, 2072, 2584
C_BQ, C_BK, C_BV, C_BZ = 3096, 3608, 4120, 4632
C_GATE = 5144


class Dep:
    __slots__ = ("w", "r")

    def __init__(self):
        self.w = None
        self.r = []


class B:
    __slots__ = ("t", "d", "ps")

    def __init__(self, t, ps=False):
        self.t = t
        self.d = Dep()
        self.ps = ps

    def __getitem__(self, k):
        return self.t[k]


class Prog:
    ENGS = ("pe", "act", "dve", "pool", "sp")

    N_SW = 8

    def __init__(self, nc, n_dma_sems=48):
        self.nc = nc
        self.ops = {e: [] for e in self.ENGS}
        self.n_dma_sems = n_dma_sems
        self.dma_rr = 0
        self.sw_rr = 0
        self.dma_val = [0] * n_dma_sems
        self.final_dma = []

    def _deps(self, eng, reads, writes):
        deps = []
        for b in reads:
            t = b.d
            if t.w is not None:
                deps.append(t.w)
        for b in writes:
            t = b.d
            if t.w is not None:
                deps.append(t.w)
            deps.extend(t.r)
        if eng == "pe":
            deps = [d for d in deps if not (d[0] == "eng" and d[1] == "pe")]
        return deps

    def _mark(self, ref, reads, writes):
        for b in reads:
            r = b.d.r
            r.append(ref)
            if len(r) > 24:
                last = {}
                for x in r:
                    key = (x[0], x[1])
                    if key not in last or x[2] > last[key][2]:
                        last[key] = x
                b.d.r = list(last.values())
        for b in writes:
            b.d.w = ref
            b.d.r = []

    def op(self, eng, fn, reads=(), writes=()):
        if any(b.ps for b in reads):
            writes = list(writes) + [b for b in reads if b.ps]
            reads = [b for b in reads if not b.ps]
        lst = self.ops[eng]
        deps = self._deps(eng, reads, writes)
        lst.append(dict(kind="c", fn=fn, deps=deps, inc=False))
        self._mark(("eng", eng, len(lst) - 1), reads, writes)

    def dma(self, fn, reads=(), writes=(), q="sp", final=False):
        lst = self.ops[q]
        deps = self._deps(q, reads, writes)
        if q == "pool":
            k = self.n_dma_sems - self.N_SW + self.sw_rr
            self.sw_rr = (self.sw_rr + 1) % self.N_SW
        else:
            k = self.dma_rr
            self.dma_rr = (self.dma_rr + 1) % (self.n_dma_sems - self.N_SW)
        prev = self.dma_val[k]
        self.dma_val[k] += 16
        val = self.dma_val[k]
        if prev > 0:
            deps.append(("dma", k, prev))
        lst.append(dict(kind="d", fn=fn, deps=deps, sem=k, val=val))
        self._mark(("dma", k, val), reads, writes)
        if final:
            self.final_dma.append((k, val))

    def emit(self):
        nc = self.nc
        for e in self.ENGS:
            for o in self.ops[e]:
                for d in o["deps"]:
                    if d[0] == "eng":
                        self.ops[d[1]][d[2]]["inc"] = True
        vals = {}
        for e in self.ENGS:
            c = 0
            v = []
            for o in self.ops[e]:
                if o["kind"] == "c" and o["inc"]:
                    c += 1
                v.append(c)
            vals[e] = v
        with contextlib.ExitStack() as st:
            esem = {e: st.enter_context(nc.semaphore("s_" + e)) for e in self.ENGS}
            dsem = [st.enter_context(nc.semaphore("d%d" % i)) for i in range(self.n_dma_sems)]
            block = st.enter_context(nc.Block())
            engobj = {"pe": "tensor", "act": "scalar", "dve": "vector", "pool": "gpsimd", "sp": "sync"}

            def run(e, eng):
                seen = {}
                dseen = {}
                for o in self.ops[e]:
                    need = {}
                    dneed = {}
                    for d in o["deps"]:
                        if d[0] == "eng":
                            v = vals[d[1]][d[2]]
                            if v > seen.get(d[1], 0):
                                need[d[1]] = max(need.get(d[1], 0), v)
                        else:
                            if d[2] > dseen.get(d[1], 0):
                                dneed[d[1]] = max(dneed.get(d[1], 0), d[2])
                    for k2, v in need.items():
                        eng.wait_ge(esem[k2], v)
                        seen[k2] = v
                    for k2, v in dneed.items():
                        eng.wait_ge(dsem[k2], v)
                        dseen[k2] = v
                    if o["kind"] == "b":
                        continue
                    ins = o["fn"](eng)
                    if o["kind"] == "d":
                        ins.then_inc(dsem[o["sem"]], 16)
                    elif o["inc"]:
                        ins.then_inc(esem[e], 1)
                if e == "sp":
                    for (k2, v) in self.final_dma:
                        if v > dseen.get(k2, 0):
                            eng.wait_ge(dsem[k2], v)
                            dseen[k2] = v

            for e in self.ENGS:
                if not self.ops[e] and e != "sp":
                    continue
                getattr(block, engobj[e])(lambda eng, e=e: run(e, eng))


K_ID, K_TRIS, K_SUTS, K_TRIN, K_TRI01, K_NEGM, K_ONES, K_SELL128, K_SELL8 = [128 * i for i in range(9)]
K_CAUS = 128 * 9
K_BD = K_CAUS + 4 * 512
K_IOTA = K_BD + 8
K_ID32 = K_IOTA + 1
K_OH = K_ID32 + 32 * 64
NCONST = K_OH + 16 * 128


def make_consts():
    c = np.zeros((128, NCONST), np.float32)
    p = np.arange(128)[:, None]
    f = np.arange(128)[None, :]
    c[:, K_ID:K_ID + 128] = (p == f)
    c[:, K_TRIS:K_TRIS + 128] = (p <= f) * (-1.0 / 16.0)
    c[:, K_SUTS:K_SUTS + 128] = (p > f) * (-1.0 / 16.0)
    c[:, K_TRIN:K_TRIN + 128] = (p <= f) * (-1.0)
    c[:, K_TRI01:K_TRI01 + 128] = (p <= f)
    c[:, K_NEGM:K_NEGM + 128] = np.where(f <= p, 0.0, NEG)
    c[:, K_ONES:K_ONES + 128] = 1.0
    c[:, K_SELL128:K_SELL128 + 128] = (p == 127)
    c[:, K_SELL8:K_SELL8 + 128] = (p == 7)
    f5 = np.arange(512)[None, :]
    for r in range(4):
        c[:, K_CAUS + 512 * r:K_CAUS + 512 * (r + 1)] = ((128 * r + p) <= f5)
    c[:, K_BD:K_BD + 8] = ((p // 8) == np.arange(8)[None, :])
    c[:, K_IOTA] = np.arange(128)
    k32 = np.arange(128)[:, None, None]
    n32 = np.arange(32)[None, :, None]
    c[:, K_ID32:K_ID32 + 32 * 64] = np.broadcast_to((k32 == n32), (128, 32, 64)).reshape(128, 2048)
    c[:, K_OH:K_OH + 2048] = np.broadcast_to((np.arange(128)[:, None, None] == np.arange(16)[None, :, None]), (128, 16, 128)).reshape(128, 2048)
    return c


class Cfg:
    def __init__(self, L=4096, NS=4, TS=8, NPG=64, NPOOL=2560, DEPTH=2, branches="gmb", sample=True, prompt=True):
        self.L, self.NS, self.TS, self.NPG, self.NPOOL, self.DEPTH = L, NS, TS, NPG, NPOOL, DEPTH
        self.branches, self.sample, self.prompt = branches, sample, prompt


def build(cfg):
    nc = bass.Bass("TRN2", target_bir_lowering=False)
    import os
    STOP = int(os.environ.get("KSTOP", "99"))
    L, NS, TS, NPG, NPOOL, DEPTH = cfg.L, cfg.NS, cfg.TS, cfg.NPG, cfg.NPOOL, cfg.DEPTH
    NT, NG = L // 128, L // 512
    NBLK = L // 256
    P = Prog(nc)
    uid = [0]

    def din(name, shape, dt=F32):
        return nc.dram_tensor(name, list(shape), dt, kind="ExternalInput").ap()

    def dout(name, shape, dt=F32):
        return B(nc.dram_tensor(name, list(shape), dt, kind="ExternalOutput").ap())

    def dscr(name, shape, dt):
        return nc.dram_tensor(name, list(shape), dt).ap()

    xp = din("xp", [L, D]); xs = din("xs", [NS * TS, D])
    ck = [din("ck%d" % l_, [NPOOL * 128, 512]) for l_ in range(DEPTH)]; cv = [din("cv%d" % l_, [NPOOL * 128, 512]) for l_ in range(DEPTH)]
    pt = din("pt", [NS, NPG], I32)
    sg_in = din("sg", [DEPTH, NS, 4, 64, 128]); sc_in = din("sc", [DEPTH, NS, 4, 128, 128])
    sn_in = din("sn", [DEPTH, NS, 4, 128]); sm_in = din("sm", [DEPTH, NS, 4]); scv_in = din("scv", [DEPTH, NS, 3, 512])
    w = {}
    for name, shape in (("norm_g", [DEPTH, D]), ("w_in", [DEPTH, D, D_IN]), ("gla_w_a2", [DEPTH, 16, 256]),
                        ("gla_b_a", [DEPTH, 256]), ("gla_norm", [DEPTH, 128]), ("gla_w_out", [DEPTH, 512, D]),
                        ("ml_conv_w", [DEPTH, 4, 512]), ("ml_conv_b", [DEPTH, 512]), ("ml_w_q", [DEPTH, 4, 128, 128]),
                        ("ml_w_k", [DEPTH, 4, 128, 128]), ("ml_w_v", [DEPTH, 4, 128, 128]), ("ml_b_i", [DEPTH, 4]),
                        ("ml_b_f", [DEPTH, 4]), ("ml_norm", [DEPTH, 128]), ("ml_skip", [DEPTH, 512]),
                        ("ml_w_out", [DEPTH, 512, D]), ("mb_q_norm", [DEPTH, 64]), ("mb_k_norm", [DEPTH, 64]),
                        ("mb_w_out", [DEPTH, 512, D]), ("w_out", [DEPTH, D, D])):
        w[name] = din(name, shape)
    consts_d = din("consts", [128, NCONST])

    yp = dout("yp", [L, D]); ys = dout("ys", [NS * TS, D])
    p_gla = dout("p_gla", [DEPTH, 4, 64, 128]); p_mc = dout("p_mc", [DEPTH, 4, 128, 128])
    p_mn = dout("p_mn", [DEPTH, 4, 128]); p_mm = dout("p_mm", [DEPTH, 4]); p_conv = dout("p_conv", [DEPTH, 3, 512])
    p_k = dout("p_k", [DEPTH, L, 512]); p_v = dout("p_v", [DEPTH, L, 512])
    s_gla = dout("s_gla", [DEPTH, NS, 4, 64, 128]); s_mc = dout("s_mc", [DEPTH, NS, 4, 128, 128])
    s_mn = dout("s_mn", [DEPTH, NS, 4, 128]); s_mm = dout("s_mm", [DEPTH, NS, 4]); s_conv = dout("s_conv", [DEPTH, NS, 3, 512])
    s_k = dout("s_k", [DEPTH, NS * TS, 512]); s_v = dout("s_v", [DEPTH, NS * TS, 512])

    hT_d = dscr("hT_d", [max(NG, 1), 128, 8, 512], BF16); hT_dep = [B(None) for _ in range(max(NG, 1))]
    yT_d = dscr("yT_d", [max(NG, 1), 128, 8, 512], BF16); yT_dep = [B(None) for _ in range(max(NG, 1))]
    x1_d = dscr("x1_d", [L, D], F32); x1_dep = [B(None) for _ in range(max(NT, 1))]
    qT_d = dscr("qT_d", [max(NG, 1), 128, 4, 2, 512], BF16); qT_dep = [B(None) for _ in range(max(NG, 1))]
    mbias_d = dscr("mbias_d", [max(NG, 1), 16, 8, 512], BF16); mbias_dep = [B(None) for _ in range(max(NG, 1))]

    with contextlib.ExitStack() as gst:
        cur = [gst]

        def sb(shape, dt, name="t"):
            uid[0] += 1
            return B(cur[0].enter_context(nc.sbuf_tensor("%s_%d" % (name, uid[0]), list(shape), dt)))

        def zsb(shape, dt, name="z"):
            b = sb(shape, dt, name)
            nd = len(shape)
            P.op("pool", lambda e: e.memset(b[(slice(None),) * nd], 0.0), [], [b])
            return b

        def psum(dt=F32, name="ps"):
            uid[0] += 1
            n = 512 if dt == F32 else 1024
            return B(cur[0].enter_context(nc.psum_tensor("%s_%d" % (name, uid[0]), [128, n], dt)), ps=True)

        class Rot:
            def __init__(self, mk, n):
                self.b = [mk() for _ in range(n)]
                self.i = 0

            def next(self):
                b = self.b[self.i % len(self.b)]
                self.i += 1
                return b

        def V(fn, r, wr): P.op("dve", fn, r, wr)
        def A(fn, r, wr): P.op("act", fn, r, wr)
        def G(fn, r, wr): P.op("pool", fn, r, wr)
        def M(fn, r, wr): P.op("pe", fn, r, wr)
        def DMA(fn, r, wr, q="sp", final=False): P.dma(fn, r, wr, q=q, final=final)

        def barrier():
            refs = []
            for e in Prog.ENGS:
                if P.ops[e]:
                    n = len(P.ops[e]) - 1
                    if P.ops[e][n]["kind"] == "c":
                        refs.append(("eng", e, n))
                    else:
                        for m in range(n, -1, -1):
                            if P.ops[e][m]["kind"] == "c":
                                refs.append(("eng", e, m))
                                break
            for k in range(P.n_dma_sems):
                if P.dma_val[k] > 0:
                    refs.append(("dma", k, P.dma_val[k]))
            for e in Prog.ENGS:
                P.ops[e].append(dict(kind="b", fn=None, deps=list(refs), inc=False))

        cst = sb([128, 8 * 128 + 16], F32, "cstf")
        cstb = sb([128, 3 * 128 + 4 * 512], BF16, "cstb")
        IDF = cst[:, 0:128]; TRIS = cst[:, 128:256]; SUTS = cst[:, 256:384]; TRIN = cst[:, 384:512]
        NEGM = cst[:, 512:640]; ONESF = cst[:, 640:768]; SELL128 = cst[:, 768:896]; SELL8 = cst[:, 896:1024]
        BDm = cst[:, 1024:1032]; IOTA = cst[:, 1032:1033]
        IDB = cstb[:, 0:128]; TRI01 = cstb[:, 128:256]; ONESB = cstb[:, 256:384]
        CAUS = [cstb[:, 384 + 512 * r:384 + 512 * (r + 1)] for r in range(4)]
        ohb = sb([16, 16, 128], BF16, "ohb")
        with contextlib.ExitStack() as pst:
            cur[0] = pst
            stg = sb([128, NCONST], F32, "cstg")
            DMA(lambda e: e.dma_start(out=stg[:, :], in_=consts_d[:, :]), [], [stg])
            for dst, src in ((0, K_ID), (128, K_TRIS), (256, K_SUTS), (384, K_TRIN), (512, K_NEGM), (640, K_ONES),
                             (768, K_SELL128), (896, K_SELL8)):
                V(lambda e, dst=dst, src=src: e.tensor_copy(out=cst[:, dst:dst + 128], in_=stg[:, src:src + 128]), [stg], [cst])
            V(lambda e: e.tensor_copy(out=cst[:, 1024:1033], in_=stg[:, K_BD:K_BD + 9]), [stg], [cst])
            for dst, src in ((0, K_ID), (128, K_TRI01), (256, K_ONES)):
                V(lambda e, dst=dst, src=src: e.tensor_copy(out=cstb[:, dst:dst + 128], in_=stg[:, src:src + 128]), [stg], [cstb])
            V(lambda e: e.tensor_copy(out=cstb[:, 384:384 + 2048], in_=stg[:, K_CAUS:K_CAUS + 2048]), [stg], [cstb])
            V(lambda e: e.tensor_copy(out=ohb[:, :, :], in_=stg[0:16, K_OH:K_OH + 2048].rearrange("p (n s) -> p n s", n=16)), [stg], [ohb])
            barrier()
        cur[0] = gst

        NTOK = NS * TS
        hTs = sb([128, 8, NTOK], BF16, "hTs"); yTs = sb([128, 8, NTOK], BF16, "yTs")
        xs1_d = dscr("xs1_d", [NTOK, D], F32); xs1_dep = B(None)

        cast_rr = [0]

        def load_w(dst, c0, ncols, src, stgpool, kc=8, rows=128):
            step = 256
            for a in range(0, ncols, step):
                n = min(step, ncols - a)
                s = stgpool.next()
                DMA(lambda e, s=s, a=a, n=n: e.dma_start(
                    out=s[0:rows, 0:kc * n].rearrange("p (k n) -> p k n", k=kc), in_=src(a, n)), [], [s])
                eng = ("pool", "dve", "act")[cast_rr[0] % 3]
                cast_rr[0] += 1
                if eng == "act":
                    fn = lambda e, s=s, a=a, n=n: e.activation(
                        out=dst[0:rows, :, c0 + a:c0 + a + n], in_=s[0:rows, 0:kc * n].rearrange("p (k n) -> p k n", k=kc), func=AF.Copy)
                else:
                    fn = lambda e, s=s, a=a, n=n: e.tensor_copy(
                        out=dst[0:rows, :, c0 + a:c0 + a + n], in_=s[0:rows, 0:kc * n].rearrange("p (k n) -> p k n", k=kc))
                P.op(eng, fn, [s], [dst])

        class WLoad:
            def __enter__(self):
                self.st = contextlib.ExitStack()
                self.prev = cur[0]
                self.st.__enter__()
                cur[0] = self.st
                self.pool = Rot(lambda: sb([128, 8 * 256], F32, "wstg"), 3)
                cur[0] = self.prev
                return self.pool

            def __exit__(self, *a):
                barrier()
                self.st.__exit__(*a)
                return False

        def w_in_src(l, col0):
            return lambda a, n: w["w_in"][l, :, col0 + a:col0 + a + n].rearrange("(k p) n -> p k n", p=128)

        def tok_proj(po, pob, hT, hTb, t0, T, W, Wb_, c0, n):
            for k in range(8):
                M(lambda e, k=k: e.matmul(po, lhsT=hT[:, k, t0:t0 + T], rhs=W[:, k, c0:c0 + n], start=(k == 0), stop=(k == 7)),
                  [hTb, Wb_], [pob])

        def feat_proj(po, pob, W, Wb_, c0, m, hT, hTb, t0, N):
            for k in range(8):
                M(lambda e, k=k: e.matmul(po, lhsT=W[:, k, c0:c0 + m], rhs=hT[:, k, t0:t0 + N], start=(k == 0), stop=(k == 7)),
                  [hTb, Wb_], [pob])

        def rstd_from_ss(ss_ap, ssb, out_ap, outb, tmp_ap, tmpb, inv_n):
            A(lambda e: e.activation(out=tmp_ap, in_=ss_ap, func=AF.Ln, scale=inv_n, bias=EPS), [ssb], [tmpb])
            A(lambda e: e.activation(out=out_ap, in_=tmp_ap, func=AF.Exp, scale=-0.5), [tmpb], [outb])

        def bvec(src_ap, n, name):
            t = sb([128, n], F32, name)
            DMA(lambda e: e.dma_start(out=t[:, :], in_=src_ap.partition_broadcast(128)), [], [t])
            return t

        def norm_tile(xt, T, gn, hTg, off, wk):
            junk, ss, lnv, rstd, hb, pT = wk["junk"], wk["ss"].next(), wk["lnv"].next(), wk["rstd"].next(), wk["hb"].next(), wk["pT"].next()
            A(lambda e: e.activation(out=junk[0:T, :], in_=xt[0:T, :], func=AF.Square, accum_out=ss[0:T, 0:1]), [xt], [junk, ss])
            rstd_from_ss(ss[0:T, 0:1], ss, rstd[0:T, 0:1], rstd, lnv[0:T, 0:1], lnv, 1.0 / D)
            V(lambda e: e.scalar_tensor_tensor(out=hb[0:T, :], in0=xt[0:T, :], scalar=rstd[0:T, 0:1], in1=gn[0:T, :],
                                               op0=ALU.mult, op1=ALU.mult), [xt, rstd, gn], [hb])
            pTv = pT[:, :].rearrange("p (k t) -> p k t", k=8)
            for k in range(8):
                M(lambda e, k=k: e.transpose(out=pTv[:, k, 0:T], in_=hb[0:T, k * 128:(k + 1) * 128], identity=IDB[0:T, 0:T]),
                  [hb, cstb], [pT])
            A(lambda e: e.activation(out=hTg[:, :, off:off + T], in_=pTv[:, :, 0:T], func=AF.Copy), [pT], [hTg])

        def norm_work():
            return dict(junk=sb([128, D], BF16, "junk"), ss=Rot(lambda: sb([128, 1], F32, "ss"), 2),
                        lnv=Rot(lambda: sb([128, 1], F32, "lnv"), 2), rstd=Rot(lambda: sb([128, 1], F32, "rstd"), 2),
                        hb=Rot(lambda: sb([128, D], BF16, "hb"), 2), pT=Rot(lambda: psum(BF16, "pTn"), 2))

        def phase_N(l):
            with contextlib.ExitStack() as pst:
                cur[0] = pst
                gn = bvec(w["norm_g"][l:l + 1, :], D, "gn")
                wk = norm_work()
                xpool = Rot(lambda: sb([128, D], F32, "xt"), 3)
                hpool = Rot(lambda: sb([128, 8, 512], BF16, "hTg"), 2)
                for g in range(NG):
                    hTg = hpool.next()
                    for j in range(4):
                        i = 4 * g + j
                        xt = xpool.next()
                        DMA(lambda e, xt=xt, i=i: e.dma_start(out=xt[:, :], in_=xp[i * 128:(i + 1) * 128, :]), [], [xt])
                        norm_tile(xt, 128, gn, hTg, j * 128, wk)
                    DMA(lambda e, hTg=hTg, g=g: e.dma_start(out=hT_d[g], in_=hTg[:, :, :]), [hTg], [hT_dep[g]])
                if cfg.sample:
                    xt = xpool.next()
                    DMA(lambda e, xt=xt: e.dma_start(out=xt[0:NTOK, :], in_=xs[:, :]), [], [xt])
                    norm_tile(xt, NTOK, gn, hTs, 0, wk)
                barrier()
            cur[0] = gst

        def branch_out(ZT_list, Zb_, Wo, Wob, Wgt, Wgtb, hTg, N, yTg, first, wk, kdim=128):
            for c in range(8):
                pg = wk["pG"].next()
                feat_proj(pg[:, 0:N], pg, Wgt, Wgtb, c * 128, 128, hTg, hTg, 0, N)
                sg = wk["sig"].next()
                A(lambda e, pg=pg, sg=sg: e.activation(out=sg[:, 0:N], in_=pg[:, 0:N], func=AF.Sigmoid), [pg], [sg])
                py = wk["pY"].next()
                nterm = len(ZT_list)
                for ti, (lhs_fn, rhs_ap) in enumerate(ZT_list):
                    M(lambda e, py=py, lhs_fn=lhs_fn, rhs_ap=rhs_ap, ti=ti, c=c: e.matmul(
                        py[:, 0:N], lhsT=lhs_fn(c), rhs=rhs_ap, start=(ti == 0), stop=(ti == nterm - 1)), [Zb_, Wob], [py])
                if first:
                    V(lambda e, py=py, sg=sg, c=c: e.tensor_tensor(out=yTg[:, c, 0:N], in0=py[:, 0:N], in1=sg[:, 0:N], op=ALU.mult),
                      [py, sg], [yTg])
                else:
                    tm = wk["tmpy"].next()
                    V(lambda e, py=py, sg=sg, tm=tm: e.tensor_tensor(out=tm[:, 0:N], in0=py[:, 0:N], in1=sg[:, 0:N], op=ALU.mult),
                      [py, sg], [tm])
                    G(lambda e, tm=tm, c=c: e.tensor_tensor(out=yTg[:, c, 0:N], in0=yTg[:, c, 0:N], in1=tm[:, 0:N], op=ALU.add),
                      [tm, yTg], [yTg])

        def branch_work(pR):
            return dict(pG=pR, pY=pR,
                        sig=Rot(lambda: sb([128, 512], F32, "sig"), 2), tmpy=Rot(lambda: sb([128, 512], F32, "tmpy"), 2))

        def load_gate_w(Wgt, l, bidx, stgpool):
            load_w(Wgt, 0, D, w_in_src(l, C_GATE + bidx * D), stgpool)

        def load_outw(Wo, name, l, stgpool):
            load_w(Wo, 0, D, lambda a, n: w[name][l, :, a:a + n].rearrange("(k p) n -> p k n", p=128), stgpool, kc=4)

        def small_bf16(src_ap, rows, cols, name):
            s = sb([rows, cols], F32, name + "_f")
            d = sb([rows, cols], BF16, name)
            DMA(lambda e: e.dma_start(out=s[:, :], in_=src_ap), [], [s])
            V(lambda e: e.tensor_copy(out=d[:, :], in_=s[:, :]), [s], [d])
            return d

        def gla_consts(l):
            return dict(wa2=small_bf16(w["gla_w_a2"][l], 16, 256, "wa2"),
                        ba=small_bf16(w["gla_b_a"][l:l + 1, :], 1, 256, "ba"),
                        gnorm=bvec(w["gla_norm"][l:l + 1, :], 128, "gnorm"))

        def gla_work():
            return dict(
                pTr=psum(BF16, "pTr"), pB=psum(F32, "pB"), pA=psum(F32, "pA"), pO=psum(F32, "pO"), pSU=psum(F32, "pSU"),
                pR=Rot(lambda: psum(F32, "pR"), 3),
                qT=sb([128, 2, 512], F32, "qTs"), kT=sb([128, 2, 512], F32, "kTs"), aT=sb([16, 512], BF16, "aTs"),
                sp=Rot(lambda: sb([128, 256], F32, "sp"), 2), E1=Rot(lambda: sb([128, 2, 128], F32, "E1"), 2),
                E2=Rot(lambda: sb([128, 2, 128], F32, "E2"), 2), E3=Rot(lambda: sb([128, 256], F32, "E3"), 2),
                qt=Rot(lambda: zsb([128, 2, 2, 128], BF16, "qtm"), 2), kt=Rot(lambda: sb([128, 2, 128], BF16, "kt"), 2),
                vtok=Rot(lambda: sb([128, 512], BF16, "vtok"), 2), kh=Rot(lambda: sb([128, 256], BF16, "kh"), 2),
                sz=Rot(lambda: sb([128, 512], F32, "sz"), 2), AT=Rot(lambda: sb([128, 4, 128], BF16, "AT"), 2),
                osq=Rot(lambda: sb([128, 512], F32, "osq"), 1), ss4=Rot(lambda: sb([128, 4], F32, "ss4"), 2),
                ln4=Rot(lambda: sb([128, 4], F32, "ln4"), 2), rs4=Rot(lambda: sb([128, 4], F32, "rs4"), 2),
                t1=Rot(lambda: sb([128, 512], F32, "t1"), 2), g1=Rot(lambda: sb([128, 512], F32, "g1"), 2),
                Zg=Rot(lambda: sb([128, 512], BF16, "Zg"), 2))

        def gla_group_prep(Wg, hTg, N, wk, hoff=0):
            for c in range(2):
                pf = wk["pR"].next()
                feat_proj(pf[:, 0:N], pf, Wg, Wg, C_GQ + c * 128, 128, hTg, hTg, hoff, N)
                A(lambda e, pf=pf, c=c: e.activation(out=wk["qT"][:, c, 0:N], in_=pf[:, 0:N], func=AF.Copy), [pf], [wk["qT"]])
                pf = wk["pR"].next()
                feat_proj(pf[:, 0:N], pf, Wg, Wg, C_GK + c * 128, 128, hTg, hTg, hoff, N)
                V(lambda e, pf=pf, c=c: e.tensor_copy(out=wk["kT"][:, c, 0:N], in_=pf[:, 0:N]), [pf], [wk["kT"]])
            pf = wk["pR"].next()
            feat_proj(pf[0:16, 0:N], pf, Wg, Wg, C_GA, 16, hTg, hTg, hoff, N)
            A(lambda e, pf=pf: e.activation(out=wk["aT"][0:16, 0:N], in_=pf[0:16, 0:N], func=AF.Copy), [pf], [wk["aT"]])

        def gla_tile(T, t0, Wg, hTg, gc, S2, S2b, ZgT, wk, hoff=None, zoff=None):
            import os
            KSTEP = int(os.environ.get('KSTEP', '99'))
            hc = t0 if hoff is None else hoff
            zc = t0 if zoff is None else zoff
            pTrb, pB, pA, pO, pSU = wk["pTr"], wk["pB"], wk["pA"], wk["pO"], wk["pSU"]
            pMisc = wk["pR"].next()
            qT, kT, aT = wk["qT"], wk["kT"], wk["aT"]
            wa2, ba, gnorm = gc["wa2"], gc["ba"], gc["gnorm"]
            M(lambda e: e.matmul(pMisc[0:T, 0:256], lhsT=aT[0:16, t0:t0 + T], rhs=wa2[0:16, :], start=True, stop=False), [aT, wa2], [pMisc])
            M(lambda e: e.matmul(pMisc[0:T, 0:256], lhsT=ONESB[0:1, 0:T], rhs=ba[0:1, :], start=False, stop=True), [cstb, ba], [pMisc])
            if KSTEP < 2:
                return
            sp = wk["sp"].next()
            A(lambda e: e.activation(out=sp[0:T, :], in_=pMisc[0:T, 0:256], func=AF.Exp, scale=-1.0), [pMisc], [sp])
            A(lambda e: e.activation(out=sp[0:T, :], in_=sp[0:T, :], func=AF.Ln, bias=1.0), [sp], [sp])
            if KSTEP < 3:
                return
            pBv = pB[:, 0:256].rearrange("p (c t) -> p c t", c=2)
            for c in range(2):
                M(lambda e, c=c: e.matmul(pBv[:, c, 0:T], lhsT=sp[0:T, c * 128:(c + 1) * 128], rhs=TRIS[0:T, 0:T], start=True, stop=True), [sp, cst], [pB])
            M(lambda e: e.matmul(pB[0:T, 256:512], lhsT=SUTS[0:T, 0:T], rhs=sp[0:T, 0:256], start=True, stop=True), [sp, cst], [pB])
            E1, E2, E3 = wk["E1"].next(), wk["E2"].next(), wk["E3"].next()
            A(lambda e: e.activation(out=E1[:, :, 0:T], in_=pBv[:, :, 0:T], func=AF.Exp), [pB], [E1])
            A(lambda e: e.activation(out=E2[:, :, 0:T], in_=pBv[:, :, 0:T], func=AF.Exp, scale=-1.0), [pB], [E2])
            A(lambda e: e.activation(out=E3[0:T, :], in_=pB[0:T, 256:512], func=AF.Exp), [pB], [E3])
            if KSTEP < 5:
                return
            qt, kt = wk["qt"].next(), wk["kt"].next()
            for hl in range(2):
                r = hl * 64
                V(lambda e, hl=hl, r=r: e.scalar_tensor_tensor(out=qt[r:r + 64, :, hl, 0:T], in0=qT[r:r + 64, :, t0:t0 + T], scalar=0.125,
                                                               in1=E1[r:r + 64, :, 0:T], op0=ALU.mult, op1=ALU.mult), [qT, E1], [qt])
            V(lambda e: e.tensor_tensor(out=kt[:, :, 0:T], in0=kT[:, :, t0:t0 + T], in1=E2[:, :, 0:T], op=ALU.mult), [kT, E2], [kt])
            if KSTEP < 6:
                return
            vtok, kh, sz = wk["vtok"].next(), wk["kh"].next(), wk["sz"].next()
            pv = wk["pR"].next()
            tok_proj(pv[0:T, 0:512], pv, hTg, hTg, hc, T, Wg, Wg, C_GV, 512)
            A(lambda e: e.activation(out=vtok[0:T, :], in_=pv[0:T, :], func=AF.Copy), [pv], [vtok])
            pk = wk["pR"].next()
            tok_proj(pk[0:T, 0:256], pk, hTg, hTg, hc, T, Wg, Wg, C_GK, 256)
            V(lambda e: e.tensor_tensor(out=kh[0:T, :], in0=pk[0:T, 0:256], in1=E3[0:T, :], op=ALU.mult), [pk, E3], [kh])
            pz = wk["pR"].next()
            tok_proj(pz[0:T, 0:512], pz, hTg, hTg, hc, T, Wg, Wg, C_GZ, 512)
            A(lambda e: e.activation(out=sz[0:T, :], in_=pz[0:T, :], func=AF.Silu), [pz], [sz])
            if KSTEP < 7:
                return
            pAv = pA[:, :].rearrange("p (h t) -> p h t", h=4)
            for h in range(4):
                c, r = h // 2, (h % 2) * 64
                M(lambda e, h=h, c=c: e.matmul(pAv[0:T, h, 0:T], lhsT=kt[:, c, 0:T], rhs=qt[:, c, h % 2, 0:T], start=True, stop=True),
                  [kt, qt], [pA])
            AT = wk["AT"].next()
            V(lambda e: e.tensor_tensor(out=AT[0:T, :, 0:T], in0=pAv[0:T, :, 0:T],
                                        in1=TRI01[0:T, 0:T].unsqueeze(1).to_broadcast([T, 4, T]), op=ALU.mult), [pA, cstb], [AT])
            if KSTEP < 8:
                return
            for h in range(4):
                c, r = h // 2, (h % 2) * 64
                M(lambda e, h=h: e.matmul(pO[0:T, h * 128:(h + 1) * 128], lhsT=AT[0:T, h, 0:T], rhs=vtok[0:T, h * 128:(h + 1) * 128],
                                          start=True, stop=False), [AT, vtok], [pO])
                M(lambda e, h=h, c=c: e.matmul(pO[0:T, h * 128:(h + 1) * 128], lhsT=qt[:, c, h % 2, 0:T], rhs=S2b[:, c, :],
                                               start=False, stop=True), [qt, S2b], [pO])
            if KSTEP < 9:
                return
            for c in range(2):
                M(lambda e, c=c: e.matmul(pSU[:, c * 256:(c + 1) * 256], lhsT=kh[0:T, c * 128:(c + 1) * 128], rhs=vtok[0:T, c * 256:(c + 1) * 256],
                                          start=True, stop=True), [kh, vtok], [pSU])
            for c in range(2):
                for hl in range(2):
                    r = hl * 64
                    V(lambda e, c=c, hl=hl, r=r: e.scalar_tensor_tensor(
                        out=S2[r:r + 64, c, :], in0=S2[r:r + 64, c, :], scalar=E1[r:r + 64, c, T - 1:T],
                        in1=pSU[r:r + 64, c * 256 + hl * 128:c * 256 + (hl + 1) * 128], op0=ALU.mult, op1=ALU.add), [S2, E1, pSU], [S2])
            G(lambda e: e.tensor_copy(out=S2b[:, :, :], in_=S2[:, :, :]), [S2], [S2b])
            if KSTEP < 10:
                return
            osq, ss4, ln4, rs4, t1, g1, Zg = [wk[k].next() for k in ("osq", "ss4", "ln4", "rs4", "t1", "g1", "Zg")]
            A(lambda e: e.activation(out=osq[0:T, :], in_=pO[0:T, :], func=AF.Square), [pO], [osq])
            V(lambda e: e.tensor_reduce(out=ss4[0:T, :], in_=osq[0:T, :].rearrange("p (h v) -> p h v", h=4), axis=AX.X, op=ALU.add), [osq], [ss4])
            rstd_from_ss(ss4[0:T, :], ss4, rs4[0:T, :], rs4, ln4[0:T, :], ln4, 1.0 / 128)
            V(lambda e: e.tensor_tensor(out=t1[0:T, :].rearrange("p (h v) -> p h v", h=4), in0=pO[0:T, :].rearrange("p (h v) -> p h v", h=4),
                                        in1=rs4[0:T, :].unsqueeze(2).to_broadcast([T, 4, 128]), op=ALU.mult), [pO, rs4], [t1])
            G(lambda e: e.tensor_tensor(out=g1[0:T, :].rearrange("p (h v) -> p h v", h=4), in0=sz[0:T, :].rearrange("p (h v) -> p h v", h=4),
                                        in1=gnorm[0:T, :].unsqueeze(1).to_broadcast([T, 4, 128]), op=ALU.mult), [sz, gnorm], [g1])
            V(lambda e: e.tensor_tensor(out=Zg[0:T, :], in0=t1[0:T, :], in1=g1[0:T, :], op=ALU.mult), [t1, g1], [Zg])
            if KSTEP < 11:
                return
            pTr = pTrb[:, 0:512].rearrange("p (k t) -> p k t", k=4)
            for k4 in range(4):
                M(lambda e, k4=k4: e.transpose(out=pTr[:, k4, 0:T], in_=Zg[0:T, k4 * 128:(k4 + 1) * 128], identity=IDB[0:T, 0:T]), [Zg, cstb], [pTrb])
            A(lambda e: e.activation(out=ZgT[:, :, zc:zc + T], in_=pTr[:, :, 0:T], func=AF.Copy), [pTrb], [ZgT])

        def phase_G(l, first):
            with contextlib.ExitStack() as pst:
                cur[0] = pst
                Wg = sb([128, 8, 1552], BF16, "Wg"); Wgt = sb([128, 8, D], BF16, "Wgt"); Wo = sb([128, 4, D], BF16, "Wo")
                with WLoad() as stg:
                    load_w(Wg, 0, 1552, w_in_src(l, C_GQ), stg)
                    load_gate_w(Wgt, l, 0, stg)
                    load_outw(Wo, "gla_w_out", l, stg)
                gc = gla_consts(l)
                wk = gla_work(); bw = branch_work(wk["pR"])
                S2 = sb([128, 2, 128], F32, "S2"); S2b = sb([128, 2, 128], BF16, "S2b")
                V(lambda e: e.memset(S2[:, :, :], 0.0), [], [S2]); V(lambda e: e.memset(S2b[:, :, :], 0.0), [], [S2b])
                hpool = Rot(lambda: sb([128, 8, 512], BF16, "hTg"), 2)
                ypool = Rot(lambda: sb([128, 8, 512], BF16, "yTg"), 2)
                zpool = Rot(lambda: sb([128, 4, 512], BF16, "ZgT"), 2)
                for g in range(NG):
                    hTg = hpool.next(); yTg = ypool.next(); ZgT = zpool.next()
                    DMA(lambda e, hTg=hTg, g=g: e.dma_start(out=hTg[:, :, :], in_=hT_d[g]), [hT_dep[g]], [hTg])
                    if not first:
                        DMA(lambda e, yTg=yTg, g=g: e.dma_start(out=yTg[:, :, :], in_=yT_d[g]), [yT_dep[g]], [yTg])
                    gla_group_prep(Wg, hTg, 512, wk)
                    for j in range(4):
                        gla_tile(128, j * 128, Wg, hTg, gc, S2, S2b, ZgT, wk)
                    terms = [((lambda c, k4=k4: Wo[:, k4, c * 128:(c + 1) * 128]), ZgT[:, k4, :]) for k4 in range(4)]
                    if STOP >= 4:
                        branch_out(terms, ZgT, Wo, Wo, Wgt, Wgt, hTg, 512, yTg, first, bw)
                    DMA(lambda e, yTg=yTg, g=g: e.dma_start(out=yT_d[g], in_=yTg[:, :, :]), [yTg], [yT_dep[g]])
                DMA(lambda e: e.dma_start(out=p_gla[l].rearrange("(c hl) d v -> (hl d) c v", hl=2), in_=S2[:, :, :]), [S2], [p_gla], final=True)
                if cfg.sample:
                    ZgTs = zpool.next()
                    for sq_ in range(NS):
                        DMA(lambda e, sq_=sq_: e.dma_start(out=S2[:, :, :], in_=sg_in[l, sq_].rearrange("(c hl) d v -> (hl d) c v", hl=2)), [], [S2])
                        G(lambda e: e.tensor_copy(out=S2b[:, :, :], in_=S2[:, :, :]), [S2], [S2b])
                        gla_group_prep(Wg, hTs, TS, wk, hoff=sq_ * TS)
                        gla_tile(TS, 0, Wg, hTs, gc, S2, S2b, ZgTs, wk, hoff=sq_ * TS, zoff=sq_ * TS)
                        DMA(lambda e, sq_=sq_: e.dma_start(out=s_gla[l, sq_].rearrange("(c hl) d v -> (hl d) c v", hl=2), in_=S2[:, :, :]), [S2], [], final=True)
                    terms = [((lambda c, k4=k4: Wo[:, k4, c * 128:(c + 1) * 128]), ZgTs[:, k4, 0:NTOK]) for k4 in range(4)]
                    branch_out(terms, ZgTs, Wo, Wo, Wgt, Wgt, hTs, NTOK, yTs, first, bw)
                barrier()
            cur[0] = gst

        def phase_O(l):
            last = (l == DEPTH - 1)
            with contextlib.ExitStack() as pst:
                cur[0] = pst
                Wout = sb([128, 8, D], BF16, "Wout")
                with WLoad() as stg:
                    load_w(Wout, 0, D, lambda a, n: w["w_out"][l, :, a:a + n].rearrange("(k p) n -> p k n", p=128), stg)
                pR = Rot(lambda: psum(F32, "pRo"), 4)
                xpool = Rot(lambda: sb([128, D], F32, "xt"), 3)
                xnpool = Rot(lambda: sb([128, D], F32, "xn"), 3)
                ypool = Rot(lambda: sb([128, 8, 512], BF16, "yTg"), 2)
                if not last:
                    gn = bvec(w["norm_g"][l + 1:l + 2, :], D, "gn")
                    wkn = norm_work()
                    hpool = Rot(lambda: sb([128, 8, 512], BF16, "hTg"), 2)
                for g in range(NG):
                    yTg = ypool.next()
                    DMA(lambda e, yTg=yTg, g=g: e.dma_start(out=yTg[:, :, :], in_=yT_d[g]), [yT_dep[g]], [yTg])
                    if not last:
                        hTg = hpool.next()
                    for j in range(4):
                        i = 4 * g + j
                        xt = xpool.next(); xn = xnpool.next()
                        if l == 0:
                            DMA(lambda e, xt=xt, i=i: e.dma_start(out=xt[:, :], in_=xp[i * 128:(i + 1) * 128, :]), [], [xt])
                        else:
                            DMA(lambda e, xt=xt, i=i: e.dma_start(out=xt[:, :], in_=x1_d[i * 128:(i + 1) * 128, :]), [x1_dep[i]], [xt])
                        for half in range(2):
                            py = pR.next()
                            for c in range(8):
                                M(lambda e, py=py, c=c, j=j, half=half, yTg=yTg: e.matmul(
                                    py[:, 0:512], lhsT=yTg[:, c, j * 128:(j + 1) * 128], rhs=Wout[:, c, half * 512:(half + 1) * 512],
                                    start=(c == 0), stop=(c == 7)), [yTg, Wout], [py])
                            V(lambda e, py=py, xt=xt, xn=xn, half=half: e.tensor_tensor(
                                out=xn[:, half * 512:(half + 1) * 512], in0=py[:, 0:512], in1=xt[:, half * 512:(half + 1) * 512], op=ALU.add),
                              [py, xt], [xn])
                        if last:
                            DMA(lambda e, xn=xn, i=i: e.dma_start(out=yp[i * 128:(i + 1) * 128, :], in_=xn[:, :]), [xn], [], final=True)
                        else:
                            DMA(lambda e, xn=xn, i=i: e.dma_start(out=x1_d[i * 128:(i + 1) * 128, :], in_=xn[:, :]), [xn], [x1_dep[i]])
                            norm_tile(xn, 128, gn, hTg, j * 128, wkn)
                    if not last:
                        DMA(lambda e, hTg=hTg, g=g: e.dma_start(out=hT_d[g], in_=hTg[:, :, :]), [hTg], [hT_dep[g]])
                if cfg.sample:
                    xt = xpool.next(); xn = xnpool.next()
                    if l == 0:
                        DMA(lambda e, xt=xt: e.dma_start(out=xt[0:NTOK, :], in_=xs[:, :]), [], [xt])
                    else:
                        DMA(lambda e, xt=xt: e.dma_start(out=xt[0:NTOK, :], in_=xs1_d[:, :]), [xs1_dep], [xt])
                    for half in range(2):
                        py = pR.next()
                        for c in range(8):
                            M(lambda e, py=py, c=c, half=half: e.matmul(py[0:NTOK, 0:512], lhsT=yTs[:, c, 0:NTOK], rhs=Wout[:, c, half * 512:(half + 1) * 512],
                                                                        start=(c == 0), stop=(c == 7)), [yTs, Wout], [py])
                        V(lambda e, py=py, xt=xt, xn=xn, half=half: e.tensor_tensor(out=xn[0:NTOK, half * 512:(half + 1) * 512], in0=py[0:NTOK, 0:512],
                                                                                   in1=xt[0:NTOK, half * 512:(half + 1) * 512], op=ALU.add), [py, xt], [xn])
                    if last:
                        DMA(lambda e, xn=xn: e.dma_start(out=ys[:, :], in_=xn[0:NTOK, :]), [xn], [], final=True)
                    else:
                        DMA(lambda e, xn=xn: e.dma_start(out=xs1_d[:, :], in_=xn[0:NTOK, :]), [xn], [xs1_dep])
                        norm_tile(xn, NTOK, gn, hTs, 0, wkn)
                barrier()
            cur[0] = gst


        RS128 = 128.0 ** -0.5

        def ml_consts(l):
            cw = sb([128, 4, 4], F32, "cw"); cb = sb([128, 4], F32, "cb")
            for c in range(4):
                for j in range(4):
                    DMA(lambda e, c=c, j=j: e.dma_start(out=cw[:, c, j:j + 1], in_=w["ml_conv_w"][l, j, c * 128:(c + 1) * 128].rearrange("(p o) -> p o", o=1)), [], [cw])
                DMA(lambda e, c=c: e.dma_start(out=cb[:, c:c + 1], in_=w["ml_conv_b"][l, c * 128:(c + 1) * 128].rearrange("(p o) -> p o", o=1)), [], [cb])
            return dict(cw=cw, cb=cb, bi=bvec(w["ml_b_i"][l:l + 1, :], 4, "bi"), bf=bvec(w["ml_b_f"][l:l + 1, :], 4, "bf"),
                        mlnorm=bvec(w["ml_norm"][l:l + 1, :], 128, "mlnorm"), skip=bvec(w["ml_skip"][l:l + 1, :], 512, "skip"))

        def ml_weights_alloc():
            return sb([128, 4, 128], BF16, "wq"), sb([128, 4, 128], BF16, "wk"), sb([128, 4, 128], BF16, "wv")

        def ml_weights(l, stg, wts):
            wq, wk_, wv = wts
            for dst, nm in ((wq, "ml_w_q"), (wk_, "ml_w_k"), (wv, "ml_w_v")):
                load_w(dst, 0, 128, (lambda a, n, nm=nm: w[nm][l, :, :, a:a + n].rearrange("h d n -> d h n")), stg, kc=4)
            return wq, wk_, wv

        def ml_work():
            def ones_aug():
                b = sb([128, 4, 128], BF16, "vaug")
                return b
            return dict(
                pR=Rot(lambda: psum(F32, "pR"), 2), pK=psum(F32, "pK"), pBF=psum(BF16, "pBF"), pX1=psum(F32, "pX1"),
                pX2=psum(F32, "pX2"), pX3=psum(F32, "pX3"), pS=psum(F32, "pS"),
                xT=sb([128, 4, 3 + 512], F32, "xTe"), xTb=sb([128, 4, 512], BF16, "xTb"), mcT=sb([128, 4, 512], BF16, "mcT"),
                qTm=sb([128, 4, 512], BF16, "qTm"), kTm=sb([128, 4, 512], BF16, "kTm"),
                acc=Rot(lambda: sb([128, 512], F32, "acc"), 2),
                vaug=Rot(lambda: sb([128, 4, 128], BF16, "vaug"), 2),
                g4=Rot(lambda: sb([128, 4], F32, "g4"), 24), t12=Rot(lambda: sb([128, 12], F32, "t12"), 2),
                dg=Rot(lambda: sb([128, 4, 128], F32, "dg"), 1), Dm=Rot(lambda: sb([128, 4, 128], F32, "Dm"), 2),
                Wt=Rot(lambda: sb([128, 4, 128], F32, "Wt"), 1), Sqk=Rot(lambda: sb([128, 4, 128], BF16, "Sqk"), 2),
                SqkT=Rot(lambda: sb([128, 4, 128], BF16, "SqkT"), 2), P1s=Rot(lambda: sb([128, 512], F32, "P1s"), 1),
                f5=Rot(lambda: sb([128, 512], F32, "f5"), 6), kw=Rot(lambda: sb([128, 4, 128], BF16, "kw"), 2),
                Zm=Rot(lambda: sb([128, 512], BF16, "Zm"), 2))

        def ml_group_prep(Wm, hTg, N, mc_, wts, wk, hoff=0):
            import os
            if int(os.environ.get('KSTEP', '99')) < 0:
                return
            KSUB = int(os.environ.get('KSUB', '99'))
            wq, wk_, wv = wts
            xT, xTb, mcT, qTm, kTm = wk["xT"], wk["xTb"], wk["mcT"], wk["qTm"], wk["kTm"]
            cw, cb = mc_["cw"], mc_["cb"]
            for c in range(4):
                pf = wk["pR"].next()
                feat_proj(pf[:, 0:N], pf, Wm, Wm, c * 128, 128, hTg, hTg, hoff, N)
                KVAR = os.environ.get('KVAR', '')
                if KVAR != 'noact':
                    A(lambda e, pf=pf, c=c: e.activation(out=xT[:, c, 3:3 + N], in_=pf[:, 0:N], func=AF.Copy), [pf], [xT])
                if KVAR != 'nodve':
                    V(lambda e, pf=pf, c=c: e.tensor_copy(out=xTb[:, c, 0:N], in_=pf[:, 0:N]), [pf], [xTb])
                if KSUB < 2:
                    continue
                acc = wk["acc"].next()
                V(lambda e, c=c, acc=acc: e.tensor_scalar(out=acc[:, 0:N], in0=xT[:, c, 0:N], scalar1=cw[:, c, 0:1], scalar2=cb[:, c:c + 1],
                                                          op0=ALU.mult, op1=ALU.add), [xT, cw, cb], [acc])
                for j in range(1, 4):
                    V(lambda e, c=c, j=j, acc=acc: e.scalar_tensor_tensor(out=acc[:, 0:N], in0=xT[:, c, j:j + N], scalar=cw[:, c, j:j + 1], in1=acc[:, 0:N],
                                                                         op0=ALU.mult, op1=ALU.add), [xT, cw, acc], [acc])
                if KSUB < 3:
                    continue
                A(lambda e, c=c, acc=acc: e.activation(out=mcT[:, c, 0:N], in_=acc[:, 0:N], func=AF.Silu), [acc], [mcT])
            if KSUB < 4:
                return
            for h in range(4):
                pf = wk["pR"].next()
                M(lambda e, pf=pf, h=h: e.matmul(pf[:, 0:N], lhsT=wq[:, h, :], rhs=mcT[:, h, 0:N], start=True, stop=True), [wq, mcT], [pf])
                A(lambda e, pf=pf, h=h: e.activation(out=qTm[:, h, 0:N], in_=pf[:, 0:N], func=AF.Copy), [pf], [qTm])
                pf = wk["pR"].next()
                M(lambda e, pf=pf, h=h: e.matmul(pf[:, 0:N], lhsT=wk_[:, h, :], rhs=mcT[:, h, 0:N], start=True, stop=True), [wk_, mcT], [pf])
                A(lambda e, pf=pf, h=h: e.activation(out=kTm[:, h, 0:N], in_=pf[:, 0:N], func=AF.Identity, scale=RS128), [pf], [kTm])

        def ml_carry(N, wk):
            xT = wk["xT"]
            V(lambda e: e.tensor_copy(out=xT[:, :, 0:3], in_=xT[:, :, N:N + 3]), [xT], [xT])

        def ml_tile(T, t0, Wm, hTg, mc_, wts, st, ZmT, wk, hoff=None, zoff=None):
            import os
            KSTEP = int(os.environ.get('KSTEP', '99'))
            if KSTEP < 1:
                return
            hc = t0 if hoff is None else hoff
            zc = t0 if zoff is None else zoff
            wq, wk_, wv = wts
            CT, CTb, nT, nb, mprev = st["CT"], st["CTb"], st["nT"], st["nb"], st["mprev"]
            xTb, mcT, qTm, kTm = wk["xTb"], wk["mcT"], wk["qTm"], wk["kTm"]
            pK, pBF, pX1, pX2, pX3, pS = wk["pK"], wk["pBF"], wk["pX1"], wk["pX2"], wk["pX3"], wk["pS"]
            g4 = lambda: wk["g4"].next()
            SELL = SELL128 if T == 128 else SELL8
            v3 = lambda ap: ap.rearrange("p (h v) -> p h v", h=4)
            pif = wk["pR"].next()
            tok_proj(pif[0:T, 0:8], pif, hTg, hTg, hc, T, Wm, Wm, 512, 8)
            uf, ig = g4(), g4()
            V(lambda e: e.tensor_tensor(out=uf[0:T, :], in0=pif[0:T, 4:8], in1=mc_["bf"][0:T, :], op=ALU.add), [pif, mc_["bf"]], [uf])
            V(lambda e: e.tensor_tensor(out=ig[0:T, :], in0=pif[0:T, 0:4], in1=mc_["bi"][0:T, :], op=ALU.add), [pif, mc_["bi"]], [ig])
            spf = g4()
            A(lambda e: e.activation(out=spf[0:T, :], in_=uf[0:T, :], func=AF.Exp, scale=-1.0), [uf], [spf])
            A(lambda e: e.activation(out=spf[0:T, :], in_=spf[0:T, :], func=AF.Ln, bias=1.0), [spf], [spf])
            for h in range(4):
                M(lambda e, h=h: e.matmul(pK[0:T, h * 128:(h + 1) * 128], lhsT=mcT[:, h, t0:t0 + T], rhs=wk_[:, h, :], start=True, stop=True), [mcT, wk_], [pK])
            pv = wk["pR"].next()
            for h in range(4):
                M(lambda e, h=h: e.matmul(pv[0:T, h * 128:(h + 1) * 128], lhsT=xTb[:, h, t0:t0 + T], rhs=wv[:, h, :], start=True, stop=True), [xTb, wv], [pv])
            vaug = wk["vaug"].next()
            A(lambda e: e.activation(out=vaug[0:T, :, :], in_=v3(pv[0:T, :]), func=AF.Copy), [pv], [vaug])
            pmc = pBF[:, 0:512].rearrange("p (c f) -> p c f", c=4)
            pTr = pBF[:, 512:1024].rearrange("p (h t) -> p h t", h=4)
            for c in range(4):
                M(lambda e, c=c: e.transpose(out=pmc[0:T, c, :], in_=mcT[:, c, t0:t0 + T], identity=IDB[:, :]), [mcT, cstb], [pBF])
            mcs = wk["f5"].next()
            V(lambda e: e.tensor_tensor(out=mcs[0:T, :], in0=pBF[0:T, 0:512], in1=mc_["skip"][0:T, :], op=ALU.mult), [pBF, mc_["skip"]], [mcs])
            if KSTEP < 3:
                return
            M(lambda e: e.matmul(pS[0:T, 0:4], lhsT=TRIN[0:T, 0:T], rhs=spf[0:T, 0:4], start=True, stop=True), [spf, cst], [pS])
            Fs, r = g4(), g4()
            V(lambda e: e.tensor_copy(out=Fs[0:T, :], in_=pS[0:T, 0:4]), [pS], [Fs])
            V(lambda e: e.tensor_tensor(out=r[0:T, :], in0=ig[0:T, :], in1=Fs[0:T, :], op=ALU.subtract), [ig, Fs], [r])
            if KSTEP < 4:
                return
            dg = wk["dg"].next()
            V(lambda e: e.tensor_tensor(out=dg[0:T, :, 0:T], in0=IDF[0:T, 0:T].unsqueeze(1).to_broadcast([T, 4, T]),
                                        in1=r[0:T, :].unsqueeze(2).to_broadcast([T, 4, T]), op=ALU.mult), [cst, r], [dg])
            pRb = pX1[:, :].rearrange("p (h t) -> p h t", h=4)
            for h in range(4):
                M(lambda e, h=h: e.matmul(pRb[0:T, h, 0:T], lhsT=ONESF[0:T, 0:T], rhs=dg[0:T, h, 0:T], start=True, stop=True), [dg, cst], [pX1])
            if KSTEP < 5:
                return
            Dm = wk["Dm"].next()
            for h in range(4):
                V(lambda e, h=h: e.scalar_tensor_tensor(out=Dm[0:T, h, 0:T], in0=pRb[0:T, h, 0:T], scalar=Fs[0:T, h:h + 1], in1=NEGM[0:T, 0:T],
                                                        op0=ALU.add, op1=ALU.add), [pX1, Fs, cst], [Dm])
            mx, lin, t12, negm, win, enm = g4(), g4(), wk["t12"].next(), g4(), g4(), g4()
            V(lambda e: e.tensor_reduce(out=mx[0:T, :], in_=Dm[0:T, :, 0:T], axis=AX.X, op=ALU.max), [Dm], [mx])
            V(lambda e: e.tensor_tensor(out=lin[0:T, :], in0=Fs[0:T, :], in1=mprev[0:T, :], op=ALU.add), [Fs, mprev], [lin])
            V(lambda e: e.tensor_tensor(out=t12[0:T, 8:12], in0=lin[0:T, :], in1=mx[0:T, :], op=ALU.max), [lin, mx], [t12])
            V(lambda e: e.tensor_scalar(out=negm[0:T, :], in0=t12[0:T, 8:12], scalar1=-1.0, scalar2=None, op0=ALU.mult), [t12], [negm])
            V(lambda e: e.tensor_tensor(out=t12[0:T, 4:8], in0=lin[0:T, :], in1=t12[0:T, 8:12], op=ALU.subtract), [lin, t12], [t12])
            V(lambda e: e.tensor_tensor(out=t12[0:T, 0:4], in0=Fs[0:T, :], in1=t12[0:T, 8:12], op=ALU.subtract), [Fs, t12], [t12])
            A(lambda e: e.activation(out=win[0:T, :], in_=t12[0:T, 4:8], func=AF.Exp), [t12], [win])
            A(lambda e: e.activation(out=enm[0:T, :], in_=negm[0:T, :], func=AF.Exp), [negm], [enm])
            if KSTEP < 7:
                return
            Wt = wk["Wt"].next()
            for h in range(4):
                A(lambda e, h=h: e.activation(out=Wt[0:T, h, 0:T], in_=Dm[0:T, h, 0:T], func=AF.Exp, bias=negm[0:T, h:h + 1]), [Dm, negm], [Wt])
            if KSTEP < 8:
                return
            pQK = pX1[:, :].rearrange("p (h t) -> p h t", h=4)
            for h in range(4):
                M(lambda e, h=h: e.matmul(pQK[0:T, h, 0:T], lhsT=qTm[:, h, t0:t0 + T], rhs=kTm[:, h, t0:t0 + T], start=True, stop=True), [qTm, kTm], [pX1])
            Sqk = wk["Sqk"].next()
            V(lambda e: e.tensor_tensor(out=Sqk[0:T, :, 0:T], in0=pQK[0:T, :, 0:T], in1=Wt[0:T, :, 0:T], op=ALU.mult), [pX1, Wt], [Sqk])
            rs = g4()
            V(lambda e: e.tensor_reduce(out=rs[0:T, :], in_=Sqk[0:T, :, 0:T], axis=AX.X, op=ALU.add), [Sqk], [rs])
            for h in range(4):
                M(lambda e, h=h: e.transpose(out=pTr[0:T, h, 0:T], in_=Sqk[0:T, h, 0:T], identity=IDB[0:T, 0:T]), [Sqk, cstb], [pBF])
            SqkT = wk["SqkT"].next()
            A(lambda e: e.activation(out=SqkT[0:T, :, 0:T], in_=pTr[0:T, :, 0:T], func=AF.Copy), [pBF], [SqkT])
            if KSTEP < 10:
                return
            for h in range(4):
                M(lambda e, h=h: e.matmul(pX2[0:T, h * 128:(h + 1) * 128], lhsT=SqkT[0:T, h, 0:T], rhs=vaug[0:T, h, :], start=True, stop=True), [SqkT, vaug], [pX2])
            for h in range(4):
                M(lambda e, h=h: e.matmul(pX3[0:T, h * 128:(h + 1) * 128], lhsT=qTm[:, h, t0:t0 + T], rhs=CTb[:, h, :], start=True, stop=True), [qTm, CTb], [pX3])
                M(lambda e, h=h: e.matmul(pS[0:T, 4 + h:5 + h], lhsT=qTm[:, h, t0:t0 + T], rhs=nb[:, h:h + 1], start=True, stop=True), [qTm, nb], [pS])
            P1s = wk["P1s"].next()
            A(lambda e: e.activation(out=P1s[0:T, :], in_=pX2[0:T, :], func=AF.Copy), [pX2], [P1s])
            comb = wk["f5"].next()
            for h in range(4):
                V(lambda e, h=h: e.scalar_tensor_tensor(out=comb[0:T, h * 128:(h + 1) * 128], in0=pX3[0:T, h * 128:(h + 1) * 128], scalar=win[0:T, h:h + 1],
                                                        in1=P1s[0:T, h * 128:(h + 1) * 128], op0=ALU.mult, op1=ALU.add), [pX3, win, P1s], [comb])
            den, d2, rd = g4(), g4(), g4()
            V(lambda e: e.tensor_tensor(out=den[0:T, :], in0=pS[0:T, 4:8], in1=win[0:T, :], op=ALU.mult), [pS, win], [den])
            V(lambda e: e.tensor_tensor(out=den[0:T, :], in0=den[0:T, :], in1=rs[0:T, :], op=ALU.add), [den, rs], [den])
            V(lambda e: e.tensor_scalar(out=d2[0:T, :], in0=den[0:T, :], scalar1=-1.0, scalar2=None, op0=ALU.mult), [den], [d2])
            V(lambda e: e.tensor_tensor(out=d2[0:T, :], in0=d2[0:T, :], in1=den[0:T, :], op=ALU.max), [d2, den], [d2])
            V(lambda e: e.tensor_tensor(out=d2[0:T, :], in0=d2[0:T, :], in1=enm[0:T, :], op=ALU.max), [d2, enm], [d2])
            V(lambda e: e.reciprocal(out=rd[0:T, :], in_=d2[0:T, :]), [d2], [rd])
            if KSTEP < 12:
                return
            po = wk["pR"].next()
            tok_proj(po[0:T, 0:512], po, hTg, hTg, hc, T, Wm, Wm, 520, 512)
            sgo = wk["f5"].next()
            A(lambda e: e.activation(out=sgo[0:T, :], in_=po[0:T, :], func=AF.Sigmoid), [po], [sgo])
            pz = wk["pR"].next()
            tok_proj(pz[0:T, 0:512], pz, hTg, hTg, hc, T, Wm, Wm, 1032, 512)
            szm = wk["f5"].next()
            A(lambda e: e.activation(out=szm[0:T, :], in_=pz[0:T, :], func=AF.Silu), [pz], [szm])
            t2 = wk["f5"].next()
            V(lambda e: e.tensor_tensor(out=v3(t2[0:T, :]), in0=v3(comb[0:T, :]), in1=rd[0:T, :].unsqueeze(2).to_broadcast([T, 4, 128]), op=ALU.mult), [comb, rd], [t2])
            G(lambda e: e.tensor_tensor(out=t2[0:T, :], in0=t2[0:T, :], in1=sgo[0:T, :], op=ALU.mult), [t2, sgo], [t2])
            sq = wk["f5"].next()
            ss4, ln4, rs4 = g4(), g4(), g4()
            A(lambda e: e.activation(out=sq[0:T, :], in_=t2[0:T, :], func=AF.Square), [t2], [sq])
            V(lambda e: e.tensor_reduce(out=ss4[0:T, :], in_=v3(sq[0:T, :]), axis=AX.X, op=ALU.add), [sq], [ss4])
            rstd_from_ss(ss4[0:T, :], ss4, rs4[0:T, :], rs4, ln4[0:T, :], ln4, 1.0 / 128)
            V(lambda e: e.tensor_tensor(out=v3(t2[0:T, :]), in0=v3(t2[0:T, :]), in1=rs4[0:T, :].unsqueeze(2).to_broadcast([T, 4, 128]), op=ALU.mult), [t2, rs4], [t2])
            G(lambda e: e.tensor_tensor(out=v3(t2[0:T, :]), in0=v3(t2[0:T, :]), in1=mc_["mlnorm"][0:T, :].unsqueeze(1).to_broadcast([T, 4, 128]), op=ALU.mult),
              [t2, mc_["mlnorm"]], [t2])
            G(lambda e: e.tensor_tensor(out=t2[0:T, :], in0=t2[0:T, :], in1=mcs[0:T, :], op=ALU.add), [t2, mcs], [t2])
            Zm = wk["Zm"].next()
            V(lambda e: e.tensor_tensor(out=Zm[0:T, :], in0=t2[0:T, :], in1=szm[0:T, :], op=ALU.mult), [t2, szm], [Zm])
            for k4 in range(4):
                M(lambda e, k4=k4: e.transpose(out=pTr[:, k4, 0:T], in_=Zm[0:T, k4 * 128:(k4 + 1) * 128], identity=IDB[0:T, 0:T]), [Zm, cstb], [pBF])
            A(lambda e: e.activation(out=ZmT[:, :, zc:zc + T], in_=pTr[:, :, 0:T], func=AF.Copy), [pBF], [ZmT])
            if KSTEP < 14:
                return
            M(lambda e: e.matmul(pS[:, 8:20], lhsT=SELL[0:T, :], rhs=t12[0:T, 0:12], start=True, stop=True), [t12, cst], [pS])
            wend, dec = g4(), g4()
            V(lambda e: e.tensor_tensor(out=wend[0:T, :], in0=r[0:T, :], in1=pS[0:T, 8:12], op=ALU.add), [r, pS], [wend])
            A(lambda e: e.activation(out=wend[0:T, :], in_=wend[0:T, :], func=AF.Exp), [wend], [wend])
            A(lambda e: e.activation(out=dec[:, :], in_=pS[:, 12:16], func=AF.Exp), [pS], [dec])
            V(lambda e: e.tensor_copy(out=mprev[:, :], in_=pS[:, 16:20]), [pS], [mprev])
            kw = wk["kw"].next()
            V(lambda e: e.scalar_tensor_tensor(out=kw[0:T, :, :], in0=v3(pK[0:T, :]), scalar=RS128, in1=wend[0:T, :].unsqueeze(2).to_broadcast([T, 4, 128]),
                                               op0=ALU.mult, op1=ALU.mult), [pK, wend], [kw])
            for h in range(4):
                M(lambda e, h=h: e.matmul(pX2[:, h * 128:(h + 1) * 128], lhsT=kw[0:T, h, :], rhs=vaug[0:T, h, :], start=True, stop=True), [kw, vaug], [pX2])
                M(lambda e, h=h: e.matmul(pS[:, 20 + h:21 + h], lhsT=kw[0:T, h, :], rhs=ONESB[0:T, 0:1], start=True, stop=True), [kw, cstb], [pS])
            for h in range(4):
                V(lambda e, h=h: e.scalar_tensor_tensor(out=CT[:, h, :], in0=CT[:, h, :], scalar=dec[:, h:h + 1], in1=pX2[:, h * 128:(h + 1) * 128],
                                                        op0=ALU.mult, op1=ALU.add), [CT, dec, pX2], [CT])
            V(lambda e: e.tensor_tensor(out=nT[:, :], in0=nT[:, :], in1=dec[:, :], op=ALU.mult), [nT, dec], [nT])
            V(lambda e: e.tensor_tensor(out=nT[:, :], in0=nT[:, :], in1=pS[:, 20:24], op=ALU.add), [nT, pS], [nT])
            G(lambda e: e.tensor_copy(out=CTb[:, :, :], in_=CT[:, :, :]), [CT], [CTb])
            G(lambda e: e.tensor_copy(out=nb[:, :], in_=nT[:, :]), [nT], [nb])

        def ml_state_out(st, wk, o_mc, o_mn, o_mm, o_conv, fin=True):
            CT, nT, mprev, xT = st["CT"], st["nT"], st["mprev"], wk["xT"]
            pX2, pS, pX3 = wk["pX2"], wk["pS"], wk["pX3"]
            for h in range(4):
                M(lambda e, h=h: e.transpose(out=pX2[:, h * 128:(h + 1) * 128], in_=CT[:, h, :], identity=IDF[:, :]), [CT, cst], [pX2])
            co = wk["f5"].next()
            V(lambda e: e.tensor_copy(out=co[:, :], in_=pX2[:, :]), [pX2], [co])
            DMA(lambda e: e.dma_start(out=o_mc.rearrange("h v d -> v h d"), in_=co[:, :].rearrange("p (h d) -> p h d", h=4)), [co], [], final=fin)
            M(lambda e: e.transpose(out=pS[0:4, 0:128], in_=nT[:, 0:4], identity=IDF[:, :]), [nT, cst], [pS])
            no = wk["f5"].next()
            V(lambda e: e.tensor_copy(out=no[0:4, 0:128], in_=pS[0:4, 0:128]), [pS], [no])
            DMA(lambda e: e.dma_start(out=o_mn, in_=no[0:4, 0:128]), [no], [], final=fin)
            DMA(lambda e: e.dma_start(out=o_mm, in_=mprev[0:1, 0:4]), [mprev], [], final=fin)
            for c in range(4):
                M(lambda e, c=c: e.transpose(out=pX3[0:3, c * 128:(c + 1) * 128], in_=xT[:, c, 0:3], identity=IDF[:, :]), [xT, cst], [pX3])
            cvo = wk["f5"].next()
            V(lambda e: e.tensor_copy(out=cvo[0:3, :], in_=pX3[0:3, :]), [pX3], [cvo])
            DMA(lambda e: e.dma_start(out=o_conv, in_=cvo[0:3, :]), [cvo], [], final=fin)

        def ml_state_new(zero=True):
            st = dict(CT=sb([128, 4, 128], F32, "CT"), CTb=sb([128, 4, 128], BF16, "CTb"), nT=sb([128, 4], F32, "nT"),
                      nb=sb([128, 4], BF16, "nb"), mprev=sb([128, 4], F32, "mprev"))
            if zero:
                for k_, b in st.items():
                    nd = len(b.t.shape)
                    V(lambda e, b=b, nd=nd: e.memset(b[(slice(None),) * nd], 0.0), [], [b])
            return st

        def phase_M(l, first):
            with contextlib.ExitStack() as pst:
                cur[0] = pst
                Wm = sb([128, 8, 1544], BF16, "Wm"); Wgt = sb([128, 8, D], BF16, "Wgt"); Wo = sb([128, 4, D], BF16, "Wo")
                wts = ml_weights_alloc()
                with WLoad() as stg:
                    load_w(Wm, 0, 1544, w_in_src(l, C_MX), stg)
                    load_gate_w(Wgt, l, 1, stg)
                    load_outw(Wo, "ml_w_out", l, stg)
                    ml_weights(l, stg, wts)
                mc_ = ml_consts(l)
                wk = ml_work(); bw = branch_work(wk["pR"])
                st = ml_state_new(True)
                V(lambda e: e.memset(wk["xT"][:, :, 0:3], 0.0), [], [wk["xT"]])
                hpool = Rot(lambda: sb([128, 8, 512], BF16, "hTg"), 2)
                ypool = Rot(lambda: sb([128, 8, 512], BF16, "yTg"), 2)
                zpool = Rot(lambda: sb([128, 4, 512], BF16, "ZmT"), 2)
                for g in range(NG):
                    hTg = hpool.next(); yTg = ypool.next(); ZmT = zpool.next()
                    DMA(lambda e, hTg=hTg, g=g: e.dma_start(out=hTg[:, :, :], in_=hT_d[g]), [hT_dep[g]], [hTg])
                    if not first:
                        DMA(lambda e, yTg=yTg, g=g: e.dma_start(out=yTg[:, :, :], in_=yT_d[g]), [yT_dep[g]], [yTg])
                    ml_group_prep(Wm, hTg, 512, mc_, wts, wk)
                    for j in range(4):
                        ml_tile(128, j * 128, Wm, hTg, mc_, wts, st, ZmT, wk)
                    ml_carry(512, wk)
                    terms = [((lambda c, k4=k4: Wo[:, k4, c * 128:(c + 1) * 128]), ZmT[:, k4, :]) for k4 in range(4)]
                    if STOP >= 4:
                        branch_out(terms, ZmT, Wo, Wo, Wgt, Wgt, hTg, 512, yTg, first, bw)
                    DMA(lambda e, yTg=yTg, g=g: e.dma_start(out=yT_d[g], in_=yTg[:, :, :]), [yTg], [yT_dep[g]])
                if int(os.environ.get('KSTEP', '99')) >= 15:
                    ml_state_out(st, wk, p_mc[l], p_mn[l], p_mm[l:l + 1, :], p_conv[l])
                if cfg.sample:
                    ZmTs = zpool.next()
                    ctok = sb([128, 4, 128], F32, "ctok"); ntok = sb([4, 128], F32, "ntok"); cvtok = sb([3, 512], F32, "cvtok")
                    for sq_ in range(NS):
                        DMA(lambda e, sq_=sq_: e.dma_start(out=ctok[:, :, :], in_=sc_in[l, sq_].rearrange("h v d -> v h d")), [], [ctok])
                        DMA(lambda e, sq_=sq_: e.dma_start(out=ntok[:, :], in_=sn_in[l, sq_]), [], [ntok])
                        DMA(lambda e, sq_=sq_: e.dma_start(out=cvtok[:, :], in_=scv_in[l, sq_]), [], [cvtok])
                        DMA(lambda e, sq_=sq_: e.dma_start(out=st["mprev"][:, :], in_=sm_in[l, sq_:sq_ + 1, :].partition_broadcast(128)), [], [st["mprev"]])
                        pX2, pS, pX3 = wk["pX2"], wk["pS"], wk["pX3"]
                        for h in range(4):
                            M(lambda e, h=h: e.transpose(out=pX2[:, h * 128:(h + 1) * 128], in_=ctok[:, h, :], identity=IDF[:, :]), [ctok, cst], [pX2])
                        V(lambda e: e.tensor_copy(out=st["CT"][:, :, :], in_=pX2[:, :].rearrange("p (h v) -> p h v", h=4)), [pX2], [st["CT"]])
                        G(lambda e: e.tensor_copy(out=st["CTb"][:, :, :], in_=st["CT"][:, :, :]), [st["CT"]], [st["CTb"]])
                        M(lambda e: e.transpose(out=pS[:, 24:28], in_=ntok[0:4, :], identity=IDF[0:4, 0:4]), [ntok, cst], [pS])
                        V(lambda e: e.tensor_copy(out=st["nT"][:, :], in_=pS[:, 24:28]), [pS], [st["nT"]])
                        G(lambda e: e.tensor_copy(out=st["nb"][:, :], in_=st["nT"][:, :]), [st["nT"]], [st["nb"]])
                        for c in range(4):
                            M(lambda e, c=c: e.transpose(out=pX3[:, c * 4:c * 4 + 3], in_=cvtok[0:3, c * 128:(c + 1) * 128], identity=IDF[0:3, 0:3]), [cvtok, cst], [pX3])
                        V(lambda e: e.tensor_copy(out=wk["xT"][:, :, 0:3], in_=pX3[:, 0:16].rearrange("p (c j) -> p c j", c=4)[:, :, 0:3]), [pX3], [wk["xT"]])
                        ml_group_prep(Wm, hTs, TS, mc_, wts, wk, hoff=sq_ * TS)
                        ml_tile(TS, 0, Wm, hTs, mc_, wts, st, ZmTs, wk, hoff=sq_ * TS, zoff=sq_ * TS)
                        ml_carry(TS, wk)
                        ml_state_out(st, wk, s_mc[l, sq_], s_mn[l, sq_], s_mm[l, sq_:sq_ + 1, :], s_conv[l, sq_])
                    terms = [((lambda c, k4=k4: Wo[:, k4, c * 128:(c + 1) * 128]), ZmTs[:, k4, 0:NTOK]) for k4 in range(4)]
                    branch_out(terms, ZmTs, Wo, Wo, Wgt, Wgt, hTs, NTOK, yTs, first, bw)
                barrier()
            cur[0] = gst


        def qk_norm(T, pq, nrm_bc, scale, wk, out_f32):
            sq = wk["f5"].next(); ss8, ln8, rs8 = wk["g8"].next(), wk["g8"].next(), wk["g8"].next()
            v8 = lambda ap: ap.rearrange("p (h d) -> p h d", h=8)
            A(lambda e: e.activation(out=sq[0:T, :], in_=pq[0:T, :], func=AF.Square), [pq], [sq])
            V(lambda e: e.tensor_reduce(out=ss8[0:T, :], in_=v8(sq[0:T, :]), axis=AX.X, op=ALU.add), [sq], [ss8])
            rstd_from_ss(ss8[0:T, :], ss8, rs8[0:T, :], rs8, ln8[0:T, :], ln8, 1.0 / 64)
            V(lambda e: e.tensor_tensor(out=v8(out_f32[0:T, :]), in0=v8(pq[0:T, :]), in1=rs8[0:T, :].unsqueeze(2).to_broadcast([T, 8, 64]), op=ALU.mult),
              [pq, rs8], [out_f32])
            V(lambda e: e.scalar_tensor_tensor(out=v8(out_f32[0:T, :]), in0=v8(out_f32[0:T, :]), scalar=scale,
                                               in1=nrm_bc[0:T, :].unsqueeze(1).to_broadcast([T, 8, 64]), op0=ALU.mult, op1=ALU.mult), [out_f32, nrm_bc], [out_f32])


        OTs = sb([64, 8, NTOK], F32, "OTs")
        NBP = NPG // 2

        def sample_moba(l, Wb, qn_bc, kn_bc):
            wk = dict(f5=Rot(lambda: sb([128, 512], F32, "f5s"), 8), g8=Rot(lambda: sb([128, 8], F32, "g8s"), 12),
                      b5=Rot(lambda: sb([128, 512], BF16, "b5s"), 6))
            pR = Rot(lambda: psum(F32, "pRs"), 3); pT = Rot(lambda: psum(BF16, "pTs"), 2); pBS = psum(F32, "pBSs")
            pOV = psum(F32, "pOV"); pMk = psum(F32, "pMk")
            id32f = sb([32, 2048], F32, "id32f"); id32 = sb([32, 32, 64], BF16, "id32")
            DMA(lambda e: e.dma_start(out=id32f[:, :], in_=consts_d[0:32, K_ID32:K_ID32 + 2048]), [], [id32f])
            V(lambda e: e.tensor_copy(out=id32[:, :, :], in_=id32f[:, :].rearrange("p (n t) -> p n t", n=32)), [id32f], [id32])
            ptb = sb([128, NPG], I32, "ptb"); ptf = sb([128, NPG], F32, "ptf"); idxi = sb([128, NPG], I32, "idxi")
            S_all = sb([128, NPG, 64], F32, "S_all"); Pm = sb([128, NPG, 64], BF16, "Pm")
            QBD = zsb([128, 4, 2, TS], BF16, "QBD"); KTn = sb([128, 4, TS], BF16, "KTn")
            KTt = Rot(lambda: sb([128, 4, 128], BF16, "KTt"), 4)
            S_dep = [B(None) for _ in range(NPG)]
            bs_sb = sb([64, max(NBP, 8)], F32, "bs_sb"); m8 = sb([64, 8], F32, "m8s"); sel = sb([64, 32], F32, "sel"); selT = sb([32, 64], F32, "selT")
            Dexp = sb([32, 32, 64], BF16, "Dexp"); Pn = sb([TS, 8, TS], BF16, "Pn"); Pnf = sb([TS, 8, TS], F32, "Pnf")
            tmpo = sb([64, 8, 64], F32, "tmpo"); OD = sb([64, 64], F32, "OD"); rdn = sb([64, 1], F32, "rdn")
            ck2 = ck[l][:, :]; cv2 = cv[l][:, :]
            idx1 = Rot(lambda: sb([128, 1], I32, "idx1"), 4)
            for sq_ in range(NS):
                tsl = slice(sq_ * TS, (sq_ + 1) * TS)
                pv = pR.next(); tok_proj(pv[0:TS, 0:512], pv, hTs, hTs, sq_ * TS, TS, Wb, Wb, 1024, 512)
                vf = wk["f5"].next()
                A(lambda e, pv=pv, vf=vf: e.activation(out=vf[0:TS, :], in_=pv[0:TS, 0:512], func=AF.Copy), [pv], [vf])
                DMA(lambda e, vf=vf, tsl=tsl: e.dma_start(out=s_v[l, tsl, :], in_=vf[0:TS, :]), [vf], [], final=True)
                vbn = wk["b5"].next()
                G(lambda e, vf=vf, vbn=vbn: e.tensor_copy(out=vbn[0:TS, :], in_=vf[0:TS, :]), [vf], [vbn])
                pk = pR.next(); tok_proj(pk[0:TS, 0:512], pk, hTs, hTs, sq_ * TS, TS, Wb, Wb, 512, 512)
                kf = wk["f5"].next(); qk_norm(TS, pk, kn_bc, 1.0, wk, kf)
                DMA(lambda e, kf=kf, tsl=tsl: e.dma_start(out=s_k[l, tsl, :], in_=kf[0:TS, :]), [kf], [], final=True)
                kbn = wk["b5"].next()
                G(lambda e, kf=kf, kbn=kbn: e.tensor_copy(out=kbn[0:TS, :], in_=kf[0:TS, :]), [kf], [kbn])
                pq = pR.next(); tok_proj(pq[0:TS, 0:512], pq, hTs, hTs, sq_ * TS, TS, Wb, Wb, 0, 512)
                qf = wk["f5"].next(); qk_norm(TS, pq, qn_bc, 0.125, wk, qf)
                qbn = wk["b5"].next()
                G(lambda e, qf=qf, qbn=qbn: e.tensor_copy(out=qbn[0:TS, :], in_=qf[0:TS, :]), [qf], [qbn])
                pt_ = pT.next(); ptv = pt_[:, 0:4 * TS].rearrange("p (c t) -> p c t", c=4)
                for c in range(4):
                    M(lambda e, c=c, qbn=qbn, ptv=ptv: e.transpose(out=ptv[:, c, :], in_=qbn[0:TS, c * 128:(c + 1) * 128], identity=IDB[0:TS, 0:TS]), [qbn, cstb], [pt_])
                A(lambda e, ptv=ptv: e.activation(out=QBD[0:64, :, 0, :], in_=ptv[0:64, :, :], func=AF.Copy), [pt_], [QBD])
                V(lambda e, ptv=ptv: e.tensor_copy(out=QBD[64:128, :, 1, :], in_=ptv[64:128, :, :]), [pt_], [QBD])
                pt2 = pT.next(); ptv2 = pt2[:, 0:4 * TS].rearrange("p (c t) -> p c t", c=4)
                for c in range(4):
                    M(lambda e, c=c, kbn=kbn, ptv2=ptv2: e.transpose(out=ptv2[:, c, :], in_=kbn[0:TS, c * 128:(c + 1) * 128], identity=IDB[0:TS, 0:TS]), [kbn, cstb], [pt2])
                A(lambda e, ptv2=ptv2: e.activation(out=KTn[:, :, :], in_=ptv2[:, :, :], func=AF.Copy), [pt2], [KTn])
                DMA(lambda e, sq_=sq_: e.dma_start(out=ptb[:, :], in_=pt[sq_:sq_ + 1, :].partition_broadcast(128)), [], [ptb])
                V(lambda e: e.tensor_copy(out=ptf[:, :], in_=ptb[:, :]), [ptb], [ptf])
                V(lambda e: e.tensor_scalar(out=ptf[:, :], in0=ptf[:, :], scalar1=128.0, scalar2=IOTA, op0=ALU.mult, op1=ALU.add), [ptf, cst], [ptf])
                V(lambda e: e.tensor_copy(out=idxi[:, :], in_=ptf[:, :]), [ptf], [idxi])
                def stA(j):
                    Kt = wk["f5"].next(); ix = idx1.next()
                    G(lambda e, ix=ix, j=j: e.tensor_copy(out=ix[:, :], in_=idxi[:, j:j + 1]), [idxi], [ix])
                    DMA(lambda e, Kt=Kt, ix=ix: e.indirect_dma_start(out=Kt[:, :], out_offset=None, in_=ck2,
                                                                     in_offset=bass.IndirectOffsetOnAxis(ap=ix[:, :], axis=0)), [ix], [Kt], q="pool")
                    Kb = wk["b5"].next()
                    V(lambda e, Kt=Kt, Kb=Kb: e.tensor_copy(out=Kb[:, :], in_=Kt[:, :]), [Kt], [Kb])
                    pk_ = pT.next(); pkv = pk_[:, 0:512].rearrange("p (c t) -> p c t", c=4)
                    for c in range(4):
                        M(lambda e, c=c, Kb=Kb, pkv=pkv: e.transpose(out=pkv[:, c, :], in_=Kb[:, c * 128:(c + 1) * 128], identity=IDB[:, :]), [Kb, cstb], [pk_])
                    kt_ = KTt.next()
                    A(lambda e, pkv=pkv, kt_=kt_: e.activation(out=kt_[:, :, :], in_=pkv[:, :, :], func=AF.Copy), [pk_], [kt_])
                    return kt_

                def stB(j, kt_):
                    pST = pR.next()
                    for c in range(4):
                        M(lambda e, c=c, kt_=kt_, pST=pST: e.matmul(pST[:, c * 2 * TS:(c + 1) * 2 * TS], lhsT=kt_[:, c, :], rhs=QBD[:, c, :, :].rearrange("p a t -> p (a t)"),
                                                                    start=True, stop=True), [kt_, QBD], [pST])
                    wr = [S_dep[j]] + ([S_all] if j == 0 else [])
                    V(lambda e, j=j, pST=pST: e.tensor_copy(out=S_all[:, j, :], in_=pST[:, 0:64]), [pST], wr)

                def stC(j):
                    M(lambda e, j=j: e.matmul(pBS[0:64, j // 2:j // 2 + 1], lhsT=S_all[:, j, :], rhs=ONESF[:, 0:1], start=(j % 2 == 0), stop=(j % 2 == 1)),
                      [S_dep[j], cst], [pBS])

                kts = {0: stA(0)}
                if NPG > 1:
                    kts[1] = stA(1)
                for j in range(NPG):
                    stB(j, kts.pop(j))
                    if j + 2 < NPG:
                        kts[j + 2] = stA(j + 2)
                    stC(j)
                if NBP < 8:
                    V(lambda e: e.memset(bs_sb[:, :], NEG), [], [bs_sb])
                V(lambda e: e.tensor_copy(out=bs_sb[:, 0:NBP], in_=pBS[0:64, 0:NBP]), [pBS], [bs_sb])
                V(lambda e: e.max(out=m8[:, :], in_=bs_sb[:, :]), [bs_sb], [m8])
                V(lambda e: e.tensor_tensor(out=sel[:, 0:NBP], in0=bs_sb[:, 0:NBP], in1=m8[:, 2:3].to_broadcast([64, NBP]), op=ALU.is_ge), [bs_sb, m8], [sel])
                M(lambda e: e.transpose(out=pMk[0:NBP, 0:64], in_=sel[:, 0:NBP], identity=IDF[0:64, 0:64]), [sel, cst], [pMk])
                V(lambda e: e.tensor_copy(out=selT[0:NBP, :], in_=pMk[0:NBP, 0:64]), [pMk], [selT])
                V(lambda e: e.tensor_tensor(out=Dexp[0:NBP, 0:NBP, :], in0=id32[0:NBP, 0:NBP, :], in1=selT[0:NBP, :].unsqueeze(1).to_broadcast([NBP, NBP, 64]), op=ALU.mult),
                  [id32, selT], [Dexp])
                A(lambda e: e.activation(out=S_all[:, :, :], in_=S_all[:, :, :], func=AF.Exp), [], [S_all] + S_dep)
                for n0 in range(0, NBP, 8):
                    nn = min(8, NBP - n0)
                    M(lambda e, n0=n0, nn=nn: e.matmul(pMk[:, 0:nn * 64], lhsT=ONESB[0:NBP, :], rhs=Dexp[0:NBP, n0:n0 + nn, :].rearrange("p n t -> p (n t)"),
                                                       start=True, stop=True), [Dexp, cstb], [pMk])
                    V(lambda e, n0=n0, nn=nn: e.tensor_tensor(
                        out=Pm[:, 2 * n0:2 * (n0 + nn), :].rearrange("p (n two) t -> p n two t", two=2),
                        in0=S_all[:, 2 * n0:2 * (n0 + nn), :].rearrange("p (n two) t -> p n two t", two=2),
                        in1=pMk[:, 0:nn * 64].rearrange("p (n t) -> p n t", n=nn).unsqueeze(2).to_broadcast([128, nn, 2, 64]), op=ALU.mult), [S_all, pMk], [Pm])
                pST = pR.next()
                for c in range(4):
                    M(lambda e, c=c, pST=pST: e.matmul(pST[0:TS, c * 2 * TS:(c + 1) * 2 * TS], lhsT=KTn[:, c, :], rhs=QBD[:, c, :, :].rearrange("p a t -> p (a t)"),
                                              start=True, stop=True), [KTn, QBD], [pST])
                A(lambda e, pST=pST: e.activation(out=Pnf[:, :, :], in_=pST[0:TS, 0:64].rearrange("p (h t) -> p h t", h=8), func=AF.Exp), [pST], [Pnf])
                V(lambda e: e.tensor_tensor(out=Pn[:, :, :], in0=Pnf[:, :, :], in1=TRI01[0:TS, 0:TS].unsqueeze(1).to_broadcast([TS, 8, TS]), op=ALU.mult), [Pnf, cstb], [Pn])
                for j in range(NPG):
                    Vt = wk["f5"].next(); ix = idx1.next()
                    G(lambda e, ix=ix, j=j: e.tensor_copy(out=ix[:, :], in_=idxi[:, j:j + 1]), [idxi], [ix])
                    DMA(lambda e, Vt=Vt, ix=ix: e.indirect_dma_start(out=Vt[:, :], out_offset=None, in_=cv2,
                                                                     in_offset=bass.IndirectOffsetOnAxis(ap=ix[:, :], axis=0)), [ix], [Vt], q="pool")
                    Vb = wk["b5"].next()
                    if j % 2 == 0:
                        V(lambda e, Vt=Vt, Vb=Vb: e.tensor_copy(out=Vb[:, :], in_=Vt[:, :]), [Vt], [Vb])
                    else:
                        A(lambda e, Vt=Vt, Vb=Vb: e.activation(out=Vb[:, :], in_=Vt[:, :], func=AF.Copy), [Vt], [Vb])
                    M(lambda e, j=j, Vb=Vb: e.matmul(pOV[0:64, 0:512], lhsT=Pm[:, j, :], rhs=Vb[:, :], start=(j == 0), stop=False), [Pm, Vb], [pOV])
                    M(lambda e, j=j: e.matmul(pBS[0:64, 64:65], lhsT=Pm[:, j, :], rhs=ONESB[:, 0:1], start=(j == 0), stop=False), [Pm, cstb], [pBS])
                M(lambda e, vbn=vbn: e.matmul(pOV[0:64, 0:512], lhsT=Pn[:, :, :].rearrange("p h t -> p (h t)"), rhs=vbn[0:TS, :], start=False, stop=True), [Pn, vbn], [pOV])
                M(lambda e: e.matmul(pBS[0:64, 64:65], lhsT=Pn[:, :, :].rearrange("p h t -> p (h t)"), rhs=ONESB[0:TS, 0:1], start=False, stop=True), [Pn, cstb], [pBS])
                V(lambda e: e.tensor_tensor(out=tmpo[:, :, :], in0=pOV[0:64, 0:512].rearrange("p (h d) -> p h d", h=8),
                                            in1=BDm[0:64, 0:8].unsqueeze(2).to_broadcast([64, 8, 64]), op=ALU.mult), [pOV, cst], [tmpo])
                V(lambda e: e.tensor_reduce(out=OD[:, :], in_=tmpo[:, :, :].rearrange("p h d -> p d h"), axis=AX.X, op=ALU.add), [tmpo], [OD])
                V(lambda e: e.reciprocal(out=rdn[:, :], in_=pBS[0:64, 64:65]), [pBS], [rdn])
                V(lambda e: e.tensor_scalar(out=OD[:, :], in0=OD[:, :], scalar1=rdn[:, 0:1], scalar2=None, op0=ALU.mult), [OD, rdn], [OD])
                M(lambda e: e.transpose(out=pMk[0:64, 0:64], in_=OD[:, :], identity=IDF[0:64, 0:64]), [OD, cst], [pMk])
                V(lambda e, sq_=sq_: e.tensor_copy(out=OTs[:, :, sq_ * TS:(sq_ + 1) * TS], in_=pMk[0:64, 0:64].rearrange("p (h t) -> p h t", h=8)), [pMk], [OTs])

        def phase_B1(l, KT2, Vall):
            with contextlib.ExitStack() as pst:
                cur[0] = pst
                Wb = sb([128, 8, 1536], BF16, "Wb")
                with WLoad() as stg:
                    load_w(Wb, 0, 1536, w_in_src(l, C_BQ), stg)
                qn_bc = bvec(w["mb_q_norm"][l:l + 1, :], 64, "qn"); kn_bc = bvec(w["mb_k_norm"][l:l + 1, :], 64, "kn")
                p1 = contextlib.ExitStack(); p1.__enter__(); cur[0] = p1
                wk = dict(f5=Rot(lambda: sb([128, 512], F32, "f5"), 6), g8=Rot(lambda: sb([128, 8], F32, "g8"), 12),
                          b5=Rot(lambda: sb([128, 512], BF16, "b5"), 4))
                pR = Rot(lambda: psum(F32, "pRb"), 3); pT = Rot(lambda: psum(BF16, "pTb"), 2); pBS = psum(F32, "pBS"); pMB = psum(BF16, "pMB")
                hpool = Rot(lambda: sb([128, 8, 512], BF16, "hTg"), 2)
                qpool = Rot(lambda: zsb([128, 4, 2, 512], BF16, "QT2m"), 2)
                bpool = Rot(lambda: sb([16, 8, 512], BF16, "biasT"), 2)
                KBf = sb([128, 4, 16], F32, "KBf"); KBb = sb([128, 4, 16], BF16, "KBb")
                V(lambda e: e.memset(KBf[:, :, :], 0.0), [], [KBf]); V(lambda e: e.memset(KBb[:, :, :], 0.0), [], [KBb])
                bsm = sb([128, 8, 16], F32, "bsm"); m8 = sb([128, 8, 8], F32, "m8"); ge = sb([128, 8, 16], F32, "ge")
                mbias = sb([128, 8, 16], BF16, "mbias")
                for g in range(NG):
                    hTg = hpool.next(); QT2m = qpool.next(); biasT = bpool.next()
                    DMA(lambda e, hTg=hTg, g=g: e.dma_start(out=hTg[:, :, :], in_=hT_d[g]), [hT_dep[g]], [hTg])
                    for j in range(4):
                        i = 4 * g + j; t0 = j * 128; own = i // 2
                        pv = pR.next()
                        tok_proj(pv[:, 0:512], pv, hTg, hTg, t0, 128, Wb, Wb, 1024, 512)
                        vf = wk["f5"].next()
                        A(lambda e, pv=pv, vf=vf: e.activation(out=vf[:, :], in_=pv[:, 0:512], func=AF.Copy), [pv], [vf])
                        DMA(lambda e, vf=vf, i=i: e.dma_start(out=p_v[l, i * 128:(i + 1) * 128, :], in_=vf[:, :]), [vf], [], final=True)
                        G(lambda e, vf=vf, i=i: e.tensor_copy(out=Vall[:, i, :], in_=vf[:, :]), [vf], [Vall])
                        pk = pR.next()
                        tok_proj(pk[:, 0:512], pk, hTg, hTg, t0, 128, Wb, Wb, 512, 512)
                        kf = wk["f5"].next()
                        qk_norm(128, pk, kn_bc, 1.0, wk, kf)
                        DMA(lambda e, kf=kf, i=i: e.dma_start(out=p_k[l, i * 128:(i + 1) * 128, :], in_=kf[:, :]), [kf], [], final=True)
                        kb = wk["b5"].next()
                        G(lambda e, kf=kf, kb=kb: e.tensor_copy(out=kb[:, :], in_=kf[:, :]), [kf], [kb])
                        pt_ = pT.next()
                        ptv = pt_[:, 0:512].rearrange("p (c t) -> p c t", c=4)
                        for c in range(4):
                            M(lambda e, c=c, kb=kb, ptv=ptv: e.transpose(out=ptv[:, c, :], in_=kb[:, c * 128:(c + 1) * 128], identity=IDB[:, :]), [kb, cstb], [pt_])
                        A(lambda e, ptv=ptv, i=i: e.activation(out=KT2[:, :, i * 128:(i + 1) * 128], in_=ptv[:, :, :], func=AF.Copy), [pt_], [KT2])
                        pq = pR.next()
                        tok_proj(pq[:, 0:512], pq, hTg, hTg, t0, 128, Wb, Wb, 0, 512)
                        qf = wk["f5"].next()
                        qk_norm(128, pq, qn_bc, 0.125, wk, qf)
                        qb = wk["b5"].next()
                        G(lambda e, qf=qf, qb=qb: e.tensor_copy(out=qb[:, :], in_=qf[:, :]), [qf], [qb])
                        pt2 = pT.next()
                        ptv2 = pt2[:, 0:512].rearrange("p (c t) -> p c t", c=4)
                        for c in range(4):
                            M(lambda e, c=c, qb=qb, ptv2=ptv2: e.transpose(out=ptv2[:, c, :], in_=qb[:, c * 128:(c + 1) * 128], identity=IDB[:, :]), [qb, cstb], [pt2])
                        A(lambda e, ptv2=ptv2, t0=t0, QT2m=QT2m: e.activation(out=QT2m[0:64, :, 0, t0:t0 + 128], in_=ptv2[0:64, :, :], func=AF.Copy), [pt2], [QT2m])
                        V(lambda e, ptv2=ptv2, t0=t0, QT2m=QT2m: e.tensor_copy(out=QT2m[64:128, :, 1, t0:t0 + 128], in_=ptv2[64:128, :, :]), [pt2], [QT2m])
                        G(lambda e: e.memset(mbias[:, :, :], 0.0), [], [mbias])
                        if own >= 4:
                            pbv = pBS[:, 0:128].rearrange("p (h n) -> p h n", h=8)
                            for h in range(8):
                                M(lambda e, h=h, QT2m=QT2m, t0=t0, pbv=pbv: e.matmul(pbv[:, h, :], lhsT=QT2m[:, h // 2, h % 2, t0:t0 + 128], rhs=KBb[:, h // 2, :],
                                                                                  start=True, stop=True), [QT2m, KBb], [pBS])
                            G(lambda e: e.memset(bsm[:, :, :], NEG), [], [bsm])
                            V(lambda e, pbv=pbv, own=own: e.tensor_copy(out=bsm[:, :, 0:own], in_=pbv[:, :, 0:own]), [pBS], [bsm])
                            for h in range(8):
                                V(lambda e, h=h: e.max(out=m8[:, h, :], in_=bsm[:, h, :]), [bsm], [m8])
                            V(lambda e: e.tensor_tensor(out=ge[:, :, :], in0=bsm[:, :, :], in1=m8[:, :, 2:3].to_broadcast([128, 8, 16]), op=ALU.is_ge), [bsm, m8], [ge])
                            V(lambda e, own=own: e.tensor_scalar(out=mbias[:, :, 0:own], in0=ge[:, :, 0:own], scalar1=-1.0, scalar2=-MB_NEG, op0=ALU.add, op1=ALU.mult),
                              [ge], [mbias])
                        pmv = pMB[0:16, 0:1024].rearrange("p (h t) -> p h t", h=8)
                        for h in range(8):
                            M(lambda e, h=h, pmv=pmv: e.transpose(out=pmv[:, h, :], in_=mbias[:, h, :], identity=IDB[:, :]), [mbias, cstb], [pMB])
                        A(lambda e, pmv=pmv, biasT=biasT, t0=t0: e.activation(out=biasT[0:16, :, t0:t0 + 128], in_=pmv[:, :, :], func=AF.Copy), [pMB], [biasT])
                        if i % 2 == 1:
                            n = i // 2
                            V(lambda e, n=n: e.tensor_reduce(out=KBf[:, :, n:n + 1], in_=KT2[:, :, n * 256:(n + 1) * 256], axis=AX.X, op=ALU.add), [KT2], [KBf])
                            V(lambda e: e.tensor_copy(out=KBb[:, :, :], in_=KBf[:, :, :]), [KBf], [KBb])
                    DMA(lambda e, QT2m=QT2m, g=g: e.dma_start(out=qT_d[g], in_=QT2m[:, :, :, :]), [QT2m], [qT_dep[g]])
                    DMA(lambda e, biasT=biasT, g=g: e.dma_start(out=mbias_d[g], in_=biasT[:, :, :]), [biasT], [mbias_dep[g]])
                barrier()
                p1.__exit__(None, None, None)
                cur[0] = pst
                if cfg.sample:
                    with contextlib.ExitStack() as p2:
                        cur[0] = p2
                        sample_moba(l, Wb, qn_bc, kn_bc)
                        barrier()
                    cur[0] = pst
            cur[0] = gst

        def phase_B2(l, first, KT2, Vall):
            with contextlib.ExitStack() as pst:
                cur[0] = pst
                Wz = sb([128, 8, 512], BF16, "Wz"); Wgt = sb([128, 8, D], BF16, "Wgt"); Wmb = sb([64, 8, D], BF16, "Wmb")
                with WLoad() as stg:
                    load_w(Wz, 0, 512, w_in_src(l, C_BZ), stg)
                    load_gate_w(Wgt, l, 2, stg)
                    load_w(Wmb, 0, D, lambda a, n: w["mb_w_out"][l, :, a:a + n].rearrange("(h p) n -> p h n", p=64), stg, kc=8, rows=64)
                pR = Rot(lambda: psum(F32, "pR2"), 2); pSr = Rot(lambda: psum(F32, "pS2"), 3); pOr = Rot(lambda: psum(F32, "pO2"), 2); pDr = Rot(lambda: psum(F32, "pD2"), 1)
                bw = branch_work(pR)
                hpool = Rot(lambda: sb([128, 8, 512], BF16, "hTg"), 1)
                ypool = Rot(lambda: sb([128, 8, 512], BF16, "yTg"), 1)
                qpool = Rot(lambda: sb([128, 4, 2, 512], BF16, "QT2m"), 1)
                bpool = Rot(lambda: sb([16, 8, 512], BF16, "biasT"), 1)
                ZbT = sb([64, 8, 512], BF16, "ZbT")
                PTp = Rot(lambda: sb([128, 512], BF16, "PT"), 4)
                dsp = Rot(lambda: sb([64, 512], F32, "dsb"), 2); szp = Rot(lambda: sb([64, 512], F32, "szT"), 2); rdp = Rot(lambda: sb([64, 512], F32, "rd"), 2); otp = Rot(lambda: sb([64, 512], F32, "ot"), 2)
                for g in range(NG):
                    hTg = hpool.next(); yTg = ypool.next(); QT2m = qpool.next(); biasT = bpool.next()
                    DMA(lambda e, hTg=hTg, g=g: e.dma_start(out=hTg[:, :, :], in_=hT_d[g]), [hT_dep[g]], [hTg])
                    if not first:
                        DMA(lambda e, yTg=yTg, g=g: e.dma_start(out=yTg[:, :, :], in_=yT_d[g]), [yT_dep[g]], [yTg])
                    DMA(lambda e, QT2m=QT2m, g=g: e.dma_start(out=QT2m[:, :, :, :], in_=qT_d[g]), [qT_dep[g]], [QT2m])
                    DMA(lambda e, biasT=biasT, g=g: e.dma_start(out=biasT[:, :, :], in_=mbias_d[g]), [mbias_dep[g]], [biasT])
                    nj = 4 * g + 4

                    def qk(h, j):
                        ps_ = pSr.next()
                        M(lambda e, ps_=ps_, j=j, h=h, QT2m=QT2m: e.matmul(ps_[:, 0:512], lhsT=KT2[:, h // 2, j * 128:(j + 1) * 128], rhs=QT2m[:, h // 2, h % 2, :],
                                                                       start=True, stop=False), [KT2, QT2m], [ps_])
                        M(lambda e, ps_=ps_, j=j, h=h, biasT=biasT: e.matmul(ps_[:, 0:512], lhsT=ohb[0:16, j // 2, :], rhs=biasT[0:16, h, :], start=False, stop=True),
                          [ohb, biasT], [ps_])
                        return ps_

                    seq = [(h, j) for h in range(8) for j in range(nj)]
                    pendq = [qk(*seq[0])]
                    if len(seq) > 1:
                        pendq.append(qk(*seq[1]))
                    for idx_, (h, j) in enumerate(seq):
                        ps_ = pendq.pop(0)
                        if j == 0:
                            pz = pR.next()
                            feat_proj(pz[0:64, 0:512], pz, Wz, Wz, h * 64, 64, hTg, hTg, 0, 512)
                            szT = szp.next()
                            A(lambda e, pz=pz, szT=szT: e.activation(out=szT[:, :], in_=pz[0:64, 0:512], func=AF.Silu), [pz], [szT])
                            pO = pOr.next(); pD = pDr.next()
                        PT = PTp.next()
                        A(lambda e, ps_=ps_, PT=PT: e.activation(out=PT[:, :], in_=ps_[:, 0:512], func=AF.Exp), [ps_], [PT])
                        if idx_ + 2 < len(seq):
                            pendq.append(qk(*seq[idx_ + 2]))
                        if j >= 4 * g:
                            G(lambda e, PT=PT, r=j - 4 * g: e.tensor_tensor(out=PT[:, :], in0=PT[:, :], in1=CAUS[r], op=ALU.mult), [PT, cstb], [PT])
                        M(lambda e, PT=PT, j=j, h=h, nj=nj, pO=pO: e.matmul(pO[0:64, 0:512], lhsT=Vall[:, j, h * 64:(h + 1) * 64], rhs=PT[:, :], start=(j == 0), stop=(j == nj - 1)),
                          [Vall, PT], [pO])
                        M(lambda e, PT=PT, j=j, nj=nj, pD=pD: e.matmul(pD[0:64, 0:512], lhsT=ONESB[:, 0:64], rhs=PT[:, :], start=(j == 0), stop=(j == nj - 1)), [cstb, PT], [pD])
                        if j == nj - 1:
                            rd = rdp.next(); ot = otp.next(); dsb = dsp.next()
                            A(lambda e, dsb=dsb, pD=pD: e.activation(out=dsb[:, :], in_=pD[0:64, 0:512], func=AF.Copy), [pD], [dsb])
                            V(lambda e, rd=rd, dsb=dsb: e.reciprocal(out=rd[:, :], in_=dsb[:, :]), [dsb], [rd])
                            V(lambda e, rd=rd, ot=ot, pO=pO: e.tensor_tensor(out=ot[:, :], in0=pO[0:64, 0:512], in1=rd[:, :], op=ALU.mult), [pO, rd], [ot])
                            G(lambda e, ot=ot, szT=szT, h=h: e.tensor_tensor(out=ZbT[:, h, :], in0=ot[:, :], in1=szT[:, :], op=ALU.mult), [ot, szT], [ZbT])
                    terms = [((lambda c, h=h: Wmb[0:64, h, c * 128:(c + 1) * 128]), ZbT[0:64, h, :]) for h in range(8)]
                    branch_out(terms, ZbT, Wmb, Wmb, Wgt, Wgt, hTg, 512, yTg, first, bw)
                    DMA(lambda e, yTg=yTg, g=g: e.dma_start(out=yT_d[g], in_=yTg[:, :, :]), [yTg], [yT_dep[g]])
                if cfg.sample:
                    for h in range(8):
                        pz = pR.next()
                        feat_proj(pz[0:64, 0:NTOK], pz, Wz, Wz, h * 64, 64, hTs, hTs, 0, NTOK)
                        szT = szp.next()
                        A(lambda e, pz=pz, szT=szT: e.activation(out=szT[:, 0:NTOK], in_=pz[0:64, 0:NTOK], func=AF.Silu), [pz], [szT])
                        V(lambda e, szT=szT, h=h: e.tensor_tensor(out=ZbT[:, h, 0:NTOK], in0=OTs[:, h, :], in1=szT[:, 0:NTOK], op=ALU.mult), [OTs, szT], [ZbT])
                    terms = [((lambda c, h=h: Wmb[0:64, h, c * 128:(c + 1) * 128]), ZbT[0:64, h, 0:NTOK]) for h in range(8)]
                    branch_out(terms, ZbT, Wmb, Wmb, Wgt, Wgt, hTs, NTOK, yTs, first, bw)
                barrier()
            cur[0] = gst

        def phase_B(l, first):
            with contextlib.ExitStack() as bst:
                cur[0] = bst
                KT2 = sb([128, 4, L], BF16, "KT2"); Vall = sb([128, NT, 512], BF16, "Vall")
                phase_B1(l, KT2, Vall)
                cur[0] = bst
                phase_B2(l, first, KT2, Vall)
            cur[0] = gst

        for l in range(DEPTH):
            if cfg.prompt:
                if l == 0:
                    phase_N(0)
                if STOP == 1:
                    break
                first = True
                if "g" in cfg.branches:
                    phase_G(l, first); first = False
                    if STOP <= 4:
                        break
                if "m" in cfg.branches:
                    phase_M(l, first); first = False
                    if STOP <= 4:
                        break
                if "b" in cfg.branches:
                    phase_B(l, first); first = False
                phase_O(l)
        P.emit()
    return nc


W_NAMES = ("norm_g", "w_in", "gla_w_a2", "gla_b_a", "gla_norm", "gla_w_out", "ml_conv_w", "ml_conv_b", "ml_w_q", "ml_w_k",
           "ml_w_v", "ml_b_i", "ml_b_f", "ml_norm", "ml_skip", "ml_w_out", "mb_q_norm", "mb_k_norm", "mb_w_out", "w_out")
_CACHE = {}


def run(cfg, inputs, n_cores=8):
    key = (cfg.L, cfg.NS, cfg.TS, cfg.NPG, cfg.NPOOL, cfg.DEPTH, cfg.branches, cfg.sample, cfg.prompt)
    if key not in _CACHE:
        _CACHE[key] = build(cfg)
    nc = _CACHE[key]
    f32 = lambda a: np.ascontiguousarray(np.asarray(a), dtype=np.float32)
    consts = make_consts()
    Bp = inputs["x_prompt"].shape[0]
    NS, TS, DEPTH = cfg.NS, cfg.TS, cfg.DEPTH
    ck = f32(inputs["cache_k"]).reshape(DEPTH, cfg.NPOOL * 128, 512)
    cv = f32(inputs["cache_v"]).reshape(DEPTH, cfg.NPOOL * 128, 512)
    wts = {k: f32(inputs[k]) for k in W_NAMES}
    in_maps = []
    for c in range(n_cores):
        b = c % Bp
        sl = slice(c * NS, (c + 1) * NS)
        m = dict(wts)
        m["consts"] = consts
        m["xp"] = f32(inputs["x_prompt"][b])
        m["xs"] = f32(inputs["x_sample"][sl]).reshape(NS * TS, D)
        for l_ in range(DEPTH):
            m["ck%d" % l_] = ck[l_]
            m["cv%d" % l_] = cv[l_]
        m["pt"] = np.ascontiguousarray(np.asarray(inputs["page_table"])[sl], dtype=np.int32)
        m["sg"] = f32(np.asarray(inputs["state_gla"])[:, sl])
        m["sc"] = f32(np.asarray(inputs["state_mlstm_c"])[:, sl])
        m["sn"] = f32(np.asarray(inputs["state_mlstm_n"])[:, sl])
        m["sm"] = f32(np.asarray(inputs["state_mlstm_m"])[:, sl])
        m["scv"] = f32(np.asarray(inputs["state_mlstm_conv"])[:, sl])
        in_maps.append(m)
    res = run_bass_kernel_spmd(nc, in_maps, core_ids=list(range(n_cores))).results
    L = cfg.L
    nb = min(Bp, n_cores)
    st = lambda name, shp: np.stack([res[b][name].reshape(shp) for b in range(nb)], axis=1)
    y_prompt = np.stack([res[b]["yp"] for b in range(nb)], 0)
    y_sample = np.concatenate([res[c]["ys"].reshape(NS, TS, D) for c in range(n_cores)], 0)
    p_gla = st("p_gla", (DEPTH, 4, 64, 128)); p_mc = st("p_mc", (DEPTH, 4, 128, 128)); p_mn = st("p_mn", (DEPTH, 4, 128))
    p_mm = st("p_mm", (DEPTH, 4)); p_conv = st("p_conv", (DEPTH, 3, 512))
    p_k = st("p_k", (DEPTH, L, 8, 64)); p_v = st("p_v", (DEPTH, L, 8, 64))
    cs = lambda name, shp: np.concatenate([res[c][name].reshape(shp) for c in range(n_cores)], axis=1)
    s_gla = cs("s_gla", (DEPTH, NS, 4, 64, 128)); s_mc = cs("s_mc", (DEPTH, NS, 4, 128, 128)); s_mn = cs("s_mn", (DEPTH, NS, 4, 128))
    s_mm = cs("s_mm", (DEPTH, NS, 4)); s_conv = cs("s_conv", (DEPTH, NS, 3, 512))
    s_k = cs("s_k", (DEPTH, NS, TS, 8, 64)); s_v = cs("s_v", (DEPTH, NS, TS, 8, 64))
    return (y_prompt, y_sample, p_gla, p_mc, p_mn, p_mm, p_conv, p_k, p_v, s_gla, s_mc, s_mn, s_mm, s_conv, s_k, s_v)


def kernel(**inputs):
    return run(Cfg(), inputs, n_cores=8)
```

```python
import contextlib
import numpy as np
import concourse.bass as bass
import concourse.mybir as mybir
from concourse.bass_utils import run_bass_kernel_spmd

F32 = mybir.dt.float32
BF16 = mybir.dt.bfloat16
I32 = mybir.dt.int32
AF = mybir.ActivationFunctionType
ALU = mybir.AluOpType
AX = mybir.AxisListType

D = 1024
D_IN = 8216
EPS = 1e-6
NEG = -1e30
MB_NEG = -30000.0
C_GQ, C_GK, C_GV, C_GA, C_GZ = 0, 256, 512, 1024, 1040
C_MX, C_MI, C_MF, C_MO, C_MZ = 1552, 2064, 2068, 2072, 2584
C_BQ, C_BK, C_BV, C_BZ = 3096, 3608, 4120, 4632
C_GATE = 5144


class Dep:
    __slots__ = ("w", "r")

    def __init__(self):
        self.w = None
        self.r = []


class B:
    __slots__ = ("t", "d", "ps")

    def __init__(self, t, ps=False):
        self.t = t
        self.d = Dep()
        self.ps = ps

    def __getitem__(self, k):
        return self.t[k]


class Prog:
    ENGS = ("pe", "act", "dve", "pool", "sp")

    N_SW = 8

    def __init__(self, nc, n_dma_sems=48):
        self.nc = nc
        self.ops = {e: [] for e in self.ENGS}
        self.n_dma_sems = n_dma_sems
        self.dma_rr = 0
        self.sw_rr = 0
        self.dma_val = [0] * n_dma_sems
        self.final_dma = []

    def _deps(self, eng, reads, writes):
        deps = []
        for b in reads:
            t = b.d
            if t.w is not None:
                deps.append(t.w)
        for b in writes:
            t = b.d
            if t.w is not None:
                deps.append(t.w)
            deps.extend(t.r)
        if eng == "pe":
            deps = [d for d in deps if not (d[0] == "eng" and d[1] == "pe")]
        return deps

    def _mark(self, ref, reads, writes):
        for b in reads:
            r = b.d.r
            r.append(ref)
            if len(r) > 24:
                last = {}
                for x in r:
                    key = (x[0], x[1])
                    if key not in last or x[2] > last[key][2]:
                        last[key] = x
                b.d.r = list(last.values())
        for b in writes:
            b.d.w = ref
            b.d.r = []

    def op(self, eng, fn, reads=(), writes=()):
        if any(b.ps for b in reads):
            writes = list(writes) + [b for b in reads if b.ps]
            reads = [b for b in reads if not b.ps]
        lst = self.ops[eng]
        deps = self._deps(eng, reads, writes)
        lst.append(dict(kind="c", fn=fn, deps=deps, inc=False))
        self._mark(("eng", eng, len(lst) - 1), reads, writes)

    def dma(self, fn, reads=(), writes=(), q="sp", final=False):
        lst = self.ops[q]
        deps = self._deps(q, reads, writes)
        if q == "pool":
            k = self.n_dma_sems - self.N_SW + self.sw_rr
            self.sw_rr = (self.sw_rr + 1) % self.N_SW
        else:
            k = self.dma_rr
            self.dma_rr = (self.dma_rr + 1) % (self.n_dma_sems - self.N_SW)
        prev = self.dma_val[k]
        self.dma_val[k] += 16
        val = self.dma_val[k]
        if prev > 0:
            deps.append(("dma", k, prev))
        lst.append(dict(kind="d", fn=fn, deps=deps, sem=k, val=val))
        self._mark(("dma", k, val), reads, writes)
        if final:
            self.final_dma.append((k, val))

    def emit(self):
        nc = self.nc
        for e in self.ENGS:
            for o in self.ops[e]:
                for d in o["deps"]:
                    if d[0] == "eng":
                        self.ops[d[1]][d[2]]["inc"] = True
        vals = {}
        for e in self.ENGS:
            c = 0
            v = []
            for o in self.ops[e]:
                if o["kind"] == "c" and o["inc"]:
                    c += 1
                v.append(c)
            vals[e] = v
        with contextlib.ExitStack() as st:
            esem = {e: st.enter_context(nc.semaphore("s_" + e)) for e in self.ENGS}
            dsem = [st.enter_context(nc.semaphore("d%d" % i)) for i in range(self.n_dma_sems)]
            block = st.enter_context(nc.Block())
            engobj = {"pe": "tensor", "act": "scalar", "dve": "vector", "pool": "gpsimd", "sp": "sync"}

            def run(e, eng):
                seen = {}
                dseen = {}
                for o in self.ops[e]:
                    need = {}
                    dneed = {}
                    for d in o["deps"]:
                        if d[0] == "eng":
                            v = vals[d[1]][d[2]]
                            if v > seen.get(d[1], 0):
                                need[d[1]] = max(need.get(d[1], 0), v)
                        else:
                            if d[2] > dseen.get(d[1], 0):
                                dneed[d[1]] = max(dneed.get(d[1], 0), d[2])
                    for k2, v in need.items():
                        eng.wait_ge(esem[k2], v)
                        seen[k2] = v
                    for k2, v in dneed.items():
                        eng.wait_ge(dsem[k2], v)
                        dseen[k2] = v
                    if o["kind"] == "b":
                        continue
                    ins = o["fn"](eng)
                    if o["kind"] == "d":
                        ins.then_inc(dsem[o["sem"]], 16)
                    elif o["inc"]:
                        ins.then_inc(esem[e], 1)
                if e == "sp":
                    for (k2, v) in self.final_dma:
                        if v > dseen.get(k2, 0):
                            eng.wait_ge(dsem[k2], v)
                            dseen[k2] = v

            for e in self.ENGS:
                if not self.ops[e] and e != "sp":
                    continue
                getattr(block, engobj[e])(lambda eng, e=e: run(e, eng))


K_ID, K_TRIS, K_SUTS, K_TRIN, K_TRI01, K_NEGM, K_ONES, K_SELL128, K_SELL8 = [128 * i for i in range(9)]
K_CAUS = 128 * 9
K_BD = K_CAUS + 4 * 512
K_IOTA = K_BD + 8
K_ID32 = K_IOTA + 1
K_OH = K_ID32 + 32 * 64
NCONST = K_OH + 16 * 128


def make_consts():
    c = np.zeros((128, NCONST), np.float32)
    p = np.arange(128)[:, None]
    f = np.arange(128)[None, :]
    c[:, K_ID:K_ID + 128] = (p == f)
    c[:, K_TRIS:K_TRIS + 128] = (p <= f) * (-1.0 / 16.0)
    c[:, K_SUTS:K_SUTS + 128] = (p > f) * (-1.0 / 16.0)
    c[:, K_TRIN:K_TRIN + 128] = (p <= f) * (-1.0)
    c[:, K_TRI01:K_TRI01 + 128] = (p <= f)
    c[:, K_NEGM:K_NEGM + 128] = np.where(f <= p, 0.0, NEG)
    c[:, K_ONES:K_ONES + 128] = 1.0
    c[:, K_SELL128:K_SELL128 + 128] = (p == 127)
    c[:, K_SELL8:K_SELL8 + 128] = (p == 7)
    f5 = np.arange(512)[None, :]
    for r in range(4):
        c[:, K_CAUS + 512 * r:K_CAUS + 512 * (r + 1)] = ((128 * r + p) <= f5)
    c[:, K_BD:K_BD + 8] = ((p // 8) == np.arange(8)[None, :])
    c[:, K_IOTA] = np.arange(128)
    k32 = np.arange(128)[:, None, None]
    n32 = np.arange(32)[None, :, None]
    c[:, K_ID32:K_ID32 + 32 * 64] = np.broadcast_to((k32 == n32), (128, 32, 64)).reshape(128, 2048)
    c[:, K_OH:K_OH + 2048] = np.broadcast_to((np.arange(128)[:, None, None] == np.arange(16)[None, :, None]), (128, 16, 128)).reshape(128, 2048)
    return c


class Cfg:
    def __init__(self, L=4096, NS=4, TS=8, NPG=64, NPOOL=2560, DEPTH=2, branches="gmb", sample=True, prompt=True):
        self.L, self.NS, self.TS, self.NPG, self.NPOOL, self.DEPTH = L, NS, TS, NPG, NPOOL, DEPTH
        self.branches, self.sample, self.prompt = branches, sample, prompt


def build(cfg):
    nc = bass.Bass("TRN2", target_bir_lowering=False)
    import os
    STOP = int(os.environ.get("KSTOP", "99"))
    L, NS, TS, NPG, NPOOL, DEPTH = cfg.L, cfg.NS, cfg.TS, cfg.NPG, cfg.NPOOL, cfg.DEPTH
    NT, NG = L // 128, L // 512
    NBLK = L // 256
    P = Prog(nc)
    uid = [0]

    def din(name, shape, dt=F32):
        return nc.dram_tensor(name, list(shape), dt, kind="ExternalInput").ap()

    def dout(name, shape, dt=F32):
        return B(nc.dram_tensor(name, list(shape), dt, kind="ExternalOutput").ap())

    def dscr(name, shape, dt):
        return nc.dram_tensor(name, list(shape), dt).ap()

    xp = din("xp", [L, D]); xs = din("xs", [NS * TS, D])
    ck = [din("ck%d" % l_, [NPOOL * 128, 512]) for l_ in range(DEPTH)]; cv = [din("cv%d" % l_, [NPOOL * 128, 512]) for l_ in range(DEPTH)]
    pt = din("pt", [NS, NPG], I32)
    sg_in = din("sg", [DEPTH, NS, 4, 64, 128]); sc_in = din("sc", [DEPTH, NS, 4, 128, 128])
    sn_in = din("sn", [DEPTH, NS, 4, 128]); sm_in = din("sm", [DEPTH, NS, 4]); scv_in = din("scv", [DEPTH, NS, 3, 512])
    w = {}
    for name, shape in (("norm_g", [DEPTH, D]), ("w_in", [DEPTH, D, D_IN]), ("gla_w_a2", [DEPTH, 16, 256]),
                        ("gla_b_a", [DEPTH, 256]), ("gla_norm", [DEPTH, 128]), ("gla_w_out", [DEPTH, 512, D]),
                        ("ml_conv_w", [DEPTH, 4, 512]), ("ml_conv_b", [DEPTH, 512]), ("ml_w_q", [DEPTH, 4, 128, 128]),
                        ("ml_w_k", [DEPTH, 4, 128, 128]), ("ml_w_v", [DEPTH, 4, 128, 128]), ("ml_b_i", [DEPTH, 4]),
                        ("ml_b_f", [DEPTH, 4]), ("ml_norm", [DEPTH, 128]), ("ml_skip", [DEPTH, 512]),
                        ("ml_w_out", [DEPTH, 512, D]), ("mb_q_norm", [DEPTH, 64]), ("mb_k_norm", [DEPTH, 64]),
                        ("mb_w_out", [DEPTH, 512, D]), ("w_out", [DEPTH, D, D])):
        w[name] = din(name, shape)
    consts_d = din("consts", [128, NCONST])

    yp = dout("yp", [L, D]); ys = dout("ys", [NS * TS, D])
    p_gla = dout("p_gla", [DEPTH, 4, 64, 128]); p_mc = dout("p_mc", [DEPTH, 4, 128, 128])
    p_mn = dout("p_mn", [DEPTH, 4, 128]); p_mm = dout("p_mm", [DEPTH, 4]); p_conv = dout("p_conv", [DEPTH, 3, 512])
    p_k = dout("p_k", [DEPTH, L, 512]); p_v = dout("p_v", [DEPTH, L, 512])
    s_gla = dout("s_gla", [DEPTH, NS, 4, 64, 128]); s_mc = dout("s_mc", [DEPTH, NS, 4, 128, 128])
    s_mn = dout("s_mn", [DEPTH, NS, 4, 128]); s_mm = dout("s_mm", [DEPTH, NS, 4]); s_conv = dout("s_conv", [DEPTH, NS, 3, 512])
    s_k = dout("s_k", [DEPTH, NS * TS, 512]); s_v = dout("s_v", [DEPTH, NS * TS, 512])

    hT_d = dscr("hT_d", [max(NG, 1), 128, 8, 512], BF16); hT_dep = [B(None) for _ in range(max(NG, 1))]
    yT_d = dscr("yT_d", [max(NG, 1), 128, 8, 512], BF16); yT_dep = [B(None) for _ in range(max(NG, 1))]
    x1_d = dscr("x1_d", [L, D], F32); x1_dep = [B(None) for _ in range(max(NT, 1))]
    qT_d = dscr("qT_d", [max(NG, 1), 128, 4, 2, 512], BF16); qT_dep = [B(None) for _ in range(max(NG, 1))]
    mbias_d = dscr("mbias_d", [max(NG, 1), 16, 8, 512], BF16); mbias_dep = [B(None) for _ in range(max(NG, 1))]

    with contextlib.ExitStack() as gst:
        cur = [gst]

        def sb(shape, dt, name="t"):
            uid[0] += 1
            return B(cur[0].enter_context(nc.sbuf_tensor("%s_%d" % (name, uid[0]), list(shape), dt)))

        def zsb(shape, dt, name="z"):
            b = sb(shape, dt, name)
            nd = len(shape)
            P.op("pool", lambda e: e.memset(b[(slice(None),) * nd], 0.0), [], [b])
            return b

        def psum(dt=F32, name="ps"):
            uid[0] += 1
            n = 512 if dt == F32 else 1024
            return B(cur[0].enter_context(nc.psum_tensor("%s_%d" % (name, uid[0]), [128, n], dt)), ps=True)

        class Rot:
            def __init__(self, mk, n):
                self.b = [mk() for _ in range(n)]
                self.i = 0

            def next(self):
                b = self.b[self.i % len(self.b)]
                self.i += 1
                return b

        def V(fn, r, wr): P.op("dve", fn, r, wr)
        def A(fn, r, wr): P.op("act", fn, r, wr)
        def G(fn, r, wr): P.op("pool", fn, r, wr)
        def M(fn, r, wr): P.op("pe", fn, r, wr)
        def DMA(fn, r, wr, q="sp", final=False): P.dma(fn, r, wr, q=q, final=final)

        def barrier():
            refs = []
            for e in Prog.ENGS:
                if P.ops[e]:
                    n = len(P.ops[e]) - 1
                    if P.ops[e][n]["kind"] == "c":
                        refs.append(("eng", e, n))
                    else:
                        for m in range(n, -1, -1):
                            if P.ops[e][m]["kind"] == "c":
                                refs.append(("eng", e, m))
                                break
            for k in range(P.n_dma_sems):
                if P.dma_val[k] > 0:
                    refs.append(("dma", k, P.dma_val[k]))
            for e in Prog.ENGS:
                P.ops[e].append(dict(kind="b", fn=None, deps=list(refs), inc=False))

        cst = sb([128, 8 * 128 + 16], F32, "cstf")
        cstb = sb([128, 3 * 128 + 4 * 512], BF16, "cstb")
        IDF = cst[:, 0:128]; TRIS = cst[:, 128:256]; SUTS = cst[:, 256:384]; TRIN = cst[:, 384:512]
        NEGM = cst[:, 512:640]; ONESF = cst[:, 640:768]; SELL128 = cst[:, 768:896]; SELL8 = cst[:, 896:1024]
        BDm = cst[:, 1024:1032]; IOTA = cst[:, 1032:1033]
        IDB = cstb[:, 0:128]; TRI01 = cstb[:, 128:256]; ONESB = cstb[:, 256:384]
        CAUS = [cstb[:, 384 + 512 * r:384 + 512 * (r + 1)] for r in range(4)]
        ohb = sb([16, 16, 128], BF16, "ohb")
        with contextlib.ExitStack() as pst:
            cur[0] = pst
            stg = sb([128, NCONST], F32, "cstg")
            DMA(lambda e: e.dma_start(out=stg[:, :], in_=consts_d[:, :]), [], [stg])
            for dst, src in ((0, K_ID), (128, K_TRIS), (256, K_SUTS), (384, K_TRIN), (512, K_NEGM), (640, K_ONES),
                             (768, K_SELL128), (896, K_SELL8)):
                V(lambda e, dst=dst, src=src: e.tensor_copy(out=cst[:, dst:dst + 128], in_=stg[:, src:src + 128]), [stg], [cst])
            V(lambda e: e.tensor_copy(out=cst[:, 1024:1033], in_=stg[:, K_BD:K_BD + 9]), [stg], [cst])
            for dst, src in ((0, K_ID), (128, K_TRI01), (256, K_ONES)):
                V(lambda e, dst=dst, src=src: e.tensor_copy(out=cstb[:, dst:dst + 128], in_=stg[:, src:src + 128]), [stg], [cstb])
            V(lambda e: e.tensor_copy(out=cstb[:, 384:384 + 2048], in_=stg[:, K_CAUS:K_CAUS + 2048]), [stg], [cstb])
            V(lambda e: e.tensor_copy(out=ohb[:, :, :], in_=stg[0:16, K_OH:K_OH + 2048].rearrange("p (n s) -> p n s", n=16)), [stg], [ohb])
            barrier()
        cur[0] = gst

        NTOK = NS * TS
        hTs = sb([128, 8, NTOK], BF16, "hTs"); yTs = sb([128, 8, NTOK], BF16, "yTs")
        xs1_d = dscr("xs1_d", [NTOK, D], F32); xs1_dep = B(None)

        cast_rr = [0]

        def load_w(dst, c0, ncols, src, stgpool, kc=8, rows=128):
            step = 256
            for a in range(0, ncols, step):
                n = min(step, ncols - a)
                s = stgpool.next()
                DMA(lambda e, s=s, a=a, n=n: e.dma_start(
                    out=s[0:rows, 0:kc * n].rearrange("p (k n) -> p k n", k=kc), in_=src(a, n)), [], [s])
                eng = ("dve", "act", "dve", "act", "pool")[cast_rr[0] % 5]
                cast_rr[0] += 1
                if eng == "act":
                    fn = lambda e, s=s, a=a, n=n: e.activation(
                        out=dst[0:rows, :, c0 + a:c0 + a + n], in_=s[0:rows, 0:kc * n].rearrange("p (k n) -> p k n", k=kc), func=AF.Copy)
                else:
                    fn = lambda e, s=s, a=a, n=n: e.tensor_copy(
                        out=dst[0:rows, :, c0 + a:c0 + a + n], in_=s[0:rows, 0:kc * n].rearrange("p (k n) -> p k n", k=kc))
                P.op(eng, fn, [s], [dst])

        class WLoad:
            def __enter__(self):
                self.st = contextlib.ExitStack()
                self.prev = cur[0]
                self.st.__enter__()
                cur[0] = self.st
                self.pool = Rot(lambda: sb([128, 8 * 256], F32, "wstg"), 3)
                cur[0] = self.prev
                return self.pool

            def __exit__(self, *a):
                barrier()
                self.st.__exit__(*a)
                return False

        def w_in_src(l, col0):
            return lambda a, n: w["w_in"][l, :, col0 + a:col0 + a + n].rearrange("(k p) n -> p k n", p=128)

        def tok_proj(po, pob, hT, hTb, t0, T, W, Wb_, c0, n):
            for k in range(8):
                M(lambda e, k=k: e.matmul(po, lhsT=hT[:, k, t0:t0 + T], rhs=W[:, k, c0:c0 + n], start=(k == 0), stop=(k == 7)),
                  [hTb, Wb_], [pob])

        def feat_proj(po, pob, W, Wb_, c0, m, hT, hTb, t0, N):
            for k in range(8):
                M(lambda e, k=k: e.matmul(po, lhsT=W[:, k, c0:c0 + m], rhs=hT[:, k, t0:t0 + N], start=(k == 0), stop=(k == 7)),
                  [hTb, Wb_], [pob])

        def rstd_from_ss(ss_ap, ssb, out_ap, outb, tmp_ap, tmpb, inv_n):
            A(lambda e: e.activation(out=tmp_ap, in_=ss_ap, func=AF.Ln, scale=inv_n, bias=EPS), [ssb], [tmpb])
            A(lambda e: e.activation(out=out_ap, in_=tmp_ap, func=AF.Exp, scale=-0.5), [tmpb], [outb])

        def bvec(src_ap, n, name):
            t = sb([128, n], F32, name)
            DMA(lambda e: e.dma_start(out=t[:, :], in_=src_ap.partition_broadcast(128)), [], [t])
            return t

        def norm_tile(xt, T, gn, hTg, off, wk):
            junk, ss, lnv, rstd, hb, pT = wk["junk"], wk["ss"].next(), wk["lnv"].next(), wk["rstd"].next(), wk["hb"].next(), wk["pT"].next()
            A(lambda e: e.activation(out=junk[0:T, :], in_=xt[0:T, :], func=AF.Square, accum_out=ss[0:T, 0:1]), [xt], [junk, ss])
            rstd_from_ss(ss[0:T, 0:1], ss, rstd[0:T, 0:1], rstd, lnv[0:T, 0:1], lnv, 1.0 / D)
            V(lambda e: e.scalar_tensor_tensor(out=hb[0:T, :], in0=xt[0:T, :], scalar=rstd[0:T, 0:1], in1=gn[0:T, :],
                                               op0=ALU.mult, op1=ALU.mult), [xt, rstd, gn], [hb])
            pTv = pT[:, :].rearrange("p (k t) -> p k t", k=8)
            for k in range(8):
                M(lambda e, k=k: e.transpose(out=pTv[:, k, 0:T], in_=hb[0:T, k * 128:(k + 1) * 128], identity=IDB[0:T, 0:T]),
                  [hb, cstb], [pT])
            A(lambda e: e.activation(out=hTg[:, :, off:off + T], in_=pTv[:, :, 0:T], func=AF.Copy), [pT], [hTg])

        def norm_work():
            return dict(junk=sb([128, D], BF16, "junk"), ss=Rot(lambda: sb([128, 1], F32, "ss"), 2),
                        lnv=Rot(lambda: sb([128, 1], F32, "lnv"), 2), rstd=Rot(lambda: sb([128, 1], F32, "rstd"), 2),
                        hb=Rot(lambda: sb([128, D], BF16, "hb"), 2), pT=Rot(lambda: psum(BF16, "pTn"), 2))

        def phase_N(l):
            with contextlib.ExitStack() as pst:
                cur[0] = pst
                gn = bvec(w["norm_g"][l:l + 1, :], D, "gn")
                wk = norm_work()
                xpool = Rot(lambda: sb([128, D], F32, "xt"), 3)
                hpool = Rot(lambda: sb([128, 8, 512], BF16, "hTg"), 2)
                for g in range(NG):
                    hTg = hpool.next()
                    for j in range(4):
                        i = 4 * g + j
                        xt = xpool.next()
                        DMA(lambda e, xt=xt, i=i: e.dma_start(out=xt[:, :], in_=xp[i * 128:(i + 1) * 128, :]), [], [xt])
                        norm_tile(xt, 128, gn, hTg, j * 128, wk)
                    DMA(lambda e, hTg=hTg, g=g: e.dma_start(out=hT_d[g], in_=hTg[:, :, :]), [hTg], [hT_dep[g]])
                if cfg.sample:
                    xt = xpool.next()
                    DMA(lambda e, xt=xt: e.dma_start(out=xt[0:NTOK, :], in_=xs[:, :]), [], [xt])
                    norm_tile(xt, NTOK, gn, hTs, 0, wk)
                barrier()
            cur[0] = gst

        def branch_out(ZT_list, Zb_, Wo, Wob, Wgt, Wgtb, hTg, N, yTg, first, wk, kdim=128):
            for c in range(8):
                pg = wk["pG"].next()
                feat_proj(pg[:, 0:N], pg, Wgt, Wgtb, c * 128, 128, hTg, hTg, 0, N)
                sg = wk["sig"].next()
                A(lambda e, pg=pg, sg=sg: e.activation(out=sg[:, 0:N], in_=pg[:, 0:N], func=AF.Sigmoid), [pg], [sg])
                py = wk["pY"].next()
                nterm = len(ZT_list)
                for ti, (lhs_fn, rhs_ap) in enumerate(ZT_list):
                    M(lambda e, py=py, lhs_fn=lhs_fn, rhs_ap=rhs_ap, ti=ti, c=c: e.matmul(
                        py[:, 0:N], lhsT=lhs_fn(c), rhs=rhs_ap, start=(ti == 0), stop=(ti == nterm - 1)), [Zb_, Wob], [py])
                if first:
                    V(lambda e, py=py, sg=sg, c=c: e.tensor_tensor(out=yTg[:, c, 0:N], in0=py[:, 0:N], in1=sg[:, 0:N], op=ALU.mult),
                      [py, sg], [yTg])
                else:
                    tm = wk["tmpy"].next()
                    V(lambda e, py=py, sg=sg, tm=tm: e.tensor_tensor(out=tm[:, 0:N], in0=py[:, 0:N], in1=sg[:, 0:N], op=ALU.mult),
                      [py, sg], [tm])
                    G(lambda e, tm=tm, c=c: e.tensor_tensor(out=yTg[:, c, 0:N], in0=yTg[:, c, 0:N], in1=tm[:, 0:N], op=ALU.add),
                      [tm, yTg], [yTg])

        def branch_work(pR):
            return dict(pG=pR, pY=pR,
                        sig=Rot(lambda: sb([128, 512], F32, "sig"), 2), tmpy=Rot(lambda: sb([128, 512], F32, "tmpy"), 2))

        def load_gate_w(Wgt, l, bidx, stgpool):
            load_w(Wgt, 0, D, w_in_src(l, C_GATE + bidx * D), stgpool)

        def load_outw(Wo, name, l, stgpool):
            load_w(Wo, 0, D, lambda a, n: w[name][l, :, a:a + n].rearrange("(k p) n -> p k n", p=128), stgpool, kc=4)

        def small_bf16(src_ap, rows, cols, name):
            s = sb([rows, cols], F32, name + "_f")
            d = sb([rows, cols], BF16, name)
            DMA(lambda e: e.dma_start(out=s[:, :], in_=src_ap), [], [s])
            V(lambda e: e.tensor_copy(out=d[:, :], in_=s[:, :]), [s], [d])
            return d

        def gla_consts(l):
            return dict(wa2=small_bf16(w["gla_w_a2"][l], 16, 256, "wa2"),
                        ba=small_bf16(w["gla_b_a"][l:l + 1, :], 1, 256, "ba"),
                        gnorm=bvec(w["gla_norm"][l:l + 1, :], 128, "gnorm"))

        def gla_work():
            return dict(
                pTr=psum(BF16, "pTr"), pB=psum(F32, "pB"), pA=psum(F32, "pA"), pO=psum(F32, "pO"), pSU=psum(F32, "pSU"),
                pR=Rot(lambda: psum(F32, "pR"), 3),
                qT=sb([128, 2, 512], F32, "qTs"), kT=sb([128, 2, 512], F32, "kTs"), aT=sb([16, 512], BF16, "aTs"),
                sp=Rot(lambda: sb([128, 256], F32, "sp"), 2), E1=Rot(lambda: sb([128, 2, 128], F32, "E1"), 2),
                E2=Rot(lambda: sb([128, 2, 128], F32, "E2"), 2), E3=Rot(lambda: sb([128, 256], F32, "E3"), 2),
                qt=Rot(lambda: zsb([128, 2, 2, 128], BF16, "qtm"), 2), kt=Rot(lambda: sb([128, 2, 128], BF16, "kt"), 2),
                vtok=Rot(lambda: sb([128, 512], BF16, "vtok"), 2), kh=Rot(lambda: sb([128, 256], BF16, "kh"), 2),
                sz=Rot(lambda: sb([128, 512], F32, "sz"), 2), AT=Rot(lambda: sb([128, 4, 128], BF16, "AT"), 2),
                osq=Rot(lambda: sb([128, 512], F32, "osq"), 1), ss4=Rot(lambda: sb([128, 4], F32, "ss4"), 2),
                ln4=Rot(lambda: sb([128, 4], F32, "ln4"), 2), rs4=Rot(lambda: sb([128, 4], F32, "rs4"), 2),
                t1=Rot(lambda: sb([128, 512], F32, "t1"), 2), g1=Rot(lambda: sb([128, 512], F32, "g1"), 2),
                Zg=Rot(lambda: sb([128, 512], BF16, "Zg"), 2))

        def gla_group_prep(Wg, hTg, N, wk, hoff=0):
            for c in range(2):
                pf = wk["pR"].next()
                feat_proj(pf[:, 0:N], pf, Wg, Wg, C_GQ + c * 128, 128, hTg, hTg, hoff, N)
                A(lambda e, pf=pf, c=c: e.activation(out=wk["qT"][:, c, 0:N], in_=pf[:, 0:N], func=AF.Copy), [pf], [wk["qT"]])
                pf = wk["pR"].next()
                feat_proj(pf[:, 0:N], pf, Wg, Wg, C_GK + c * 128, 128, hTg, hTg, hoff, N)
                V(lambda e, pf=pf, c=c: e.tensor_copy(out=wk["kT"][:, c, 0:N], in_=pf[:, 0:N]), [pf], [wk["kT"]])
            pf = wk["pR"].next()
            feat_proj(pf[0:16, 0:N], pf, Wg, Wg, C_GA, 16, hTg, hTg, hoff, N)
            A(lambda e, pf=pf: e.activation(out=wk["aT"][0:16, 0:N], in_=pf[0:16, 0:N], func=AF.Copy), [pf], [wk["aT"]])

        def gla_tile(T, t0, Wg, hTg, gc, S2, S2b, ZgT, wk, hoff=None, zoff=None):
            import os
            KSTEP = int(os.environ.get('KSTEP', '99'))
            hc = t0 if hoff is None else hoff
            zc = t0 if zoff is None else zoff
            pTrb, pB, pA, pO, pSU = wk["pTr"], wk["pB"], wk["pA"], wk["pO"], wk["pSU"]
            pMisc = wk["pR"].next()
            qT, kT, aT = wk["qT"], wk["kT"], wk["aT"]
            wa2, ba, gnorm = gc["wa2"], gc["ba"], gc["gnorm"]
            M(lambda e: e.matmul(pMisc[0:T, 0:256], lhsT=aT[0:16, t0:t0 + T], rhs=wa2[0:16, :], start=True, stop=False), [aT, wa2], [pMisc])
            M(lambda e: e.matmul(pMisc[0:T, 0:256], lhsT=ONESB[0:1, 0:T], rhs=ba[0:1, :], start=False, stop=True), [cstb, ba], [pMisc])
            if KSTEP < 2:
                return
            sp = wk["sp"].next()
            A(lambda e: e.activation(out=sp[0:T, :], in_=pMisc[0:T, 0:256], func=AF.Exp, scale=-1.0), [pMisc], [sp])
            A(lambda e: e.activation(out=sp[0:T, :], in_=sp[0:T, :], func=AF.Ln, bias=1.0), [sp], [sp])
            if KSTEP < 3:
                return
            pBv = pB[:, 0:256].rearrange("p (c t) -> p c t", c=2)
            for c in range(2):
                M(lambda e, c=c: e.matmul(pBv[:, c, 0:T], lhsT=sp[0:T, c * 128:(c + 1) * 128], rhs=TRIS[0:T, 0:T], start=True, stop=True), [sp, cst], [pB])
            M(lambda e: e.matmul(pB[0:T, 256:512], lhsT=SUTS[0:T, 0:T], rhs=sp[0:T, 0:256], start=True, stop=True), [sp, cst], [pB])
            E1, E2, E3 = wk["E1"].next(), wk["E2"].next(), wk["E3"].next()
            A(lambda e: e.activation(out=E1[:, :, 0:T], in_=pBv[:, :, 0:T], func=AF.Exp), [pB], [E1])
            A(lambda e: e.activation(out=E2[:, :, 0:T], in_=pBv[:, :, 0:T], func=AF.Exp, scale=-1.0), [pB], [E2])
            A(lambda e: e.activation(out=E3[0:T, :], in_=pB[0:T, 256:512], func=AF.Exp), [pB], [E3])
            if KSTEP < 5:
                return
            qt, kt = wk["qt"].next(), wk["kt"].next()
            for hl in range(2):
                r = hl * 64
                V(lambda e, hl=hl, r=r: e.scalar_tensor_tensor(out=qt[r:r + 64, :, hl, 0:T], in0=qT[r:r + 64, :, t0:t0 + T], scalar=0.125,
                                                               in1=E1[r:r + 64, :, 0:T], op0=ALU.mult, op1=ALU.mult), [qT, E1], [qt])
            V(lambda e: e.tensor_tensor(out=kt[:, :, 0:T], in0=kT[:, :, t0:t0 + T], in1=E2[:, :, 0:T], op=ALU.mult), [kT, E2], [kt])
            if KSTEP < 6:
                return
            vtok, kh, sz = wk["vtok"].next(), wk["kh"].next(), wk["sz"].next()
            pv = wk["pR"].next()
            tok_proj(pv[0:T, 0:512], pv, hTg, hTg, hc, T, Wg, Wg, C_GV, 512)
            A(lambda e: e.activation(out=vtok[0:T, :], in_=pv[0:T, :], func=AF.Copy), [pv], [vtok])
            pk = wk["pR"].next()
            tok_proj(pk[0:T, 0:256], pk, hTg, hTg, hc, T, Wg, Wg, C_GK, 256)
            V(lambda e: e.tensor_tensor(out=kh[0:T, :], in0=pk[0:T, 0:256], in1=E3[0:T, :], op=ALU.mult), [pk, E3], [kh])
            pz = wk["pR"].next()
            tok_proj(pz[0:T, 0:512], pz, hTg, hTg, hc, T, Wg, Wg, C_GZ, 512)
            A(lambda e: e.activation(out=sz[0:T, :], in_=pz[0:T, :], func=AF.Silu), [pz], [sz])
            if KSTEP < 7:
                return
            pAv = pA[:, :].rearrange("p (h t) -> p h t", h=4)
            for h in range(4):
                c, r = h // 2, (h % 2) * 64
                M(lambda e, h=h, c=c: e.matmul(pAv[0:T, h, 0:T], lhsT=kt[:, c, 0:T], rhs=qt[:, c, h % 2, 0:T], start=True, stop=True),
                  [kt, qt], [pA])
            AT = wk["AT"].next()
            V(lambda e: e.tensor_tensor(out=AT[0:T, :, 0:T], in0=pAv[0:T, :, 0:T],
                                        in1=TRI01[0:T, 0:T].unsqueeze(1).to_broadcast([T, 4, T]), op=ALU.mult), [pA, cstb], [AT])
            if KSTEP < 8:
                return
            for h in range(4):
                c, r = h // 2, (h % 2) * 64
                M(lambda e, h=h: e.matmul(pO[0:T, h * 128:(h + 1) * 128], lhsT=AT[0:T, h, 0:T], rhs=vtok[0:T, h * 128:(h + 1) * 128],
                                          start=True, stop=False), [AT, vtok], [pO])
                M(lambda e, h=h, c=c: e.matmul(pO[0:T, h * 128:(h + 1) * 128], lhsT=qt[:, c, h % 2, 0:T], rhs=S2b[:, c, :],
                                               start=False, stop=True), [qt, S2b], [pO])
            if KSTEP < 9:
                return
            for c in range(2):
                M(lambda e, c=c: e.matmul(pSU[:, c * 256:(c + 1) * 256], lhsT=kh[0:T, c * 128:(c + 1) * 128], rhs=vtok[0:T, c * 256:(c + 1) * 256],
                                          start=True, stop=True), [kh, vtok], [pSU])
            for c in range(2):
                for hl in range(2):
                    r = hl * 64
                    V(lambda e, c=c, hl=hl, r=r: e.scalar_tensor_tensor(
                        out=S2[r:r + 64, c, :], in0=S2[r:r + 64, c, :], scalar=E1[r:r + 64, c, T - 1:T],
                        in1=pSU[r:r + 64, c * 256 + hl * 128:c * 256 + (hl + 1) * 128], op0=ALU.mult, op1=ALU.add), [S2, E1, pSU], [S2])
            G(lambda e: e.tensor_copy(out=S2b[:, :, :], in_=S2[:, :, :]), [S2], [S2b])
            if KSTEP < 10:
                return
            osq, ss4, ln4, rs4, t1, g1, Zg = [wk[k].next() for k in ("osq", "ss4", "ln4", "rs4", "t1", "g1", "Zg")]
            A(lambda e: e.activation(out=osq[0:T, :], in_=pO[0:T, :], func=AF.Square), [pO], [osq])
            V(lambda e: e.tensor_reduce(out=ss4[0:T, :], in_=osq[0:T, :].rearrange("p (h v) -> p h v", h=4), axis=AX.X, op=ALU.add), [osq], [ss4])
            rstd_from_ss(ss4[0:T, :], ss4, rs4[0:T, :], rs4, ln4[0:T, :], ln4, 1.0 / 128)
            V(lambda e: e.tensor_tensor(out=t1[0:T, :].rearrange("p (h v) -> p h v", h=4), in0=pO[0:T, :].rearrange("p (h v) -> p h v", h=4),
                                        in1=rs4[0:T, :].unsqueeze(2).to_broadcast([T, 4, 128]), op=ALU.mult), [pO, rs4], [t1])
            G(lambda e: e.tensor_tensor(out=g1[0:T, :].rearrange("p (h v) -> p h v", h=4), in0=sz[0:T, :].rearrange("p (h v) -> p h v", h=4),
                                        in1=gnorm[0:T, :].unsqueeze(1).to_broadcast([T, 4, 128]), op=ALU.mult), [sz, gnorm], [g1])
            V(lambda e: e.tensor_tensor(out=Zg[0:T, :], in0=t1[0:T, :], in1=g1[0:T, :], op=ALU.mult), [t1, g1], [Zg])
            if KSTEP < 11:
                return
            pTr = pTrb[:, 0:512].rearrange("p (k t) -> p k t", k=4)
            for k4 in range(4):
                M(lambda e, k4=k4: e.transpose(out=pTr[:, k4, 0:T], in_=Zg[0:T, k4 * 128:(k4 + 1) * 128], identity=IDB[0:T, 0:T]), [Zg, cstb], [pTrb])
            A(lambda e: e.activation(out=ZgT[:, :, zc:zc + T], in_=pTr[:, :, 0:T], func=AF.Copy), [pTrb], [ZgT])

        def phase_G(l, first):
            with contextlib.ExitStack() as pst:
                cur[0] = pst
                Wg = sb([128, 8, 1552], BF16, "Wg"); Wgt = sb([128, 8, D], BF16, "Wgt"); Wo = sb([128, 4, D], BF16, "Wo")
                with WLoad() as stg:
                    load_w(Wg, 0, 1552, w_in_src(l, C_GQ), stg)
                    load_gate_w(Wgt, l, 0, stg)
                    load_outw(Wo, "gla_w_out", l, stg)
                gc = gla_consts(l)
                wk = gla_work(); bw = branch_work(wk["pR"])
                S2 = sb([128, 2, 128], F32, "S2"); S2b = sb([128, 2, 128], BF16, "S2b")
                V(lambda e: e.memset(S2[:, :, :], 0.0), [], [S2]); V(lambda e: e.memset(S2b[:, :, :], 0.0), [], [S2b])
                hpool = Rot(lambda: sb([128, 8, 512], BF16, "hTg"), 2)
                ypool = Rot(lambda: sb([128, 8, 512], BF16, "yTg"), 2)
                zpool = Rot(lambda: sb([128, 4, 512], BF16, "ZgT"), 2)
                for g in range(NG):
                    hTg = hpool.next(); yTg = ypool.next(); ZgT = zpool.next()
                    DMA(lambda e, hTg=hTg, g=g: e.dma_start(out=hTg[:, :, :], in_=hT_d[g]), [hT_dep[g]], [hTg])
                    if not first:
                        DMA(lambda e, yTg=yTg, g=g: e.dma_start(out=yTg[:, :, :], in_=yT_d[g]), [yT_dep[g]], [yTg])
                    gla_group_prep(Wg, hTg, 512, wk)
                    for j in range(4):
                        gla_tile(128, j * 128, Wg, hTg, gc, S2, S2b, ZgT, wk)
                    terms = [((lambda c, k4=k4: Wo[:, k4, c * 128:(c + 1) * 128]), ZgT[:, k4, :]) for k4 in range(4)]
                    if STOP >= 4:
                        branch_out(terms, ZgT, Wo, Wo, Wgt, Wgt, hTg, 512, yTg, first, bw)
                    DMA(lambda e, yTg=yTg, g=g: e.dma_start(out=yT_d[g], in_=yTg[:, :, :]), [yTg], [yT_dep[g]])
                DMA(lambda e: e.dma_start(out=p_gla[l].rearrange("(c hl) d v -> (hl d) c v", hl=2), in_=S2[:, :, :]), [S2], [p_gla], final=True)
                if cfg.sample:
                    ZgTs = zpool.next()
                    for sq_ in range(NS):
                        DMA(lambda e, sq_=sq_: e.dma_start(out=S2[:, :, :], in_=sg_in[l, sq_].rearrange("(c hl) d v -> (hl d) c v", hl=2)), [], [S2])
                        G(lambda e: e.tensor_copy(out=S2b[:, :, :], in_=S2[:, :, :]), [S2], [S2b])
                        gla_group_prep(Wg, hTs, TS, wk, hoff=sq_ * TS)
                        gla_tile(TS, 0, Wg, hTs, gc, S2, S2b, ZgTs, wk, hoff=sq_ * TS, zoff=sq_ * TS)
                        DMA(lambda e, sq_=sq_: e.dma_start(out=s_gla[l, sq_].rearrange("(c hl) d v -> (hl d) c v", hl=2), in_=S2[:, :, :]), [S2], [], final=True)
                    terms = [((lambda c, k4=k4: Wo[:, k4, c * 128:(c + 1) * 128]), ZgTs[:, k4, 0:NTOK]) for k4 in range(4)]
                    branch_out(terms, ZgTs, Wo, Wo, Wgt, Wgt, hTs, NTOK, yTs, first, bw)
                barrier()
            cur[0] = gst

        def phase_O(l):
            last = (l == DEPTH - 1)
            with contextlib.ExitStack() as pst:
                cur[0] = pst
                Wout = sb([128, 8, D], BF16, "Wout")
                with WLoad() as stg:
                    load_w(Wout, 0, D, lambda a, n: w["w_out"][l, :, a:a + n].rearrange("(k p) n -> p k n", p=128), stg)
                pR = Rot(lambda: psum(F32, "pRo"), 4)
                xpool = Rot(lambda: sb([128, D], F32, "xt"), 3)
                xnpool = Rot(lambda: sb([128, D], F32, "xn"), 3)
                ypool = Rot(lambda: sb([128, 8, 512], BF16, "yTg"), 2)
                if not last:
                    gn = bvec(w["norm_g"][l + 1:l + 2, :], D, "gn")
                    wkn = norm_work()
                    hpool = Rot(lambda: sb([128, 8, 512], BF16, "hTg"), 2)
                for g in range(NG):
                    yTg = ypool.next()
                    DMA(lambda e, yTg=yTg, g=g: e.dma_start(out=yTg[:, :, :], in_=yT_d[g]), [yT_dep[g]], [yTg])
                    if not last:
                        hTg = hpool.next()
                    for j in range(4):
                        i = 4 * g + j
                        xt = xpool.next(); xn = xnpool.next()
                        if l == 0:
                            DMA(lambda e, xt=xt, i=i: e.dma_start(out=xt[:, :], in_=xp[i * 128:(i + 1) * 128, :]), [], [xt])
                        else:
                            DMA(lambda e, xt=xt, i=i: e.dma_start(out=xt[:, :], in_=x1_d[i * 128:(i + 1) * 128, :]), [x1_dep[i]], [xt])
                        for half in range(2):
                            py = pR.next()
                            for c in range(8):
                                M(lambda e, py=py, c=c, j=j, half=half, yTg=yTg: e.matmul(
                                    py[:, 0:512], lhsT=yTg[:, c, j * 128:(j + 1) * 128], rhs=Wout[:, c, half * 512:(half + 1) * 512],
                                    start=(c == 0), stop=(c == 7)), [yTg, Wout], [py])
                            V(lambda e, py=py, xt=xt, xn=xn, half=half: e.tensor_tensor(
                                out=xn[:, half * 512:(half + 1) * 512], in0=py[:, 0:512], in1=xt[:, half * 512:(half + 1) * 512], op=ALU.add),
                              [py, xt], [xn])
                        if last:
                            DMA(lambda e, xn=xn, i=i: e.dma_start(out=yp[i * 128:(i + 1) * 128, :], in_=xn[:, :]), [xn], [], final=True)
                        else:
                            DMA(lambda e, xn=xn, i=i: e.dma_start(out=x1_d[i * 128:(i + 1) * 128, :], in_=xn[:, :]), [xn], [x1_dep[i]])
                            norm_tile(xn, 128, gn, hTg, j * 128, wkn)
                    if not last:
                        DMA(lambda e, hTg=hTg, g=g: e.dma_start(out=hT_d[g], in_=hTg[:, :, :]), [hTg], [hT_dep[g]])
                if cfg.sample:
                    xt = xpool.next(); xn = xnpool.next()
                    if l == 0:
                        DMA(lambda e, xt=xt: e.dma_start(out=xt[0:NTOK, :], in_=xs[:, :]), [], [xt])
                    else:
                        DMA(lambda e, xt=xt: e.dma_start(out=xt[0:NTOK, :], in_=xs1_d[:, :]), [xs1_dep], [xt])
                    for half in range(2):
                        py = pR.next()
                        for c in range(8):
                            M(lambda e, py=py, c=c, half=half: e.matmul(py[0:NTOK, 0:512], lhsT=yTs[:, c, 0:NTOK], rhs=Wout[:, c, half * 512:(half + 1) * 512],
                                                                        start=(c == 0), stop=(c == 7)), [yTs, Wout], [py])
                        V(lambda e, py=py, xt=xt, xn=xn, half=half: e.tensor_tensor(out=xn[0:NTOK, half * 512:(half + 1) * 512], in0=py[0:NTOK, 0:512],
                                                                                   in1=xt[0:NTOK, half * 512:(half + 1) * 512], op=ALU.add), [py, xt], [xn])
                    if last:
                        DMA(lambda e, xn=xn: e.dma_start(out=ys[:, :], in_=xn[0:NTOK, :]), [xn], [], final=True)
                    else:
                        DMA(lambda e, xn=xn: e.dma_start(out=xs1_d[:, :], in_=xn[0:NTOK, :]), [xn], [xs1_dep])
                        norm_tile(xn, NTOK, gn, hTs, 0, wkn)
                barrier()
            cur[0] = gst


        RS128 = 128.0 ** -0.5

        def ml_consts(l):
            cw = sb([128, 4, 4], F32, "cw"); cb = sb([128, 4], F32, "cb")
            for c in range(4):
                for j in range(4):
                    DMA(lambda e, c=c, j=j: e.dma_start(out=cw[:, c, j:j + 1], in_=w["ml_conv_w"][l, j, c * 128:(c + 1) * 128].rearrange("(p o) -> p o", o=1)), [], [cw])
                DMA(lambda e, c=c: e.dma_start(out=cb[:, c:c + 1], in_=w["ml_conv_b"][l, c * 128:(c + 1) * 128].rearrange("(p o) -> p o", o=1)), [], [cb])
            return dict(cw=cw, cb=cb, bi=bvec(w["ml_b_i"][l:l + 1, :], 4, "bi"), bf=bvec(w["ml_b_f"][l:l + 1, :], 4, "bf"),
                        mlnorm=bvec(w["ml_norm"][l:l + 1, :], 128, "mlnorm"), skip=bvec(w["ml_skip"][l:l + 1, :], 512, "skip"))

        def ml_weights_alloc():
            return sb([128, 4, 128], BF16, "wq"), sb([128, 4, 128], BF16, "wk"), sb([128, 4, 128], BF16, "wv")

        def ml_weights(l, stg, wts):
            wq, wk_, wv = wts
            for dst, nm in ((wq, "ml_w_q"), (wk_, "ml_w_k"), (wv, "ml_w_v")):
                load_w(dst, 0, 128, (lambda a, n, nm=nm: w[nm][l, :, :, a:a + n].rearrange("h d n -> d h n")), stg, kc=4)
            return wq, wk_, wv

        def ml_work():
            def ones_aug():
                b = sb([128, 4, 128], BF16, "vaug")
                return b
            return dict(
                pR=Rot(lambda: psum(F32, "pR"), 2), pK=psum(F32, "pK"), pBF=psum(BF16, "pBF"), pX1=psum(F32, "pX1"),
                pX2=psum(F32, "pX2"), pX3=psum(F32, "pX3"), pS=psum(F32, "pS"),
                xT=sb([128, 4, 3 + 512], F32, "xTe"), xTb=sb([128, 4, 512], BF16, "xTb"), mcT=sb([128, 4, 512], BF16, "mcT"),
                qTm=sb([128, 4, 512], BF16, "qTm"), kTm=sb([128, 4, 512], BF16, "kTm"),
                acc=Rot(lambda: sb([128, 512], F32, "acc"), 2),
                vaug=Rot(lambda: sb([128, 4, 128], BF16, "vaug"), 2),
                g4=Rot(lambda: sb([128, 4], F32, "g4"), 24), t12=Rot(lambda: sb([128, 12], F32, "t12"), 2),
                dg=Rot(lambda: sb([128, 4, 128], F32, "dg"), 1), Dm=Rot(lambda: sb([128, 4, 128], F32, "Dm"), 2),
                Wt=Rot(lambda: sb([128, 4, 128], F32, "Wt"), 1), Sqk=Rot(lambda: sb([128, 4, 128], BF16, "Sqk"), 2),
                SqkT=Rot(lambda: sb([128, 4, 128], BF16, "SqkT"), 2), P1s=Rot(lambda: sb([128, 512], F32, "P1s"), 1),
                f5=Rot(lambda: sb([128, 512], F32, "f5"), 6), kw=Rot(lambda: sb([128, 4, 128], BF16, "kw"), 2),
                Zm=Rot(lambda: sb([128, 512], BF16, "Zm"), 2))

        def ml_group_prep(Wm, hTg, N, mc_, wts, wk, hoff=0):
            import os
            if int(os.environ.get('KSTEP', '99')) < 0:
                return
            KSUB = int(os.environ.get('KSUB', '99'))
            wq, wk_, wv = wts
            xT, xTb, mcT, qTm, kTm = wk["xT"], wk["xTb"], wk["mcT"], wk["qTm"], wk["kTm"]
            cw, cb = mc_["cw"], mc_["cb"]
            for c in range(4):
                pf = wk["pR"].next()
                feat_proj(pf[:, 0:N], pf, Wm, Wm, c * 128, 128, hTg, hTg, hoff, N)
                KVAR = os.environ.get('KVAR', '')
                if KVAR != 'noact':
                    A(lambda e, pf=pf, c=c: e.activation(out=xT[:, c, 3:3 + N], in_=pf[:, 0:N], func=AF.Copy), [pf], [xT])
                if KVAR != 'nodve':
                    V(lambda e, pf=pf, c=c: e.tensor_copy(out=xTb[:, c, 0:N], in_=pf[:, 0:N]), [pf], [xTb])
                if KSUB < 2:
                    continue
                acc = wk["acc"].next()
                V(lambda e, c=c, acc=acc: e.tensor_scalar(out=acc[:, 0:N], in0=xT[:, c, 0:N], scalar1=cw[:, c, 0:1], scalar2=cb[:, c:c + 1],
                                                          op0=ALU.mult, op1=ALU.add), [xT, cw, cb], [acc])
                for j in range(1, 4):
                    V(lambda e, c=c, j=j, acc=acc: e.scalar_tensor_tensor(out=acc[:, 0:N], in0=xT[:, c, j:j + N], scalar=cw[:, c, j:j + 1], in1=acc[:, 0:N],
                                                                         op0=ALU.mult, op1=ALU.add), [xT, cw, acc], [acc])
                if KSUB < 3:
                    continue
                A(lambda e, c=c, acc=acc: e.activation(out=mcT[:, c, 0:N], in_=acc[:, 0:N], func=AF.Silu), [acc], [mcT])
            if KSUB < 4:
                return
            for h in range(4):
                pf = wk["pR"].next()
                M(lambda e, pf=pf, h=h: e.matmul(pf[:, 0:N], lhsT=wq[:, h, :], rhs=mcT[:, h, 0:N], start=True, stop=True), [wq, mcT], [pf])
                A(lambda e, pf=pf, h=h: e.activation(out=qTm[:, h, 0:N], in_=pf[:, 0:N], func=AF.Copy), [pf], [qTm])
                pf = wk["pR"].next()
                M(lambda e, pf=pf, h=h: e.matmul(pf[:, 0:N], lhsT=wk_[:, h, :], rhs=mcT[:, h, 0:N], start=True, stop=True), [wk_, mcT], [pf])
                A(lambda e, pf=pf, h=h: e.activation(out=kTm[:, h, 0:N], in_=pf[:, 0:N], func=AF.Identity, scale=RS128), [pf], [kTm])

        def ml_carry(N, wk):
            xT = wk["xT"]
            V(lambda e: e.tensor_copy(out=xT[:, :, 0:3], in_=xT[:, :, N:N + 3]), [xT], [xT])

        def ml_tile(T, t0, Wm, hTg, mc_, wts, st, ZmT, wk, hoff=None, zoff=None):
            import os
            KSTEP = int(os.environ.get('KSTEP', '99'))
            if KSTEP < 1:
                return
            hc = t0 if hoff is None else hoff
            zc = t0 if zoff is None else zoff
            wq, wk_, wv = wts
            CT, CTb, nT, nb, mprev = st["CT"], st["CTb"], st["nT"], st["nb"], st["mprev"]
            xTb, mcT, qTm, kTm = wk["xTb"], wk["mcT"], wk["qTm"], wk["kTm"]
            pK, pBF, pX1, pX2, pX3, pS = wk["pK"], wk["pBF"], wk["pX1"], wk["pX2"], wk["pX3"], wk["pS"]
            g4 = lambda: wk["g4"].next()
            SELL = SELL128 if T == 128 else SELL8
            v3 = lambda ap: ap.rearrange("p (h v) -> p h v", h=4)
            pif = wk["pR"].next()
            tok_proj(pif[0:T, 0:8], pif, hTg, hTg, hc, T, Wm, Wm, 512, 8)
            uf, ig = g4(), g4()
            V(lambda e: e.tensor_tensor(out=uf[0:T, :], in0=pif[0:T, 4:8], in1=mc_["bf"][0:T, :], op=ALU.add), [pif, mc_["bf"]], [uf])
            V(lambda e: e.tensor_tensor(out=ig[0:T, :], in0=pif[0:T, 0:4], in1=mc_["bi"][0:T, :], op=ALU.add), [pif, mc_["bi"]], [ig])
            spf = g4()
            A(lambda e: e.activation(out=spf[0:T, :], in_=uf[0:T, :], func=AF.Exp, scale=-1.0), [uf], [spf])
            A(lambda e: e.activation(out=spf[0:T, :], in_=spf[0:T, :], func=AF.Ln, bias=1.0), [spf], [spf])
            for h in range(4):
                M(lambda e, h=h: e.matmul(pK[0:T, h * 128:(h + 1) * 128], lhsT=mcT[:, h, t0:t0 + T], rhs=wk_[:, h, :], start=True, stop=True), [mcT, wk_], [pK])
            pv = wk["pR"].next()
            for h in range(4):
                M(lambda e, h=h: e.matmul(pv[0:T, h * 128:(h + 1) * 128], lhsT=xTb[:, h, t0:t0 + T], rhs=wv[:, h, :], start=True, stop=True), [xTb, wv], [pv])
            vaug = wk["vaug"].next()
            A(lambda e: e.activation(out=vaug[0:T, :, :], in_=v3(pv[0:T, :]), func=AF.Copy), [pv], [vaug])
            pmc = pBF[:, 0:512].rearrange("p (c f) -> p c f", c=4)
            pTr = pBF[:, 512:1024].rearrange("p (h t) -> p h t", h=4)
            for c in range(4):
                M(lambda e, c=c: e.transpose(out=pmc[0:T, c, :], in_=mcT[:, c, t0:t0 + T], identity=IDB[:, :]), [mcT, cstb], [pBF])
            mcs = wk["f5"].next()
            V(lambda e: e.tensor_tensor(out=mcs[0:T, :], in0=pBF[0:T, 0:512], in1=mc_["skip"][0:T, :], op=ALU.mult), [pBF, mc_["skip"]], [mcs])
            if KSTEP < 3:
                return
            M(lambda e: e.matmul(pS[0:T, 0:4], lhsT=TRIN[0:T, 0:T], rhs=spf[0:T, 0:4], start=True, stop=True), [spf, cst], [pS])
            Fs, r = g4(), g4()
            V(lambda e: e.tensor_copy(out=Fs[0:T, :], in_=pS[0:T, 0:4]), [pS], [Fs])
            V(lambda e: e.tensor_tensor(out=r[0:T, :], in0=ig[0:T, :], in1=Fs[0:T, :], op=ALU.subtract), [ig, Fs], [r])
            if KSTEP < 4:
                return
            dg = wk["dg"].next()
            V(lambda e: e.tensor_tensor(out=dg[0:T, :, 0:T], in0=IDF[0:T, 0:T].unsqueeze(1).to_broadcast([T, 4, T]),
                                        in1=r[0:T, :].unsqueeze(2).to_broadcast([T, 4, T]), op=ALU.mult), [cst, r], [dg])
            pRb = pX1[:, :].rearrange("p (h t) -> p h t", h=4)
            for h in range(4):
                M(lambda e, h=h: e.matmul(pRb[0:T, h, 0:T], lhsT=ONESF[0:T, 0:T], rhs=dg[0:T, h, 0:T], start=True, stop=True), [dg, cst], [pX1])
            if KSTEP < 5:
                return
            Dm = wk["Dm"].next()
            for h in range(4):
                V(lambda e, h=h: e.scalar_tensor_tensor(out=Dm[0:T, h, 0:T], in0=pRb[0:T, h, 0:T], scalar=Fs[0:T, h:h + 1], in1=NEGM[0:T, 0:T],
                                                        op0=ALU.add, op1=ALU.add), [pX1, Fs, cst], [Dm])
            mx, lin, t12, negm, win, enm = g4(), g4(), wk["t12"].next(), g4(), g4(), g4()
            V(lambda e: e.tensor_reduce(out=mx[0:T, :], in_=Dm[0:T, :, 0:T], axis=AX.X, op=ALU.max), [Dm], [mx])
            V(lambda e: e.tensor_tensor(out=lin[0:T, :], in0=Fs[0:T, :], in1=mprev[0:T, :], op=ALU.add), [Fs, mprev], [lin])
            V(lambda e: e.tensor_tensor(out=t12[0:T, 8:12], in0=lin[0:T, :], in1=mx[0:T, :], op=ALU.max), [lin, mx], [t12])
            V(lambda e: e.tensor_scalar(out=negm[0:T, :], in0=t12[0:T, 8:12], scalar1=-1.0, scalar2=None, op0=ALU.mult), [t12], [negm])
            V(lambda e: e.tensor_tensor(out=t12[0:T, 4:8], in0=lin[0:T, :], in1=t12[0:T, 8:12], op=ALU.subtract), [lin, t12], [t12])
            V(lambda e: e.tensor_tensor(out=t12[0:T, 0:4], in0=Fs[0:T, :], in1=t12[0:T, 8:12], op=ALU.subtract), [Fs, t12], [t12])
            A(lambda e: e.activation(out=win[0:T, :], in_=t12[0:T, 4:8], func=AF.Exp), [t12], [win])
            A(lambda e: e.activation(out=enm[0:T, :], in_=negm[0:T, :], func=AF.Exp), [negm], [enm])
            if KSTEP < 7:
                return
            Wt = wk["Wt"].next()
            for h in range(4):
                A(lambda e, h=h: e.activation(out=Wt[0:T, h, 0:T], in_=Dm[0:T, h, 0:T], func=AF.Exp, bias=negm[0:T, h:h + 1]), [Dm, negm], [Wt])
            if KSTEP < 8:
                return
            pQK = pX1[:, :].rearrange("p (h t) -> p h t", h=4)
            for h in range(4):
                M(lambda e, h=h: e.matmul(pQK[0:T, h, 0:T], lhsT=qTm[:, h, t0:t0 + T], rhs=kTm[:, h, t0:t0 + T], start=True, stop=True), [qTm, kTm], [pX1])
            Sqk = wk["Sqk"].next()
            V(lambda e: e.tensor_tensor(out=Sqk[0:T, :, 0:T], in0=pQK[0:T, :, 0:T], in1=Wt[0:T, :, 0:T], op=ALU.mult), [pX1, Wt], [Sqk])
            rs = g4()
            V(lambda e: e.tensor_reduce(out=rs[0:T, :], in_=Sqk[0:T, :, 0:T], axis=AX.X, op=ALU.add), [Sqk], [rs])
            for h in range(4):
                M(lambda e, h=h: e.transpose(out=pTr[0:T, h, 0:T], in_=Sqk[0:T, h, 0:T], identity=IDB[0:T, 0:T]), [Sqk, cstb], [pBF])
            SqkT = wk["SqkT"].next()
            A(lambda e: e.activation(out=SqkT[0:T, :, 0:T], in_=pTr[0:T, :, 0:T], func=AF.Copy), [pBF], [SqkT])
            if KSTEP < 10:
                return
            for h in range(4):
                M(lambda e, h=h: e.matmul(pX2[0:T, h * 128:(h + 1) * 128], lhsT=SqkT[0:T, h, 0:T], rhs=vaug[0:T, h, :], start=True, stop=True), [SqkT, vaug], [pX2])
            for h in range(4):
                M(lambda e, h=h: e.matmul(pX3[0:T, h * 128:(h + 1) * 128], lhsT=qTm[:, h, t0:t0 + T], rhs=CTb[:, h, :], start=True, stop=True), [qTm, CTb], [pX3])
                M(lambda e, h=h: e.matmul(pS[0:T, 4 + h:5 + h], lhsT=qTm[:, h, t0:t0 + T], rhs=nb[:, h:h + 1], start=True, stop=True), [qTm, nb], [pS])
            P1s = wk["P1s"].next()
            A(lambda e: e.activation(out=P1s[0:T, :], in_=pX2[0:T, :], func=AF.Copy), [pX2], [P1s])
            comb = wk["f5"].next()
            for h in range(4):
                V(lambda e, h=h: e.scalar_tensor_tensor(out=comb[0:T, h * 128:(h + 1) * 128], in0=pX3[0:T, h * 128:(h + 1) * 128], scalar=win[0:T, h:h + 1],
                                                        in1=P1s[0:T, h * 128:(h + 1) * 128], op0=ALU.mult, op1=ALU.add), [pX3, win, P1s], [comb])
            den, d2, rd = g4(), g4(), g4()
            V(lambda e: e.tensor_tensor(out=den[0:T, :], in0=pS[0:T, 4:8], in1=win[0:T, :], op=ALU.mult), [pS, win], [den])
            V(lambda e: e.tensor_tensor(out=den[0:T, :], in0=den[0:T, :], in1=rs[0:T, :], op=ALU.add), [den, rs], [den])
            V(lambda e: e.tensor_scalar(out=d2[0:T, :], in0=den[0:T, :], scalar1=-1.0, scalar2=None, op0=ALU.mult), [den], [d2])
            V(lambda e: e.tensor_tensor(out=d2[0:T, :], in0=d2[0:T, :], in1=den[0:T, :], op=ALU.max), [d2, den], [d2])
            V(lambda e: e.tensor_tensor(out=d2[0:T, :], in0=d2[0:T, :], in1=enm[0:T, :], op=ALU.max), [d2, enm], [d2])
            V(lambda e: e.reciprocal(out=rd[0:T, :], in_=d2[0:T, :]), [d2], [rd])
            if KSTEP < 12:
                return
            po = wk["pR"].next()
            tok_proj(po[0:T, 0:512], po, hTg, hTg, hc, T, Wm, Wm, 520, 512)
            sgo = wk["f5"].next()
            A(lambda e: e.activation(out=sgo[0:T, :], in_=po[0:T, :], func=AF.Sigmoid), [po], [sgo])
            pz = wk["pR"].next()
            tok_proj(pz[0:T, 0:512], pz, hTg, hTg, hc, T, Wm, Wm, 1032, 512)
            szm = wk["f5"].next()
            A(lambda e: e.activation(out=szm[0:T, :], in_=pz[0:T, :], func=AF.Silu), [pz], [szm])
            t2 = wk["f5"].next()
            V(lambda e: e.tensor_tensor(out=v3(t2[0:T, :]), in0=v3(comb[0:T, :]), in1=rd[0:T, :].unsqueeze(2).to_broadcast([T, 4, 128]), op=ALU.mult), [comb, rd], [t2])
            G(lambda e: e.tensor_tensor(out=t2[0:T, :], in0=t2[0:T, :], in1=sgo[0:T, :], op=ALU.mult), [t2, sgo], [t2])
            sq = wk["f5"].next()
            ss4, ln4, rs4 = g4(), g4(), g4()
            A(lambda e: e.activation(out=sq[0:T, :], in_=t2[0:T, :], func=AF.Square), [t2], [sq])
            V(lambda e: e.tensor_reduce(out=ss4[0:T, :], in_=v3(sq[0:T, :]), axis=AX.X, op=ALU.add), [sq], [ss4])
            rstd_from_ss(ss4[0:T, :], ss4, rs4[0:T, :], rs4, ln4[0:T, :], ln4, 1.0 / 128)
            V(lambda e: e.tensor_tensor(out=v3(t2[0:T, :]), in0=v3(t2[0:T, :]), in1=rs4[0:T, :].unsqueeze(2).to_broadcast([T, 4, 128]), op=ALU.mult), [t2, rs4], [t2])
            G(lambda e: e.tensor_tensor(out=v3(t2[0:T, :]), in0=v3(t2[0:T, :]), in1=mc_["mlnorm"][0:T, :].unsqueeze(1).to_broadcast([T, 4, 128]), op=ALU.mult),
              [t2, mc_["mlnorm"]], [t2])
            G(lambda e: e.tensor_tensor(out=t2[0:T, :], in0=t2[0:T, :], in1=mcs[0:T, :], op=ALU.add), [t2, mcs], [t2])
            Zm = wk["Zm"].next()
            V(lambda e: e.tensor_tensor(out=Zm[0:T, :], in0=t2[0:T, :], in1=szm[0:T, :], op=ALU.mult), [t2, szm], [Zm])
            for k4 in range(4):
                M(lambda e, k4=k4: e.transpose(out=pTr[:, k4, 0:T], in_=Zm[0:T, k4 * 128:(k4 + 1) * 128], identity=IDB[0:T, 0:T]), [Zm, cstb], [pBF])
            A(lambda e: e.activation(out=ZmT[:, :, zc:zc + T], in_=pTr[:, :, 0:T], func=AF.Copy), [pBF], [ZmT])
            if KSTEP < 14:
                return
            M(lambda e: e.matmul(pS[:, 8:20], lhsT=SELL[0:T, :], rhs=t12[0:T, 0:12], start=True, stop=True), [t12, cst], [pS])
            wend, dec = g4(), g4()
            V(lambda e: e.tensor_tensor(out=wend[0:T, :], in0=r[0:T, :], in1=pS[0:T, 8:12], op=ALU.add), [r, pS], [wend])
            A(lambda e: e.activation(out=wend[0:T, :], in_=wend[0:T, :], func=AF.Exp), [wend], [wend])
            A(lambda e: e.activation(out=dec[:, :], in_=pS[:, 12:16], func=AF.Exp), [pS], [dec])
            V(lambda e: e.tensor_copy(out=mprev[:, :], in_=pS[:, 16:20]), [pS], [mprev])
            kw = wk["kw"].next()
            V(lambda e: e.scalar_tensor_tensor(out=kw[0:T, :, :], in0=v3(pK[0:T, :]), scalar=RS128, in1=wend[0:T, :].unsqueeze(2).to_broadcast([T, 4, 128]),
                                               op0=ALU.mult, op1=ALU.mult), [pK, wend], [kw])
            for h in range(4):
                M(lambda e, h=h: e.matmul(pX2[:, h * 128:(h + 1) * 128], lhsT=kw[0:T, h, :], rhs=vaug[0:T, h, :], start=True, stop=True), [kw, vaug], [pX2])
                M(lambda e, h=h: e.matmul(pS[:, 20 + h:21 + h], lhsT=kw[0:T, h, :], rhs=ONESB[0:T, 0:1], start=True, stop=True), [kw, cstb], [pS])
            for h in range(4):
                V(lambda e, h=h: e.scalar_tensor_tensor(out=CT[:, h, :], in0=CT[:, h, :], scalar=dec[:, h:h + 1], in1=pX2[:, h * 128:(h + 1) * 128],
                                                        op0=ALU.mult, op1=ALU.add), [CT, dec, pX2], [CT])
            V(lambda e: e.tensor_tensor(out=nT[:, :], in0=nT[:, :], in1=dec[:, :], op=ALU.mult), [nT, dec], [nT])
            V(lambda e: e.tensor_tensor(out=nT[:, :], in0=nT[:, :], in1=pS[:, 20:24], op=ALU.add), [nT, pS], [nT])
            G(lambda e: e.tensor_copy(out=CTb[:, :, :], in_=CT[:, :, :]), [CT], [CTb])
            G(lambda e: e.tensor_copy(out=nb[:, :], in_=nT[:, :]), [nT], [nb])

        def ml_state_out(st, wk, o_mc, o_mn, o_mm, o_conv, fin=True):
            CT, nT, mprev, xT = st["CT"], st["nT"], st["mprev"], wk["xT"]
            pX2, pS, pX3 = wk["pX2"], wk["pS"], wk["pX3"]
            for h in range(4):
                M(lambda e, h=h: e.transpose(out=pX2[:, h * 128:(h + 1) * 128], in_=CT[:, h, :], identity=IDF[:, :]), [CT, cst], [pX2])
            co = wk["f5"].next()
            V(lambda e: e.tensor_copy(out=co[:, :], in_=pX2[:, :]), [pX2], [co])
            DMA(lambda e: e.dma_start(out=o_mc.rearrange("h v d -> v h d"), in_=co[:, :].rearrange("p (h d) -> p h d", h=4)), [co], [], final=fin)
            M(lambda e: e.transpose(out=pS[0:4, 0:128], in_=nT[:, 0:4], identity=IDF[:, :]), [nT, cst], [pS])
            no = wk["f5"].next()
            V(lambda e: e.tensor_copy(out=no[0:4, 0:128], in_=pS[0:4, 0:128]), [pS], [no])
            DMA(lambda e: e.dma_start(out=o_mn, in_=no[0:4, 0:128]), [no], [], final=fin)
            DMA(lambda e: e.dma_start(out=o_mm, in_=mprev[0:1, 0:4]), [mprev], [], final=fin)
            for c in range(4):
                M(lambda e, c=c: e.transpose(out=pX3[0:3, c * 128:(c + 1) * 128], in_=xT[:, c, 0:3], identity=IDF[:, :]), [xT, cst], [pX3])
            cvo = wk["f5"].next()
            V(lambda e: e.tensor_copy(out=cvo[0:3, :], in_=pX3[0:3, :]), [pX3], [cvo])
            DMA(lambda e: e.dma_start(out=o_conv, in_=cvo[0:3, :]), [cvo], [], final=fin)

        def ml_state_new(zero=True):
            st = dict(CT=sb([128, 4, 128], F32, "CT"), CTb=sb([128, 4, 128], BF16, "CTb"), nT=sb([128, 4], F32, "nT"),
                      nb=sb([128, 4], BF16, "nb"), mprev=sb([128, 4], F32, "mprev"))
            if zero:
                for k_, b in st.items():
                    nd = len(b.t.shape)
                    V(lambda e, b=b, nd=nd: e.memset(b[(slice(None),) * nd], 0.0), [], [b])
            return st

        def phase_M(l, first):
            with contextlib.ExitStack() as pst:
                cur[0] = pst
                Wm = sb([128, 8, 1544], BF16, "Wm"); Wgt = sb([128, 8, D], BF16, "Wgt"); Wo = sb([128, 4, D], BF16, "Wo")
                wts = ml_weights_alloc()
                with WLoad() as stg:
                    load_w(Wm, 0, 1544, w_in_src(l, C_MX), stg)
                    load_gate_w(Wgt, l, 1, stg)
                    load_outw(Wo, "ml_w_out", l, stg)
                    ml_weights(l, stg, wts)
                mc_ = ml_consts(l)
                wk = ml_work(); bw = branch_work(wk["pR"])
                st = ml_state_new(True)
                V(lambda e: e.memset(wk["xT"][:, :, 0:3], 0.0), [], [wk["xT"]])
                hpool = Rot(lambda: sb([128, 8, 512], BF16, "hTg"), 2)
                ypool = Rot(lambda: sb([128, 8, 512], BF16, "yTg"), 2)
                zpool = Rot(lambda: sb([128, 4, 512], BF16, "ZmT"), 2)
                for g in range(NG):
                    hTg = hpool.next(); yTg = ypool.next(); ZmT = zpool.next()
                    DMA(lambda e, hTg=hTg, g=g: e.dma_start(out=hTg[:, :, :], in_=hT_d[g]), [hT_dep[g]], [hTg])
                    if not first:
                        DMA(lambda e, yTg=yTg, g=g: e.dma_start(out=yTg[:, :, :], in_=yT_d[g]), [yT_dep[g]], [yTg])
                    ml_group_prep(Wm, hTg, 512, mc_, wts, wk)
                    for j in range(4):
                        ml_tile(128, j * 128, Wm, hTg, mc_, wts, st, ZmT, wk)
                    ml_carry(512, wk)
                    terms = [((lambda c, k4=k4: Wo[:, k4, c * 128:(c + 1) * 128]), ZmT[:, k4, :]) for k4 in range(4)]
                    if STOP >= 4:
                        branch_out(terms, ZmT, Wo, Wo, Wgt, Wgt, hTg, 512, yTg, first, bw)
                    DMA(lambda e, yTg=yTg, g=g: e.dma_start(out=yT_d[g], in_=yTg[:, :, :]), [yTg], [yT_dep[g]])
                if int(os.environ.get('KSTEP', '99')) >= 15:
                    ml_state_out(st, wk, p_mc[l], p_mn[l], p_mm[l:l + 1, :], p_conv[l])
                if cfg.sample:
                    ZmTs = zpool.next()
                    ctok = sb([128, 4, 128], F32, "ctok"); ntok = sb([4, 128], F32, "ntok"); cvtok = sb([3, 512], F32, "cvtok")
                    for sq_ in range(NS):
                        DMA(lambda e, sq_=sq_: e.dma_start(out=ctok[:, :, :], in_=sc_in[l, sq_].rearrange("h v d -> v h d")), [], [ctok])
                        DMA(lambda e, sq_=sq_: e.dma_start(out=ntok[:, :], in_=sn_in[l, sq_]), [], [ntok])
                        DMA(lambda e, sq_=sq_: e.dma_start(out=cvtok[:, :], in_=scv_in[l, sq_]), [], [cvtok])
                        DMA(lambda e, sq_=sq_: e.dma_start(out=st["mprev"][:, :], in_=sm_in[l, sq_:sq_ + 1, :].partition_broadcast(128)), [], [st["mprev"]])
                        pX2, pS, pX3 = wk["pX2"], wk["pS"], wk["pX3"]
                        for h in range(4):
                            M(lambda e, h=h: e.transpose(out=pX2[:, h * 128:(h + 1) * 128], in_=ctok[:, h, :], identity=IDF[:, :]), [ctok, cst], [pX2])
                        V(lambda e: e.tensor_copy(out=st["CT"][:, :, :], in_=pX2[:, :].rearrange("p (h v) -> p h v", h=4)), [pX2], [st["CT"]])
                        G(lambda e: e.tensor_copy(out=st["CTb"][:, :, :], in_=st["CT"][:, :, :]), [st["CT"]], [st["CTb"]])
                        M(lambda e: e.transpose(out=pS[:, 24:28], in_=ntok[0:4, :], identity=IDF[0:4, 0:4]), [ntok, cst], [pS])
                        V(lambda e: e.tensor_copy(out=st["nT"][:, :], in_=pS[:, 24:28]), [pS], [st["nT"]])
                        G(lambda e: e.tensor_copy(out=st["nb"][:, :], in_=st["nT"][:, :]), [st["nT"]], [st["nb"]])
                        for c in range(4):
                            M(lambda e, c=c: e.transpose(out=pX3[:, c * 4:c * 4 + 3], in_=cvtok[0:3, c * 128:(c + 1) * 128], identity=IDF[0:3, 0:3]), [cvtok, cst], [pX3])
                        V(lambda e: e.tensor_copy(out=wk["xT"][:, :, 0:3], in_=pX3[:, 0:16].rearrange("p (c j) -> p c j", c=4)[:, :, 0:3]), [pX3], [wk["xT"]])
                        ml_group_prep(Wm, hTs, TS, mc_, wts, wk, hoff=sq_ * TS)
                        ml_tile(TS, 0, Wm, hTs, mc_, wts, st, ZmTs, wk, hoff=sq_ * TS, zoff=sq_ * TS)
                        ml_carry(TS, wk)
                        ml_state_out(st, wk, s_mc[l, sq_], s_mn[l, sq_], s_mm[l, sq_:sq_ + 1, :], s_conv[l, sq_])
                    terms = [((lambda c, k4=k4: Wo[:, k4, c * 128:(c + 1) * 128]), ZmTs[:, k4, 0:NTOK]) for k4 in range(4)]
                    branch_out(terms, ZmTs, Wo, Wo, Wgt, Wgt, hTs, NTOK, yTs, first, bw)
                barrier()
            cur[0] = gst


        def qk_norm(T, pq, nrm_bc, scale, wk, out_f32):
            sq = wk["f5"].next(); ss8, ln8, rs8 = wk["g8"].next(), wk["g8"].next(), wk["g8"].next()
            v8 = lambda ap: ap.rearrange("p (h d) -> p h d", h=8)
            A(lambda e: e.activation(out=sq[0:T, :], in_=pq[0:T, :], func=AF.Square), [pq], [sq])
            V(lambda e: e.tensor_reduce(out=ss8[0:T, :], in_=v8(sq[0:T, :]), axis=AX.X, op=ALU.add), [sq], [ss8])
            rstd_from_ss(ss8[0:T, :], ss8, rs8[0:T, :], rs8, ln8[0:T, :], ln8, 1.0 / 64)
            V(lambda e: e.tensor_tensor(out=v8(out_f32[0:T, :]), in0=v8(pq[0:T, :]), in1=rs8[0:T, :].unsqueeze(2).to_broadcast([T, 8, 64]), op=ALU.mult),
              [pq, rs8], [out_f32])
            V(lambda e: e.scalar_tensor_tensor(out=v8(out_f32[0:T, :]), in0=v8(out_f32[0:T, :]), scalar=scale,
                                               in1=nrm_bc[0:T, :].unsqueeze(1).to_broadcast([T, 8, 64]), op0=ALU.mult, op1=ALU.mult), [out_f32, nrm_bc], [out_f32])


        OTs = sb([64, 8, NTOK], F32, "OTs")
        NBP = NPG // 2

        def sample_moba(l, Wb, qn_bc, kn_bc):
            wk = dict(f5=Rot(lambda: sb([128, 512], F32, "f5s"), 8), g8=Rot(lambda: sb([128, 8], F32, "g8s"), 12),
                      b5=Rot(lambda: sb([128, 512], BF16, "b5s"), 6))
            pR = Rot(lambda: psum(F32, "pRs"), 3); pT = Rot(lambda: psum(BF16, "pTs"), 2); pBS = psum(F32, "pBSs")
            pOV = psum(F32, "pOV"); pMk = psum(F32, "pMk")
            id32f = sb([32, 2048], F32, "id32f"); id32 = sb([32, 32, 64], BF16, "id32")
            DMA(lambda e: e.dma_start(out=id32f[:, :], in_=consts_d[0:32, K_ID32:K_ID32 + 2048]), [], [id32f])
            V(lambda e: e.tensor_copy(out=id32[:, :, :], in_=id32f[:, :].rearrange("p (n t) -> p n t", n=32)), [id32f], [id32])
            ptb = sb([128, NPG], I32, "ptb"); ptf = sb([128, NPG], F32, "ptf"); idxi = sb([128, NPG], I32, "idxi")
            S_all = sb([128, NPG, 64], F32, "S_all"); Pm = sb([128, NPG, 64], BF16, "Pm")
            QBD = zsb([128, 4, 2, TS], BF16, "QBD"); KTn = sb([128, 4, TS], BF16, "KTn")
            KTt = Rot(lambda: sb([128, 4, 128], BF16, "KTt"), 4)
            S_dep = [B(None) for _ in range(NPG)]
            bs_sb = sb([64, max(NBP, 8)], F32, "bs_sb"); m8 = sb([64, 8], F32, "m8s"); sel = sb([64, 32], F32, "sel"); selT = sb([32, 64], F32, "selT")
            Dexp = sb([32, 32, 64], BF16, "Dexp"); Pn = sb([TS, 8, TS], BF16, "Pn"); Pnf = sb([TS, 8, TS], F32, "Pnf")
            tmpo = sb([64, 8, 64], F32, "tmpo"); OD = sb([64, 64], F32, "OD"); rdn = sb([64, 1], F32, "rdn")
            ck2 = ck[l][:, :]; cv2 = cv[l][:, :]
            idx1 = Rot(lambda: sb([128, 1], I32, "idx1"), 4)
            for sq_ in range(NS):
                tsl = slice(sq_ * TS, (sq_ + 1) * TS)
                pv = pR.next(); tok_proj(pv[0:TS, 0:512], pv, hTs, hTs, sq_ * TS, TS, Wb, Wb, 1024, 512)
                vf = wk["f5"].next()
                A(lambda e, pv=pv, vf=vf: e.activation(out=vf[0:TS, :], in_=pv[0:TS, 0:512], func=AF.Copy), [pv], [vf])
                DMA(lambda e, vf=vf, tsl=tsl: e.dma_start(out=s_v[l, tsl, :], in_=vf[0:TS, :]), [vf], [], final=True)
                vbn = wk["b5"].next()
                G(lambda e, vf=vf, vbn=vbn: e.tensor_copy(out=vbn[0:TS, :], in_=vf[0:TS, :]), [vf], [vbn])
                pk = pR.next(); tok_proj(pk[0:TS, 0:512], pk, hTs, hTs, sq_ * TS, TS, Wb, Wb, 512, 512)
                kf = wk["f5"].next(); qk_norm(TS, pk, kn_bc, 1.0, wk, kf)
                DMA(lambda e, kf=kf, tsl=tsl: e.dma_start(out=s_k[l, tsl, :], in_=kf[0:TS, :]), [kf], [], final=True)
                kbn = wk["b5"].next()
                G(lambda e, kf=kf, kbn=kbn: e.tensor_copy(out=kbn[0:TS, :], in_=kf[0:TS, :]), [kf], [kbn])
                pq = pR.next(); tok_proj(pq[0:TS, 0:512], pq, hTs, hTs, sq_ * TS, TS, Wb, Wb, 0, 512)
                qf = wk["f5"].next(); qk_norm(TS, pq, qn_bc, 0.125, wk, qf)
                qbn = wk["b5"].next()
                G(lambda e, qf=qf, qbn=qbn: e.tensor_copy(out=qbn[0:TS, :], in_=qf[0:TS, :]), [qf], [qbn])
                pt_ = pT.next(); ptv = pt_[:, 0:4 * TS].rearrange("p (c t) -> p c t", c=4)
                for c in range(4):
                    M(lambda e, c=c, qbn=qbn, ptv=ptv: e.transpose(out=ptv[:, c, :], in_=qbn[0:TS, c * 128:(c + 1) * 128], identity=IDB[0:TS, 0:TS]), [qbn, cstb], [pt_])
                A(lambda e, ptv=ptv: e.activation(out=QBD[0:64, :, 0, :], in_=ptv[0:64, :, :], func=AF.Copy), [pt_], [QBD])
                V(lambda e, ptv=ptv: e.tensor_copy(out=QBD[64:128, :, 1, :], in_=ptv[64:128, :, :]), [pt_], [QBD])
                pt2 = pT.next(); ptv2 = pt2[:, 0:4 * TS].rearrange("p (c t) -> p c t", c=4)
                for c in range(4):
                    M(lambda e, c=c, kbn=kbn, ptv2=ptv2: e.transpose(out=ptv2[:, c, :], in_=kbn[0:TS, c * 128:(c + 1) * 128], identity=IDB[0:TS, 0:TS]), [kbn, cstb], [pt2])
                A(lambda e, ptv2=ptv2: e.activation(out=KTn[:, :, :], in_=ptv2[:, :, :], func=AF.Copy), [pt2], [KTn])
                DMA(lambda e, sq_=sq_: e.dma_start(out=ptb[:, :], in_=pt[sq_:sq_ + 1, :].partition_broadcast(128)), [], [ptb])
                V(lambda e: e.tensor_copy(out=ptf[:, :], in_=ptb[:, :]), [ptb], [ptf])
                V(lambda e: e.tensor_scalar(out=ptf[:, :], in0=ptf[:, :], scalar1=128.0, scalar2=IOTA, op0=ALU.mult, op1=ALU.add), [ptf, cst], [ptf])
                V(lambda e: e.tensor_copy(out=idxi[:, :], in_=ptf[:, :]), [ptf], [idxi])
                def stA(j):
                    Kt = wk["f5"].next(); ix = idx1.next()
                    G(lambda e, ix=ix, j=j: e.tensor_copy(out=ix[:, :], in_=idxi[:, j:j + 1]), [idxi], [ix])
                    DMA(lambda e, Kt=Kt, ix=ix: e.indirect_dma_start(out=Kt[:, :], out_offset=None, in_=ck2,
                                                                     in_offset=bass.IndirectOffsetOnAxis(ap=ix[:, :], axis=0)), [ix], [Kt], q="pool")
                    Kb = wk["b5"].next()
                    V(lambda e, Kt=Kt, Kb=Kb: e.tensor_copy(out=Kb[:, :], in_=Kt[:, :]), [Kt], [Kb])
                    pk_ = pT.next(); pkv = pk_[:, 0:512].rearrange("p (c t) -> p c t", c=4)
                    for c in range(4):
                        M(lambda e, c=c, Kb=Kb, pkv=pkv: e.transpose(out=pkv[:, c, :], in_=Kb[:, c * 128:(c + 1) * 128], identity=IDB[:, :]), [Kb, cstb], [pk_])
                    kt_ = KTt.next()
                    A(lambda e, pkv=pkv, kt_=kt_: e.activation(out=kt_[:, :, :], in_=pkv[:, :, :], func=AF.Copy), [pk_], [kt_])
                    return kt_

                def stB(j, kt_):
                    pST = pR.next()
                    for c in range(4):
                        M(lambda e, c=c, kt_=kt_, pST=pST: e.matmul(pST[:, c * 2 * TS:(c + 1) * 2 * TS], lhsT=kt_[:, c, :], rhs=QBD[:, c, :, :].rearrange("p a t -> p (a t)"),
                                                                    start=True, stop=True), [kt_, QBD], [pST])
                    wr = [S_dep[j]] + ([S_all] if j == 0 else [])
                    V(lambda e, j=j, pST=pST: e.tensor_copy(out=S_all[:, j, :], in_=pST[:, 0:64]), [pST], wr)

                def stC(j):
                    M(lambda e, j=j: e.matmul(pBS[0:64, j // 2:j // 2 + 1], lhsT=S_all[:, j, :], rhs=ONESF[:, 0:1], start=(j % 2 == 0), stop=(j % 2 == 1)),
                      [S_dep[j], cst], [pBS])

                kts = {0: stA(0)}
                if NPG > 1:
                    kts[1] = stA(1)
                for j in range(NPG):
                    stB(j, kts.pop(j))
                    if j + 2 < NPG:
                        kts[j + 2] = stA(j + 2)
                    stC(j)
                if NBP < 8:
                    V(lambda e: e.memset(bs_sb[:, :], NEG), [], [bs_sb])
                V(lambda e: e.tensor_copy(out=bs_sb[:, 0:NBP], in_=pBS[0:64, 0:NBP]), [pBS], [bs_sb])
                V(lambda e: e.max(out=m8[:, :], in_=bs_sb[:, :]), [bs_sb], [m8])
                V(lambda e: e.tensor_tensor(out=sel[:, 0:NBP], in0=bs_sb[:, 0:NBP], in1=m8[:, 2:3].to_broadcast([64, NBP]), op=ALU.is_ge), [bs_sb, m8], [sel])
                M(lambda e: e.transpose(out=pMk[0:NBP, 0:64], in_=sel[:, 0:NBP], identity=IDF[0:64, 0:64]), [sel, cst], [pMk])
                V(lambda e: e.tensor_copy(out=selT[0:NBP, :], in_=pMk[0:NBP, 0:64]), [pMk], [selT])
                V(lambda e: e.tensor_tensor(out=Dexp[0:NBP, 0:NBP, :], in0=id32[0:NBP, 0:NBP, :], in1=selT[0:NBP, :].unsqueeze(1).to_broadcast([NBP, NBP, 64]), op=ALU.mult),
                  [id32, selT], [Dexp])
                A(lambda e: e.activation(out=S_all[:, :, :], in_=S_all[:, :, :], func=AF.Exp), [], [S_all] + S_dep)
                for n0 in range(0, NBP, 8):
                    nn = min(8, NBP - n0)
                    M(lambda e, n0=n0, nn=nn: e.matmul(pMk[:, 0:nn * 64], lhsT=ONESB[0:NBP, :], rhs=Dexp[0:NBP, n0:n0 + nn, :].rearrange("p n t -> p (n t)"),
                                                       start=True, stop=True), [Dexp, cstb], [pMk])
                    V(lambda e, n0=n0, nn=nn: e.tensor_tensor(
                        out=Pm[:, 2 * n0:2 * (n0 + nn), :].rearrange("p (n two) t -> p n two t", two=2),
                        in0=S_all[:, 2 * n0:2 * (n0 + nn), :].rearrange("p (n two) t -> p n two t", two=2),
                        in1=pMk[:, 0:nn * 64].rearrange("p (n t) -> p n t", n=nn).unsqueeze(2).to_broadcast([128, nn, 2, 64]), op=ALU.mult), [S_all, pMk], [Pm])
                pST = pR.next()
                for c in range(4):
                    M(lambda e, c=c, pST=pST: e.matmul(pST[0:TS, c * 2 * TS:(c + 1) * 2 * TS], lhsT=KTn[:, c, :], rhs=QBD[:, c, :, :].rearrange("p a t -> p (a t)"),
                                              start=True, stop=True), [KTn, QBD], [pST])
                A(lambda e, pST=pST: e.activation(out=Pnf[:, :, :], in_=pST[0:TS, 0:64].rearrange("p (h t) -> p h t", h=8), func=AF.Exp), [pST], [Pnf])
                V(lambda e: e.tensor_tensor(out=Pn[:, :, :], in0=Pnf[:, :, :], in1=TRI01[0:TS, 0:TS].unsqueeze(1).to_broadcast([TS, 8, TS]), op=ALU.mult), [Pnf, cstb], [Pn])
                for j in range(NPG):
                    Vt = wk["f5"].next(); ix = idx1.next()
                    G(lambda e, ix=ix, j=j: e.tensor_copy(out=ix[:, :], in_=idxi[:, j:j + 1]), [idxi], [ix])
                    DMA(lambda e, Vt=Vt, ix=ix: e.indirect_dma_start(out=Vt[:, :], out_offset=None, in_=cv2,
                                                                     in_offset=bass.IndirectOffsetOnAxis(ap=ix[:, :], axis=0)), [ix], [Vt], q="pool")
                    Vb = wk["b5"].next()
                    if j % 2 == 0:
                        V(lambda e, Vt=Vt, Vb=Vb: e.tensor_copy(out=Vb[:, :], in_=Vt[:, :]), [Vt], [Vb])
                    else:
                        A(lambda e, Vt=Vt, Vb=Vb: e.activation(out=Vb[:, :], in_=Vt[:, :], func=AF.Copy), [Vt], [Vb])
                    M(lambda e, j=j, Vb=Vb: e.matmul(pOV[0:64, 0:512], lhsT=Pm[:, j, :], rhs=Vb[:, :], start=(j == 0), stop=False), [Pm, Vb], [pOV])
                    M(lambda e, j=j: e.matmul(pBS[0:64, 64:65], lhsT=Pm[:, j, :], rhs=ONESB[:, 0:1], start=(j == 0), stop=False), [Pm, cstb], [pBS])
                M(lambda e, vbn=vbn: e.matmul(pOV[0:64, 0:512], lhsT=Pn[:, :, :].rearrange("p h t -> p (h t)"), rhs=vbn[0:TS, :], start=False, stop=True), [Pn, vbn], [pOV])
                M(lambda e: e.matmul(pBS[0:64, 64:65], lhsT=Pn[:, :, :].rearrange("p h t -> p (h t)"), rhs=ONESB[0:TS, 0:1], start=False, stop=True), [Pn, cstb], [pBS])
                V(lambda e: e.tensor_tensor(out=tmpo[:, :, :], in0=pOV[0:64, 0:512].rearrange("p (h d) -> p h d", h=8),
                                            in1=BDm[0:64, 0:8].unsqueeze(2).to_broadcast([64, 8, 64]), op=ALU.mult), [pOV, cst], [tmpo])
                V(lambda e: e.tensor_reduce(out=OD[:, :], in_=tmpo[:, :, :].rearrange("p h d -> p d h"), axis=AX.X, op=ALU.add), [tmpo], [OD])
                V(lambda e: e.reciprocal(out=rdn[:, :], in_=pBS[0:64, 64:65]), [pBS], [rdn])
                V(lambda e: e.tensor_scalar(out=OD[:, :], in0=OD[:, :], scalar1=rdn[:, 0:1], scalar2=None, op0=ALU.mult), [OD, rdn], [OD])
                M(lambda e: e.transpose(out=pMk[0:64, 0:64], in_=OD[:, :], identity=IDF[0:64, 0:64]), [OD, cst], [pMk])
                V(lambda e, sq_=sq_: e.tensor_copy(out=OTs[:, :, sq_ * TS:(sq_ + 1) * TS], in_=pMk[0:64, 0:64].rearrange("p (h t) -> p h t", h=8)), [pMk], [OTs])

        def phase_B1(l, KT2, Vall):
            with contextlib.ExitStack() as pst:
                cur[0] = pst
                Wb = sb([128, 8, 1536], BF16, "Wb")
                with WLoad() as stg:
                    load_w(Wb, 0, 1536, w_in_src(l, C_BQ), stg)
                qn_bc = bvec(w["mb_q_norm"][l:l + 1, :], 64, "qn"); kn_bc = bvec(w["mb_k_norm"][l:l + 1, :], 64, "kn")
                p1 = contextlib.ExitStack(); p1.__enter__(); cur[0] = p1
                wk = dict(f5=Rot(lambda: sb([128, 512], F32, "f5"), 6), g8=Rot(lambda: sb([128, 8], F32, "g8"), 12),
                          b5=Rot(lambda: sb([128, 512], BF16, "b5"), 4))
                pR = Rot(lambda: psum(F32, "pRb"), 3); pT = Rot(lambda: psum(BF16, "pTb"), 2); pBS = psum(F32, "pBS"); pMB = psum(BF16, "pMB")
                hpool = Rot(lambda: sb([128, 8, 512], BF16, "hTg"), 2)
                qpool = Rot(lambda: zsb([128, 4, 2, 512], BF16, "QT2m"), 2)
                bpool = Rot(lambda: sb([16, 8, 512], BF16, "biasT"), 2)
                KBf = sb([128, 4, 16], F32, "KBf"); KBb = sb([128, 4, 16], BF16, "KBb")
                V(lambda e: e.memset(KBf[:, :, :], 0.0), [], [KBf]); V(lambda e: e.memset(KBb[:, :, :], 0.0), [], [KBb])
                bsm = sb([128, 8, 16], F32, "bsm"); m8 = sb([128, 8, 8], F32, "m8"); ge = sb([128, 8, 16], F32, "ge")
                mbias = sb([128, 8, 16], BF16, "mbias")
                for g in range(NG):
                    hTg = hpool.next(); QT2m = qpool.next(); biasT = bpool.next()
                    DMA(lambda e, hTg=hTg, g=g: e.dma_start(out=hTg[:, :, :], in_=hT_d[g]), [hT_dep[g]], [hTg])
                    for j in range(4):
                        i = 4 * g + j; t0 = j * 128; own = i // 2
                        pv = pR.next()
                        tok_proj(pv[:, 0:512], pv, hTg, hTg, t0, 128, Wb, Wb, 1024, 512)
                        vf = wk["f5"].next()
                        A(lambda e, pv=pv, vf=vf: e.activation(out=vf[:, :], in_=pv[:, 0:512], func=AF.Copy), [pv], [vf])
                        DMA(lambda e, vf=vf, i=i: e.dma_start(out=p_v[l, i * 128:(i + 1) * 128, :], in_=vf[:, :]), [vf], [], final=True)
                        G(lambda e, vf=vf, i=i: e.tensor_copy(out=Vall[:, i, :], in_=vf[:, :]), [vf], [Vall])
                        pk = pR.next()
                        tok_proj(pk[:, 0:512], pk, hTg, hTg, t0, 128, Wb, Wb, 512, 512)
                        kf = wk["f5"].next()
                        qk_norm(128, pk, kn_bc, 1.0, wk, kf)
                        DMA(lambda e, kf=kf, i=i: e.dma_start(out=p_k[l, i * 128:(i + 1) * 128, :], in_=kf[:, :]), [kf], [], final=True)
                        kb = wk["b5"].next()
                        G(lambda e, kf=kf, kb=kb: e.tensor_copy(out=kb[:, :], in_=kf[:, :]), [kf], [kb])
                        pt_ = pT.next()
                        ptv = pt_[:, 0:512].rearrange("p (c t) -> p c t", c=4)
                        for c in range(4):
                            M(lambda e, c=c, kb=kb, ptv=ptv: e.transpose(out=ptv[:, c, :], in_=kb[:, c * 128:(c + 1) * 128], identity=IDB[:, :]), [kb, cstb], [pt_])
                        A(lambda e, ptv=ptv, i=i: e.activation(out=KT2[:, :, i * 128:(i + 1) * 128], in_=ptv[:, :, :], func=AF.Copy), [pt_], [KT2])
                        pq = pR.next()
                        tok_proj(pq[:, 0:512], pq, hTg, hTg, t0, 128, Wb, Wb, 0, 512)
                        qf = wk["f5"].next()
                        qk_norm(128, pq, qn_bc, 0.125, wk, qf)
                        qb = wk["b5"].next()
                        G(lambda e, qf=qf, qb=qb: e.tensor_copy(out=qb[:, :], in_=qf[:, :]), [qf], [qb])
                        pt2 = pT.next()
                        ptv2 = pt2[:, 0:512].rearrange("p (c t) -> p c t", c=4)
                        for c in range(4):
                            M(lambda e, c=c, qb=qb, ptv2=ptv2: e.transpose(out=ptv2[:, c, :], in_=qb[:, c * 128:(c + 1) * 128], identity=IDB[:, :]), [qb, cstb], [pt2])
                        A(lambda e, ptv2=ptv2, t0=t0, QT2m=QT2m: e.activation(out=QT2m[0:64, :, 0, t0:t0 + 128], in_=ptv2[0:64, :, :], func=AF.Copy), [pt2], [QT2m])
                        V(lambda e, ptv2=ptv2, t0=t0, QT2m=QT2m: e.tensor_copy(out=QT2m[64:128, :, 1, t0:t0 + 128], in_=ptv2[64:128, :, :]), [pt2], [QT2m])
                        G(lambda e: e.memset(mbias[:, :, :], 0.0), [], [mbias])
                        if own >= 4:
                            pbv = pBS[:, 0:128].rearrange("p (h n) -> p h n", h=8)
                            for h in range(8):
                                M(lambda e, h=h, QT2m=QT2m, t0=t0, pbv=pbv: e.matmul(pbv[:, h, :], lhsT=QT2m[:, h // 2, h % 2, t0:t0 + 128], rhs=KBb[:, h // 2, :],
                                                                                  start=True, stop=True), [QT2m, KBb], [pBS])
                            G(lambda e: e.memset(bsm[:, :, :], NEG), [], [bsm])
                            V(lambda e, pbv=pbv, own=own: e.tensor_copy(out=bsm[:, :, 0:own], in_=pbv[:, :, 0:own]), [pBS], [bsm])
                            for h in range(8):
                                V(lambda e, h=h: e.max(out=m8[:, h, :], in_=bsm[:, h, :]), [bsm], [m8])
                            V(lambda e: e.tensor_tensor(out=ge[:, :, :], in0=bsm[:, :, :], in1=m8[:, :, 2:3].to_broadcast([128, 8, 16]), op=ALU.is_ge), [bsm, m8], [ge])
                            V(lambda e, own=own: e.tensor_scalar(out=mbias[:, :, 0:own], in0=ge[:, :, 0:own], scalar1=-1.0, scalar2=-MB_NEG, op0=ALU.add, op1=ALU.mult),
                              [ge], [mbias])
                        pmv = pMB[0:16, 0:1024].rearrange("p (h t) -> p h t", h=8)
                        for h in range(8):
                            M(lambda e, h=h, pmv=pmv: e.transpose(out=pmv[:, h, :], in_=mbias[:, h, :], identity=IDB[:, :]), [mbias, cstb], [pMB])
                        A(lambda e, pmv=pmv, biasT=biasT, t0=t0: e.activation(out=biasT[0:16, :, t0:t0 + 128], in_=pmv[:, :, :], func=AF.Copy), [pMB], [biasT])
                        if i % 2 == 1:
                            n = i // 2
                            V(lambda e, n=n: e.tensor_reduce(out=KBf[:, :, n:n + 1], in_=KT2[:, :, n * 256:(n + 1) * 256], axis=AX.X, op=ALU.add), [KT2], [KBf])
                            V(lambda e: e.tensor_copy(out=KBb[:, :, :], in_=KBf[:, :, :]), [KBf], [KBb])
                    DMA(lambda e, QT2m=QT2m, g=g: e.dma_start(out=qT_d[g], in_=QT2m[:, :, :, :]), [QT2m], [qT_dep[g]])
                    DMA(lambda e, biasT=biasT, g=g: e.dma_start(out=mbias_d[g], in_=biasT[:, :, :]), [biasT], [mbias_dep[g]])
                barrier()
                p1.__exit__(None, None, None)
                cur[0] = pst
                if cfg.sample:
                    with contextlib.ExitStack() as p2:
                        cur[0] = p2
                        sample_moba(l, Wb, qn_bc, kn_bc)
                        barrier()
                    cur[0] = pst
            cur[0] = gst

        def phase_B2(l, first, KT2, Vall):
            with contextlib.ExitStack() as pst:
                cur[0] = pst
                Wz = sb([128, 8, 512], BF16, "Wz"); Wgt = sb([128, 8, D], BF16, "Wgt"); Wmb = sb([64, 8, D], BF16, "Wmb")
                with WLoad() as stg:
                    load_w(Wz, 0, 512, w_in_src(l, C_BZ), stg)
                    load_gate_w(Wgt, l, 2, stg)
                    load_w(Wmb, 0, D, lambda a, n: w["mb_w_out"][l, :, a:a + n].rearrange("(h p) n -> p h n", p=64), stg, kc=8, rows=64)
                pR = Rot(lambda: psum(F32, "pR2"), 2); pSr = Rot(lambda: psum(F32, "pS2"), 3); pOr = Rot(lambda: psum(F32, "pO2"), 2); pDr = Rot(lambda: psum(F32, "pD2"), 1)
                bw = branch_work(pR)
                hpool = Rot(lambda: sb([128, 8, 512], BF16, "hTg"), 1)
                ypool = Rot(lambda: sb([128, 8, 512], BF16, "yTg"), 1)
                qpool = Rot(lambda: sb([128, 4, 2, 512], BF16, "QT2m"), 1)
                bpool = Rot(lambda: sb([16, 8, 512], BF16, "biasT"), 1)
                ZbT = sb([64, 8, 512], BF16, "ZbT")
                PTp = Rot(lambda: sb([128, 512], BF16, "PT"), 4)
                dsp = Rot(lambda: sb([64, 512], F32, "dsb"), 2); szp = Rot(lambda: sb([64, 512], F32, "szT"), 2); rdp = Rot(lambda: sb([64, 512], F32, "rd"), 2); otp = Rot(lambda: sb([64, 512], F32, "ot"), 2)
                for g in range(NG):
                    hTg = hpool.next(); yTg = ypool.next(); QT2m = qpool.next(); biasT = bpool.next()
                    DMA(lambda e, hTg=hTg, g=g: e.dma_start(out=hTg[:, :, :], in_=hT_d[g]), [hT_dep[g]], [hTg])
                    if not first:
                        DMA(lambda e, yTg=yTg, g=g: e.dma_start(out=yTg[:, :, :], in_=yT_d[g]), [yT_dep[g]], [yTg])
                    DMA(lambda e, QT2m=QT2m, g=g: e.dma_start(out=QT2m[:, :, :, :], in_=qT_d[g]), [qT_dep[g]], [QT2m])
                    DMA(lambda e, biasT=biasT, g=g: e.dma_start(out=biasT[:, :, :], in_=mbias_d[g]), [mbias_dep[g]], [biasT])
                    nj = 4 * g + 4

                    def qk(h, j):
                        ps_ = pSr.next()
                        nob = (g <= 1) or (j >= 4 * g + 2)
                        M(lambda e, ps_=ps_, j=j, h=h, QT2m=QT2m, nob=nob: e.matmul(ps_[:, 0:512], lhsT=KT2[:, h // 2, j * 128:(j + 1) * 128], rhs=QT2m[:, h // 2, h % 2, :],
                                                                                start=True, stop=nob), [KT2, QT2m], [ps_])
                        if not nob:
                            M(lambda e, ps_=ps_, j=j, h=h, biasT=biasT: e.matmul(ps_[:, 0:512], lhsT=ohb[0:16, j // 2, :], rhs=biasT[0:16, h, :], start=False, stop=True),
                              [ohb, biasT], [ps_])
                        return ps_

                    seq = [(h, j) for h in range(8) for j in range(nj)]
                    pendq = [qk(*seq[0])]
                    if len(seq) > 1:
                        pendq.append(qk(*seq[1]))
                    for idx_, (h, j) in enumerate(seq):
                        ps_ = pendq.pop(0)
                        if j == 0:
                            pz = pR.next()
                            feat_proj(pz[0:64, 0:512], pz, Wz, Wz, h * 64, 64, hTg, hTg, 0, 512)
                            szT = szp.next()
                            A(lambda e, pz=pz, szT=szT: e.activation(out=szT[:, :], in_=pz[0:64, 0:512], func=AF.Silu), [pz], [szT])
                            pO = pOr.next(); pD = pDr.next()
                        PT = PTp.next()
                        A(lambda e, ps_=ps_, PT=PT: e.activation(out=PT[:, :], in_=ps_[:, 0:512], func=AF.Exp), [ps_], [PT])
                        if idx_ + 2 < len(seq):
                            pendq.append(qk(*seq[idx_ + 2]))
                        if j >= 4 * g:
                            V(lambda e, PT=PT, r=j - 4 * g: e.tensor_tensor(out=PT[:, :], in0=PT[:, :], in1=CAUS[r], op=ALU.mult), [PT, cstb], [PT])
                        M(lambda e, PT=PT, j=j, h=h, nj=nj, pO=pO: e.matmul(pO[0:64, 0:512], lhsT=Vall[:, j, h * 64:(h + 1) * 64], rhs=PT[:, :], start=(j == 0), stop=(j == nj - 1)),
                          [Vall, PT], [pO])
                        M(lambda e, PT=PT, j=j, nj=nj, pD=pD: e.matmul(pD[0:64, 0:512], lhsT=ONESB[:, 0:64], rhs=PT[:, :], start=(j == 0), stop=(j == nj - 1)), [cstb, PT], [pD])
                        if j == nj - 1:
                            rd = rdp.next(); ot = otp.next(); dsb = dsp.next()
                            A(lambda e, dsb=dsb, pD=pD: e.activation(out=dsb[:, :], in_=pD[0:64, 0:512], func=AF.Copy), [pD], [dsb])
                            V(lambda e, rd=rd, dsb=dsb: e.reciprocal(out=rd[:, :], in_=dsb[:, :]), [dsb], [rd])
                            V(lambda e, rd=rd, ot=ot, pO=pO: e.tensor_tensor(out=ot[:, :], in0=pO[0:64, 0:512], in1=rd[:, :], op=ALU.mult), [pO, rd], [ot])
                            V(lambda e, ot=ot, szT=szT, h=h: e.tensor_tensor(out=ZbT[:, h, :], in0=ot[:, :], in1=szT[:, :], op=ALU.mult), [ot, szT], [ZbT])
                    terms = [((lambda c, h=h: Wmb[0:64, h, c * 128:(c + 1) * 128]), ZbT[0:64, h, :]) for h in range(8)]
                    branch_out(terms, ZbT, Wmb, Wmb, Wgt, Wgt, hTg, 512, yTg, first, bw)
                    DMA(lambda e, yTg=yTg, g=g: e.dma_start(out=yT_d[g], in_=yTg[:, :, :]), [yTg], [yT_dep[g]])
                if cfg.sample:
                    for h in range(8):
                        pz = pR.next()
                        feat_proj(pz[0:64, 0:NTOK], pz, Wz, Wz, h * 64, 64, hTs, hTs, 0, NTOK)
                        szT = szp.next()
                        A(lambda e, pz=pz, szT=szT: e.activation(out=szT[:, 0:NTOK], in_=pz[0:64, 0:NTOK], func=AF.Silu), [pz], [szT])
                        V(lambda e, szT=szT, h=h: e.tensor_tensor(out=ZbT[:, h, 0:NTOK], in0=OTs[:, h, :], in1=szT[:, 0:NTOK], op=ALU.mult), [OTs, szT], [ZbT])
                    terms = [((lambda c, h=h: Wmb[0:64, h, c * 128:(c + 1) * 128]), ZbT[0:64, h, 0:NTOK]) for h in range(8)]
                    branch_out(terms, ZbT, Wmb, Wmb, Wgt, Wgt, hTs, NTOK, yTs, first, bw)
                barrier()
            cur[0] = gst

        def phase_B(l, first):
            with contextlib.ExitStack() as bst:
                cur[0] = bst
                KT2 = sb([128, 4, L], BF16, "KT2"); Vall = sb([128, NT, 512], BF16, "Vall")
                phase_B1(l, KT2, Vall)
                cur[0] = bst
                phase_B2(l, first, KT2, Vall)
            cur[0] = gst

        for l in range(DEPTH):
            if cfg.prompt:
                if l == 0:
                    phase_N(0)
                if STOP == 1:
                    break
                first = True
                if "g" in cfg.branches:
                    phase_G(l, first); first = False
                    if STOP <= 4:
                        break
                if "m" in cfg.branches:
                    phase_M(l, first); first = False
                    if STOP <= 4:
                        break
                if "b" in cfg.branches:
                    phase_B(l, first); first = False
                phase_O(l)
        P.emit()
    return nc


W_NAMES = ("norm_g", "w_in", "gla_w_a2", "gla_b_a", "gla_norm", "gla_w_out", "ml_conv_w", "ml_conv_b", "ml_w_q", "ml_w_k",
           "ml_w_v", "ml_b_i", "ml_b_f", "ml_norm", "ml_skip", "ml_w_out", "mb_q_norm", "mb_k_norm", "mb_w_out", "w_out")
_CACHE = {}


def run(cfg, inputs, n_cores=8):
    key = (cfg.L, cfg.NS, cfg.TS, cfg.NPG, cfg.NPOOL, cfg.DEPTH, cfg.branches, cfg.sample, cfg.prompt)
    if key not in _CACHE:
        _CACHE[key] = build(cfg)
    nc = _CACHE[key]
    f32 = lambda a: np.ascontiguousarray(np.asarray(a), dtype=np.float32)
    consts = make_consts()
    Bp = inputs["x_prompt"].shape[0]
    NS, TS, DEPTH = cfg.NS, cfg.TS, cfg.DEPTH
    ck = f32(inputs["cache_k"]).reshape(DEPTH, cfg.NPOOL * 128, 512)
    cv = f32(inputs["cache_v"]).reshape(DEPTH, cfg.NPOOL * 128, 512)
    wts = {k: f32(inputs[k]) for k in W_NAMES}
    in_maps = []
    for c in range(n_cores):
        b = c % Bp
        sl = slice(c * NS, (c + 1) * NS)
        m = dict(wts)
        m["consts"] = consts
        m["xp"] = f32(inputs["x_prompt"][b])
        m["xs"] = f32(inputs["x_sample"][sl]).reshape(NS * TS, D)
        for l_ in range(DEPTH):
            m["ck%d" % l_] = ck[l_]
            m["cv%d" % l_] = cv[l_]
        m["pt"] = np.ascontiguousarray(np.asarray(inputs["page_table"])[sl], dtype=np.int32)
        m["sg"] = f32(np.asarray(inputs["state_gla"])[:, sl])
        m["sc"] = f32(np.asarray(inputs["state_mlstm_c"])[:, sl])
        m["sn"] = f32(np.asarray(inputs["state_mlstm_n"])[:, sl])
        m["sm"] = f32(np.asarray(inputs["state_mlstm_m"])[:, sl])
        m["scv"] = f32(np.asarray(inputs["state_mlstm_conv"])[:, sl])
        in_maps.append(m)
    res = run_bass_kernel_spmd(nc, in_maps, core_ids=list(range(n_cores))).results
    L = cfg.L
    nb = min(Bp, n_cores)
    st = lambda name, shp: np.stack([res[b][name].reshape(shp) for b in range(nb)], axis=1)
    y_prompt = np.stack([res[b]["yp"] for b in range(nb)], 0)
    y_sample = np.concatenate([res[c]["ys"].reshape(NS, TS, D) for c in range(n_cores)], 0)
    p_gla = st("p_gla", (DEPTH, 4, 64, 128)); p_mc = st("p_mc", (DEPTH, 4, 128, 128)); p_mn = st("p_mn", (DEPTH, 4, 128))
    p_mm = st("p_mm", (DEPTH, 4)); p_conv = st("p_conv", (DEPTH, 3, 512))
    p_k = st("p_k", (DEPTH, L, 8, 64)); p_v = st("p_v", (DEPTH, L, 8, 64))
    cs = lambda name, shp: np.concatenate([res[c][name].reshape(shp) for c in range(n_cores)], axis=1)
    s_gla = cs("s_gla", (DEPTH, NS, 4, 64, 128)); s_mc = cs("s_mc", (DEPTH, NS, 4, 128, 128)); s_mn = cs("s_mn", (DEPTH, NS, 4, 128))
    s_mm = cs("s_mm", (DEPTH, NS, 4)); s_conv = cs("s_conv", (DEPTH, NS, 3, 512))
    s_k = cs("s_k", (DEPTH, NS, TS, 8, 64)); s_v = cs("s_v", (DEPTH, NS, TS, 8, 64))
    return (y_prompt, y_sample, p_gla, p_mc, p_mn, p_mm, p_conv, p_k, p_v, s_gla, s_mc, s_mn, s_mm, s_conv, s_k, s_v)


def kernel(**inputs):
    return run(Cfg(), inputs, n_cores=8)
```
